# Optimizing a Trainium2 kernel written in Bass

```python
import math
import jax, jax.numpy as jnp
from jax import lax
import numpy as np

D_MODEL = 1024
BATCH = 2
SEQ = 8192
DEPTH = 1
DEC_BATCH = 128
DEC_SEQ = 1
PAST_LEN = 16384
PAGE_SIZE = 128

HEAD_DIM = 64
MIX_WIDTH = D_MODEL
RWKV_WIDTH = MIX_WIDTH // 2
N_RWKV_HEADS = RWKV_WIDTH // HEAD_DIM
SWA_WIDTH = MIX_WIDTH // 4
N_SWA_HEADS = SWA_WIDTH // HEAD_DIM
N_SWA_KV_HEADS = N_SWA_HEADS // 2
SWA_GROUP = N_SWA_HEADS // N_SWA_KV_HEADS
SWA_KV_WIDTH = N_SWA_KV_HEADS * HEAD_DIM
MEM_WIDTH = MIX_WIDTH - RWKV_WIDTH - SWA_WIDTH
N_MEM_HEADS = MEM_WIDTH // HEAD_DIM
N_MEM = 256
WINDOW = 128
BLOCK = 128
N_BUCKETS = 32
MAX_DISTANCE = 128
DECAY_LORA = 64
AAA_LORA = 64
GATE_LORA = 128
RWKV_PROJ = 3 * RWKV_WIDTH + DECAY_LORA + AAA_LORA + GATE_LORA
SWA_PROJ = SWA_WIDTH + 2 * SWA_KV_WIDTH
IN_PROJ = RWKV_PROJ + SWA_PROJ + MEM_WIDTH
D_FF = 4 * D_MODEL
NORM_EPS = 1e-6
LNX_EPS = 64e-5
ATTN_SCALE = HEAD_DIM ** -0.5
R_SPLITS = [RWKV_WIDTH, 2 * RWKV_WIDTH, 3 * RWKV_WIDTH, 3 * RWKV_WIDTH + DECAY_LORA, 3 * RWKV_WIDTH + DECAY_LORA + AAA_LORA]

kernel_name = "hymba_rwkv7_swa_sink_memxattn_decode"


def rms_norm(x, g):
    xf = x.astype(jnp.float32)
    y = xf * lax.rsqrt(jnp.mean(xf * xf, -1, keepdims=True) + NORM_EPS)
    return (y * g.astype(jnp.float32)).astype(x.dtype)


def t5_bucket(dist):
    max_exact = N_BUCKETS // 2
    d = jnp.maximum(dist, 1).astype(jnp.float32)
    large = max_exact + (jnp.log(d / max_exact) / math.log(MAX_DISTANCE / max_exact) * (N_BUCKETS - max_exact)).astype(jnp.int32)
    large = jnp.minimum(large, N_BUCKETS - 1)
    return jnp.where(dist < max_exact, dist, large)


def rel_bias_logits(dist, rel_bias):
    b = rel_bias[t5_bucket(jnp.maximum(dist, 0))]
    return jnp.transpose(b, (2, 0, 1)).reshape(N_SWA_KV_HEADS, SWA_GROUP, dist.shape[0], dist.shape[1]).astype(jnp.float32)


def sink_softmax(s, sink):
    m = jnp.maximum(jnp.max(s, -1, keepdims=True), sink)
    e = jnp.exp(s - m)
    return e / (jnp.sum(e, -1, keepdims=True) + jnp.exp(sink - m))


def rwkv7_recurrence(s0, r, w, k, v, a, b):
    def step(s, inp):
        r_t, w_t, k_t, v_t, a_t, b_t = inp
        sa = jnp.einsum('bhvk,bhk->bhv', s, a_t)
        s = s * w_t[:, :, None, :] + sa[..., None] * b_t[:, :, None, :] + v_t[..., None] * k_t[:, :, None, :]
        return s, jnp.einsum('bhvk,bhk->bhv', s, r_t)
    xs = tuple(jnp.moveaxis(t, 1, 0) for t in (r, w, k, v, a, b))
    s_final, ys = lax.scan(step, s0, xs)
    return jnp.moveaxis(ys, 0, 1), s_final


def rwkv7_mix(p, shift0, s0, mu, w0, w_up_w, a0, w_up_a, w_up_g, k_k, k_a, r_k, lnx_w, lnx_b):
    B, T, _ = p.shape
    prev = jnp.concatenate([shift0[:, None, :].astype(p.dtype), p[:, :-1]], axis=1)
    xs = (p + (prev - p) * mu).astype(jnp.float32)
    r, k, v, wd, ad, gd = jnp.split(xs, R_SPLITS, axis=-1)
    heads = lambda t: t.reshape(B, T, N_RWKV_HEADS, HEAD_DIM)
    logw = -jnp.exp(-jax.nn.softplus(-(w0 + jnp.tanh(wd) @ w_up_w)) - 0.5)
    a = jax.nn.sigmoid(a0 + ad @ w_up_a)
    g = jax.nn.sigmoid(gd) @ w_up_g
    kk = heads(k * k_k)
    kk = kk / jnp.maximum(jnp.sqrt(jnp.sum(kk * kk, -1, keepdims=True)), 1e-12)
    k2 = heads(k * (1.0 + (a - 1.0) * k_a))
    rh, vh, ah = heads(r), heads(v), heads(a)
    y, s_new = rwkv7_recurrence(s0.astype(jnp.float32), rh, heads(jnp.exp(logw)), k2, vh, -kk, kk * ah)
    m = jnp.mean(y, -1, keepdims=True)
    var = jnp.mean(jnp.square(y - m), -1, keepdims=True)
    yn = ((y - m) * lax.rsqrt(var + LNX_EPS)).reshape(B, T, RWKV_WIDTH) * lnx_w + lnx_b
    bonus = (jnp.sum(rh * k2 * r_k, -1, keepdims=True) * vh).reshape(B, T, RWKV_WIDTH)
    out = (yn + bonus) * g
    return out.astype(p.dtype), p[:, -1], s_new.astype(s0.dtype)


def swa_prompt(q, k, v, rel_bias, sinks):
    B, T = q.shape[0], q.shape[1]
    nb = T // BLOCK
    qb = q.reshape(B, nb, BLOCK, N_SWA_KV_HEADS, SWA_GROUP, HEAD_DIM)
    kb = k.reshape(B, nb, BLOCK, N_SWA_KV_HEADS, HEAD_DIM)
    vb = v.reshape(B, nb, BLOCK, N_SWA_KV_HEADS, HEAD_DIM)
    pad = ((0, 0), (1, 0), (0, 0), (0, 0), (0, 0))
    kcat = jnp.concatenate([jnp.pad(kb[:, :-1], pad), kb], axis=2)
    vcat = jnp.concatenate([jnp.pad(vb[:, :-1], pad), vb], axis=2)
    s = jnp.einsum('bnqhgd,bnkhd->bnhgqk', qb, kcat).astype(jnp.float32) * ATTN_SCALE
    n = jnp.arange(nb)[:, None, None]
    qi = jnp.arange(BLOCK)[None, :, None]
    kj = jnp.arange(2 * BLOCK)[None, None, :]
    dist = BLOCK + qi - kj
    kpos = (n - 1) * BLOCK + kj
    mask = (dist >= 0) & (dist <= WINDOW) & (kpos >= 0)
    bias = rel_bias_logits(dist[0], rel_bias)
    s = jnp.where(mask[None, :, None, None], s + bias[None, None], -jnp.inf)
    p = sink_softmax(s, sinks.reshape(N_SWA_KV_HEADS, SWA_GROUP, 1, 1).astype(jnp.float32))
    o = jnp.einsum('bnhgqk,bnkhd->bnqhgd', p.astype(v.dtype), vcat)
    return o.reshape(B, T, SWA_WIDTH)


def swa_decode(q, k_new, v_new, k_buf, v_buf, rel_bias, sinks):
    Bd, S = q.shape[0], q.shape[1]
    kc = jnp.concatenate([k_buf.astype(k_new.dtype), k_new], axis=1)
    vc = jnp.concatenate([v_buf.astype(v_new.dtype), v_new], axis=1)
    qpos = PAST_LEN + jnp.arange(S)
    kpos = PAST_LEN - WINDOW + jnp.arange(WINDOW + S)
    dist = qpos[:, None] - kpos[None, :]
    mask = (dist >= 0) & (dist <= WINDOW) & (kpos >= 0)[None, :]
    qg = q.reshape(Bd, S, N_SWA_KV_HEADS, SWA_GROUP, HEAD_DIM)
    s = jnp.einsum('bqhgd,bkhd->bhgqk', qg, kc).astype(jnp.float32) * ATTN_SCALE
    s = jnp.where(mask, s + rel_bias_logits(dist, rel_bias)[None], -jnp.inf)
    p = sink_softmax(s, sinks.reshape(N_SWA_KV_HEADS, SWA_GROUP, 1, 1).astype(jnp.float32))
    o = jnp.einsum('bhgqk,bkhd->bqhgd', p.astype(vc.dtype), vc).reshape(Bd, S, SWA_WIDTH)
    return o, kc[:, -WINDOW:], vc[:, -WINDOW:]


def memory_kv(mem, g, w_kv, k_g):
    Bm, M, _ = mem.shape
    mk, mv = jnp.split(rms_norm(mem, g) @ w_kv, 2, axis=-1)
    mk = rms_norm(mk.reshape(Bm, M, N_MEM_HEADS, HEAD_DIM), k_g)
    return mk, mv.reshape(Bm, M, N_MEM_HEADS, HEAD_DIM)


def mem_attend(q, mk, mv):
    s = jnp.einsum('bqhd,bkhd->bhqk', q, mk.astype(q.dtype)).astype(jnp.float32) * ATTN_SCALE
    p = jax.nn.softmax(s, axis=-1)
    return jnp.einsum('bhqk,bkhd->bqhd', p.astype(q.dtype), mv.astype(q.dtype))


def trunk_layer(x, s_rwkv, s_shift, k_buf, v_buf, mem_k, mem_v, rel_bias, lp):
    B, T, _ = x.shape
    h = rms_norm(x, lp['norm1_g'])
    proj = h @ lp['w_in']
    p_rwkv, p_swa, p_mem = jnp.split(proj, [RWKV_PROJ, RWKV_PROJ + SWA_PROJ], axis=-1)
    y_r, shift_new, s_new = rwkv7_mix(p_rwkv, s_shift, s_rwkv, lp['mu_shift'], lp['w0'], lp['w_up_w'], lp['a0'],
                                      lp['w_up_a'], lp['w_up_g'], lp['k_k'], lp['k_a'], lp['r_k'], lp['lnx_w'], lp['lnx_b'])
    q_s, k_s, v_s = jnp.split(p_swa, [SWA_WIDTH, SWA_WIDTH + SWA_KV_WIDTH], axis=-1)
    q_s = rms_norm(q_s.reshape(B, T, N_SWA_HEADS, HEAD_DIM), lp['q_norm_swa'])
    k_s = rms_norm(k_s.reshape(B, T, N_SWA_KV_HEADS, HEAD_DIM), lp['k_norm_swa'])
    v_s = v_s.reshape(B, T, N_SWA_KV_HEADS, HEAD_DIM)
    if k_buf is None:
        y_s = swa_prompt(q_s, k_s, v_s, rel_bias, lp['sinks'])
        kb, vb = k_s[:, -WINDOW:], v_s[:, -WINDOW:]
    else:
        y_s, kb, vb = swa_decode(q_s, k_s, v_s, k_buf, v_buf, rel_bias, lp['sinks'])
    q_m = rms_norm(p_mem.reshape(B, T, N_MEM_HEADS, HEAD_DIM), lp['q_norm_mem'])
    y_m = mem_attend(q_m, mem_k, mem_v).reshape(B, T, MEM_WIDTH)
    mix = jnp.concatenate([y_r.astype(x.dtype), y_s.astype(x.dtype), y_m.astype(x.dtype)], axis=-1)
    x = x + mix @ lp['w_out']
    h2 = rms_norm(x, lp['norm2_g'])
    x = x + jnp.square(jax.nn.relu(h2 @ lp['w_ff1'])) @ lp['w_ff2']
    return x, s_new, shift_new, kb, vb


def setup_inputs(seed: int = 0) -> dict:
    key = jax.random.key(seed)
    ks = iter(jax.random.split(key, 48))
    nrm = lambda shape, scale: jax.random.normal(next(ks), shape, jnp.float32) * scale
    return {
        "x_prompt": nrm((BATCH, SEQ, D_MODEL), 1.0),
        "x_sample": nrm((DEC_BATCH, DEC_SEQ, D_MODEL), 1.0),
        "state_rwkv": nrm((DEPTH, DEC_BATCH, N_RWKV_HEADS, HEAD_DIM, HEAD_DIM), 0.3),
        "state_shift": nrm((DEPTH, DEC_BATCH, RWKV_PROJ), 1.0),
        "cache_swa_k": nrm((DEPTH, DEC_BATCH, WINDOW, N_SWA_KV_HEADS, HEAD_DIM), 1.0),
        "cache_swa_v": nrm((DEPTH, DEC_BATCH, WINDOW, N_SWA_KV_HEADS, HEAD_DIM), 1.0),
        "cache_mem_k": nrm((DEPTH, DEC_BATCH, N_MEM, N_MEM_HEADS, HEAD_DIM), 1.0),
        "cache_mem_v": nrm((DEPTH, DEC_BATCH, N_MEM, N_MEM_HEADS, HEAD_DIM), 1.0),
        "mem_prompt": nrm((BATCH, N_MEM, D_MODEL), 1.0),
        "rel_bias": nrm((N_BUCKETS, N_SWA_HEADS), 0.5),
        "norm1_g": 1.0 + nrm((DEPTH, D_MODEL), 0.02),
        "w_in": nrm((DEPTH, D_MODEL, IN_PROJ), D_MODEL ** -0.5),
        "mu_shift": jax.random.uniform(next(ks), (DEPTH, RWKV_PROJ), jnp.float32, 0.0, 1.0),
        "w0": nrm((DEPTH, RWKV_WIDTH), 0.5),
        "w_up_w": nrm((DEPTH, DECAY_LORA, RWKV_WIDTH), 0.1),
        "a0": nrm((DEPTH, RWKV_WIDTH), 0.5),
        "w_up_a": nrm((DEPTH, AAA_LORA, RWKV_WIDTH), AAA_LORA ** -0.5),
        "w_up_g": nrm((DEPTH, GATE_LORA, RWKV_WIDTH), GATE_LORA ** -0.5),
        "k_k": 0.85 + nrm((DEPTH, RWKV_WIDTH), 0.02),
        "k_a": 1.0 + nrm((DEPTH, RWKV_WIDTH), 0.02),
        "r_k": nrm((DEPTH, N_RWKV_HEADS, HEAD_DIM), 0.1),
        "lnx_w": 1.0 + nrm((DEPTH, RWKV_WIDTH), 0.02),
        "lnx_b": nrm((DEPTH, RWKV_WIDTH), 0.02),
        "q_norm_swa": 1.0 + nrm((DEPTH, HEAD_DIM), 0.02),
        "k_norm_swa": 1.0 + nrm((DEPTH, HEAD_DIM), 0.02),
        "sinks": nrm((DEPTH, N_SWA_HEADS), 0.5),
        "mem_norm_g": 1.0 + nrm((DEPTH, D_MODEL), 0.02),
        "w_mem_kv": nrm((DEPTH, D_MODEL, 2 * MEM_WIDTH), D_MODEL ** -0.5),
        "q_norm_mem": 1.0 + nrm((DEPTH, HEAD_DIM), 0.02),
        "k_norm_mem": 1.0 + nrm((DEPTH, HEAD_DIM), 0.02),
        "w_out": nrm((DEPTH, MIX_WIDTH, D_MODEL), MIX_WIDTH ** -0.5),
        "norm2_g": 1.0 + nrm((DEPTH, D_MODEL), 0.02),
        "w_ff1": nrm((DEPTH, D_MODEL, D_FF), D_MODEL ** -0.5),
        "w_ff2": nrm((DEPTH, D_FF, D_MODEL), D_FF ** -0.5),
    }


def reference(x_prompt, x_sample, state_rwkv, state_shift, cache_swa_k, cache_swa_v, cache_mem_k, cache_mem_v,
              mem_prompt, rel_bias, norm1_g, w_in, mu_shift, w0, w_up_w, a0, w_up_a, w_up_g, k_k, k_a, r_k,
              lnx_w, lnx_b, q_norm_swa, k_norm_swa, sinks, mem_norm_g, w_mem_kv, q_norm_mem, k_norm_mem,
              w_out, norm2_g, w_ff1, w_ff2):
    yp, ys = x_prompt, x_sample
    srp, shp, kbp, vbp, mkp, mvp = [], [], [], [], [], []
    srs, shs, kbs, vbs = [], [], [], []
    for l in range(DEPTH):
        lp = dict(norm1_g=norm1_g[l], w_in=w_in[l], mu_shift=mu_shift[l], w0=w0[l], w_up_w=w_up_w[l], a0=a0[l],
                  w_up_a=w_up_a[l], w_up_g=w_up_g[l], k_k=k_k[l], k_a=k_a[l], r_k=r_k[l], lnx_w=lnx_w[l],
                  lnx_b=lnx_b[l], q_norm_swa=q_norm_swa[l], k_norm_swa=k_norm_swa[l], sinks=sinks[l],
                  q_norm_mem=q_norm_mem[l], w_out=w_out[l], norm2_g=norm2_g[l], w_ff1=w_ff1[l], w_ff2=w_ff2[l])
        mk, mv = memory_kv(mem_prompt, mem_norm_g[l], w_mem_kv[l], k_norm_mem[l])
        s0 = jnp.zeros((BATCH, N_RWKV_HEADS, HEAD_DIM, HEAD_DIM), x_prompt.dtype)
        sh0 = jnp.zeros((BATCH, RWKV_PROJ), x_prompt.dtype)
        yp, s_new, sh_new, kb, vb = trunk_layer(yp, s0, sh0, None, None, mk, mv, rel_bias, lp)
        srp.append(s_new); shp.append(sh_new); kbp.append(kb); vbp.append(vb); mkp.append(mk); mvp.append(mv)
        ys, s_new, sh_new, kb, vb = trunk_layer(ys, state_rwkv[l], state_shift[l], cache_swa_k[l], cache_swa_v[l],
                                                cache_mem_k[l], cache_mem_v[l], rel_bias, lp)
        srs.append(s_new); shs.append(sh_new); kbs.append(kb); vbs.append(vb)
    return (yp, ys, jnp.stack(srp), jnp.stack(shp), jnp.stack(kbp), jnp.stack(vbp), jnp.stack(mkp), jnp.stack(mvp),
            jnp.stack(srs), jnp.stack(shs), jnp.stack(kbs), jnp.stack(vbs))
```

```python
import numpy as np
import concourse.bass as bass
import concourse.mybir as mybir
from concourse.bass_utils import run_bass_kernel_spmd

F32 = mybir.dt.float32
BF16 = mybir.dt.bfloat16
AF = mybir.ActivationFunctionType
ALU = mybir.AluOpType
AX = mybir.AxisListType


class Prog:
    CE = ("pe", "act", "dve", "pool")

    def __init__(self, nc, n_dma_sems=12):
        self.nc = nc
        self.sem = {e: nc.alloc_semaphore(name=f"s_{e}") for e in self.CE}
        self.cnt = {e: 0 for e in self.CE}
        self.dsem = {q: [nc.alloc_semaphore(name=f"d_{q}{i}") for i in range(n_dma_sems)]
                     for q in ("sp", "pool", "act")}
        self.dval = {q: [0] * n_dma_sems for q in ("sp", "pool", "act")}
        self.dnext = {q: 0 for q in ("sp", "pool", "act")}
        self.semobj = {}
        for e in self.CE:
            self.semobj[("c", e)] = self.sem[e]
        for q in self.dsem:
            for i, s in enumerate(self.dsem[q]):
                self.semobj[("d", q, i)] = s
        self.ops = {e: [] for e in ("pe", "act", "dve", "pool", "sp")}
        self.lastw = {}
        self.readers = {}
        self.seen = {e: {} for e in self.ops}
        self.nops = 0

    def _deps(self, eng, reads, writes):
        need = {}

        def add(tok):
            sid, val = tok
            if need.get(sid, 0) < val:
                need[sid] = val

        for k in reads:
            t = self.lastw.get(k)
            if t is not None:
                add(t)
        for k in writes:
            t = self.lastw.get(k)
            if t is not None:
                add(t)
            for sid, val in self.readers.get(k, {}).items():
                add((sid, val))
        out = []
        seen = self.seen[eng]
        for sid, val in need.items():
            if eng == "pe" and sid == ("c", "pe"):
                continue
            if seen.get(sid, 0) >= val:
                continue
            seen[sid] = val
            out.append((sid, val))
        return out

    def _commit(self, tok, reads, writes):
        sid, val = tok
        for k in writes:
            self.lastw[k] = tok
            self.readers[k] = {}
        for k in reads:
            r = self.readers.setdefault(k, {})
            if r.get(sid, 0) < val:
                r[sid] = val

    enabled = True

    def op(self, eng, fn, r=(), w=()):
        if not self.enabled:
            return
        if eng in ("act", "dve"):
            banks = [k for k in list(r) + list(w) if isinstance(k, str) and len(k) == 2 and k[0] == "b"]
            if banks:
                w = list(w) + [("psrd", k) for k in banks]
        waits = self._deps(eng, r, w)
        self.cnt[eng] += 1
        tok = (("c", eng), self.cnt[eng])
        self.ops[eng].append((waits, fn, ("c", eng), 1))
        self._commit(tok, r, w)
        self.nops += 1

    def dma(self, q, out, in_, r=(), w=(), **kw):
        if not self.enabled:
            return
        i = self.dnext[q]
        n = len(self.dsem[q])
        self.dnext[q] = (i + 1) % n
        sid = ("d", q, i)
        waits = self._deps(q, r, w)
        prev = self.dval[q][i]
        if prev > 0 and self.seen[q].get(sid, 0) < prev:
            self.seen[q][sid] = prev
            waits.append((sid, prev))
        self.dval[q][i] += 16
        tok = (sid, self.dval[q][i])
        self.ops[q].append((waits, (lambda e, o=out, s=in_, kw=kw: e.dma_start(out=o, in_=s, **kw)), sid, 16))
        self._commit(tok, r, w)
        self.nops += 1

    def barrier(self):
        toks = []
        for q in self.dsem:
            for i in range(len(self.dsem[q])):
                if self.dval[q][i] > 0:
                    toks.append((("d", q, i), self.dval[q][i]))
        for e in self.CE:
            if self.cnt[e] > 0:
                toks.append((("c", e), self.cnt[e]))
        for e in self.ops:
            w = [(sid, v) for sid, v in toks if self.seen[e].get(sid, 0) < v]
            for sid, v in w:
                self.seen[e][sid] = v
            self.ops[e].append((w, None, None, 0))

    def emit(self):
        nc = self.nc
        fin = []
        for q in self.dsem:
            for i in range(len(self.dsem[q])):
                if self.dval[q][i] > 0:
                    fin.append((("d", q, i), self.dval[q][i]))
        for e in self.CE:
            if self.cnt[e] > 0:
                fin.append((("c", e), self.cnt[e]))
        ops = self.ops
        semobj = self.semobj

        def run(eng, lst, final=None):
            for waits, fn, sid, inc in lst:
                for ws, wv in waits:
                    eng.wait_ge(semobj[ws], wv)
                if fn is not None:
                    if inc == 0:
                        fn(eng)
                    else:
                        fn(eng).then_inc(semobj[sid], inc)
            if final:
                for ws, wv in final:
                    eng.wait_ge(semobj[ws], wv)

        with nc.Block() as block:
            @block.sync
            def _(e):
                run(e, ops["sp"], fin)

            @block.tensor
            def _(e):
                run(e, ops["pe"])

            @block.scalar
            def _(e):
                run(e, ops["act"])

            @block.vector
            def _(e):
                run(e, ops["dve"])

            @block.gpsimd
            def _(e):
                run(e, ops["pool"])
        for e in self.ops:
            self.ops[e] = []

    pe_mode = None

    def _pe_mode(self, st):
        if not self.enabled:
            return
        ru = lambda n: 32 if n <= 32 else (64 if n <= 64 else 128)
        k = st.partition_size()
        m = st.free_size()
        mode = (ru(k), ru(m))
        if self.pe_mode is not None and mode != self.pe_mode:
            self.ops["pe"].append(([], (lambda e: e.drain()), None, 0))
        self.pe_mode = mode

    def mm(self, out, lhsT, rhs, start=True, stop=True, r=(), w=()):
        self._pe_mode(lhsT)
        self.op("pe", lambda e: e.matmul(out, lhsT, rhs, start=start, stop=stop), r, w)

    def tr(self, out, in_, ident, r=(), w=()):
        self._pe_mode(in_)
        self.op("pe", lambda e: e.transpose(out, in_, ident), r, w)

    def act(self, out, in_, func, r=(), w=(), **kw):
        self.op("act", lambda e: e.activation(out=out, in_=in_, func=func, **kw), r, w)

    def tt(self, eng, out, in0, in1, op, r=(), w=()):
        self.op(eng, lambda e: e.tensor_tensor(out=out, in0=in0, in1=in1, op=op), r, w)

    def ts(self, eng, out, in0, s1, op0, s2=None, op1=None, r=(), w=()):
        if op1 is None:
            self.op(eng, lambda e: e.tensor_scalar(out=out, in0=in0, scalar1=s1, scalar2=None, op0=op0), r, w)
        else:
            self.op(eng, lambda e: e.tensor_scalar(out=out, in0=in0, scalar1=s1, scalar2=s2, op0=op0, op1=op1), r, w)

    def stt(self, out, in0, scalar, in1, op0, op1, r=(), w=()):
        self.op("dve", lambda e: e.scalar_tensor_tensor(out=out, in0=in0, scalar=scalar, in1=in1, op0=op0, op1=op1), r, w)

    def copy(self, eng, out, in_, r=(), w=()):
        if eng == "act":
            self.op("act", lambda e: e.copy(out=out, in_=in_), r, w)
        else:
            self.op(eng, lambda e: e.tensor_scalar(out=out, in0=in_, scalar1=1.0, scalar2=None, op0=ALU.mult), r, w)

    def recip(self, out, in_, r=(), w=()):
        self.op("dve", lambda e: e.reciprocal(out=out, in_=in_), r, w)


L = 8192
D = 1024
TB = 256
C = 64
NCH = TB // C
EXPM05 = float(np.exp(-0.5))
SCALE = 0.125

PC = {}
_o = 0
for _n, _w in [("mu", 14), ("omu", 14), ("w0", 4), ("a0", 4), ("kk", 4), ("ka", 4), ("omka", 4), ("rk", 4),
               ("lnw", 4), ("lnb", 4), ("qns", 1), ("kns", 1), ("qnm", 1), ("knm", 1), ("sink", 2), ("esk", 2)]:
    PC[_n] = _o
    _o += _w
NPC = _o


def t5_bucket_np(dist):
    dist = np.asarray(dist)
    d = np.maximum(dist, 1).astype(np.float32)
    large = 16 + (np.log(d / np.float32(16)) / np.float32(np.log(128 / 16)) * np.float32(16)).astype(np.int32)
    large = np.minimum(large, 31)
    return np.where(dist < 16, dist, large)


def build(nblk=L // TB, do_ffn=True, stage=9, do_samp=True):
    nc = bass.Bass("TRN2", target_bir_lowering=False)
    NTOK = nblk * TB

    def din(name, shape, dt=F32):
        return nc.dram_tensor(name, list(shape), dt, kind="ExternalInput").ap()

    def dout(name, shape, dt=F32):
        return nc.dram_tensor(name, list(shape), dt, kind="ExternalOutput").ap()

    xseq = din("xseq", [L, D])
    mem = din("mem", [256, D])
    w_in = din("w_in", [D, 2560])
    w_out = din("w_out", [D, D])
    w_ff1 = din("w_ff1", [D, 4096])
    w_ff2 = din("w_ff2", [4096, D])
    w_mem = din("w_mem", [D, 512])
    lwa_d = din("lwa", [128, 512])
    lwg_d = din("lwg", [128, 512])
    relb = din("relb", [32, 4])
    pc_d = din("pc", [128, NPC])
    g1bc_d = din("g1bc", [128, D])
    g2bc_d = din("g2bc", [128, D])
    gmbc_d = din("gmbc", [128, D])
    ident_d = din("ident", [128, 128])
    maskg_d = din("maskg", [128, 128])
    maskn_d = din("maskn", [128, 64])
    identd_d = din("identd", [128, 64])
    bavg_d = din("bavg", [128, 128])
    ones_d = din("ones", [128, 64])
    rmask_d = din("rmask", [128, TB])
    oneh_d = din("oneh", [32, 129])
    knmbc_d = din("knmbc", [128, 256])
    aident_d = din("aident", [128, 128])

    xsamp = din("xsamp", [16, D])
    srs_in = din("srs_in", [16, 8, 64, 64])
    shift_s = din("shift_s", [16, 1792])
    ck_in = din("ck_in", [16, 128, 2, 64])
    cv_in = din("cv_in", [16, 128, 2, 64])
    cmk_in = din("cmk_in", [16, 256, 4, 64])
    cmv_in = din("cmv_in", [16, 256, 4, 64])
    lnwbh_d = din("lnwbh", [128, 64])
    lnbbh_d = din("lnbbh", [128, 64])
    skd_d = din("skd", [64, 1])
    onehr_d = din("onehr", [32, 129])
    y_s = dout("y_s", [16, D])
    srs = dout("srs", [16, 8, 64, 64])
    shs = dout("shs", [16, 1792])
    kbs = dout("kbs", [16, 128, 128])
    vbs = dout("vbs", [16, 128, 128])
    y_p = dout("y_p", [L, D])
    srp = dout("srp", [8, 64, 64])
    shp = dout("shp", [1792])
    kbp = dout("kbp", [128, 128])
    vbp = dout("vbp", [128, 128])
    mkp = dout("mkp", [256, 256])
    mvp = dout("mvp", [256, 256])

    mixD = nc.dram_tensor("mixD", [8, 128, L], BF16, kind="Internal").ap()
    fscr = nc.dram_tensor("fscr", [4, 512], F32, kind="Internal").ap()
    fdscr = nc.dram_tensor("fdscr", [4, 129], F32, kind="Internal").ap()
    scrV = nc.dram_tensor("scrV", [16, 8, 8, 64], F32, kind="Internal").ap()
    scrMix = nc.dram_tensor("scrMix", [16, 512], F32, kind="Internal").ap()
    scrQ = nc.dram_tensor("scrQ", [16, 512], F32, kind="Internal").ap()
    scrK = nc.dram_tensor("scrK", [16, 128], F32, kind="Internal").ap()
    scrVn = nc.dram_tensor("scrVn", [16, 128], F32, kind="Internal").ap()
    scrO = nc.dram_tensor("scrO", [16, 256], F32, kind="Internal").ap()
    scrOm = nc.dram_tensor("scrOm", [16, 256], F32, kind="Internal").ap()

    P = Prog(nc)

    def pcol(name, j=0):
        return pc[:, PC[name] + j:PC[name] + j + 1]

    from contextlib import ExitStack
    with ExitStack() as top:
        def sb(name, shape, dt=F32, st=top):
            return st.enter_context(nc.sbuf_tensor("s_" + name, list(shape), dt))

        ident = sb("ident", [128, 128])
        identb = sb("identb", [128, 128], BF16)
        pc = sb("pc", [128, NPC])
        mixTs = sb("mixTs", [128, 8, 16], BF16)
        ps = [top.enter_context(nc.psum_tensor(f"b{i}", [128, 512], F32)) for i in range(7)]
        psT = top.enter_context(nc.psum_tensor("bT", [128, 1024], BF16))
        P.dma("sp", ident[:], ident_d, w=["ident"])
        P.dma("pool", identb[:], ident_d, w=["identb"])
        P.dma("sp", pc[:], pc_d, w=["pc"])
        P.ts("dve", pc[:, PC["omu"]:PC["omu"] + 14], pc[:, PC["mu"]:PC["mu"] + 14], -1.0, ALU.mult, 1.0, ALU.add, r=["pc"], w=["pc"])
        P.ts("dve", pc[:, PC["omka"]:PC["omka"] + 4], pc[:, PC["ka"]:PC["ka"] + 4], -1.0, ALU.mult, 1.0, ALU.add, r=["pc"], w=["pc"])
        P.act(pc[:, PC["esk"]:PC["esk"] + 2], pc[:, PC["sink"]:PC["sink"] + 2], AF.Exp, r=["pc"], w=["pc"])

        def norm_rows(x_t, xn_t, gbc, key_x, key_xn, ss, nrows=128):
            P.act(xn_t[0:nrows, :], x_t[0:nrows, :], AF.Square, r=[key_x], w=[key_xn, "ss"], accum_out=ss[0:nrows, 0:1])
            P.act(ss[0:nrows, 1:2], ss[0:nrows, 0:1], AF.Sqrt, r=["ss"], w=["ss"], scale=1.0 / D, bias=1e-6)
            P.recip(ss[0:nrows, 2:3], ss[0:nrows, 1:2], r=["ss"], w=["ss"])
            P.stt(xn_t[0:nrows, :], x_t[0:nrows, :], ss[0:nrows, 2:3], gbc[0:nrows, :], ALU.mult, ALU.mult, r=[key_x, "ss", "gbc"], w=[key_xn])

        def headnorm(psx, ncols, gcol, outs, tmps, keyp, eps=1e-6):
            sq, sd = tmps
            P.act(sq[:, 0:ncols], psx, AF.Square, r=[keyp], w=["hn_sq"])
            P.mm(ps[3][:, 0:ncols], bavg[:], sq[:, 0:ncols], r=["bavg", "hn_sq"], w=["b3"])
            P.act(sd[:, 0:ncols], ps[3][:, 0:ncols], AF.Ln, r=["b3"], w=["hn_sd"], bias=eps)
            P.act(sd[:, 0:ncols], sd[:, 0:ncols], AF.Exp, r=["hn_sd"], w=["hn_sd"], scale=-0.5)
            for o, k in outs:
                P.stt(o, psx, gcol, sd[:, 0:ncols], ALU.mult, ALU.mult, r=[keyp, "hn_sd", "pc"], w=[k])

        with ExitStack() as ph1:
            def s1(name, shape, dt=F32):
                return sb(name, shape, dt, ph1)

            win = s1("win", [128, 8, 2560], BF16)
            lwa = s1("lwab", [128, 512], BF16)
            lwg = s1("lwgb", [128, 512], BF16)
            maskg = s1("maskg", [128, 128])
            maskn = s1("maskn", [128, 64])
            bavg = s1("bavg", [128, 128])
            ones = s1("onesb", [128, 64], BF16)
            rmask = s1("rmask", [128, TB])
            gbc = s1("gbc", [128, D])
            biasT = s1("biasT", [128, 1024])
            ss = s1("ss", [128, 4])
            xbuf = [s1(f"xb{i}", [128, D]) for i in range(2)]
            xn = s1("xn", [128, D], BF16)
            hTb = [s1("hT0", [128, 8, TB], BF16), s1("hT1", [128, 8, TB], BF16)]
            pT = s1("pT", [128, 14, TB + 1])
            xs = s1("xs", [128, 14, TB])
            lin = s1("lin", [128, TB], BF16)
            sg = s1("sg", [128, TB], BF16)
            tmp = [s1(f"t{i}", [128, TB]) for i in range(12)]
            hnA = s1("hnA", [128, TB])
            hnB = s1("hnB", [128, TB])
            gT = s1("gT", [128, 4, TB])
            bonus = s1("bonus", [128, 4, TB])
            epos = s1("epos", [128, 4, TB])
            AR = s1("AR", [128, 4, NCH, 2, C], BF16)
            BK = s1("BK", [128, 4, NCH, 2, C], BF16)
            GB = s1("GB", [128, 2, 4, 128], BF16)
            GK = s1("GK", [128, 2, 4, 128], BF16)
            Ab = [s1(f"Ab{i}", [128, 2, 4, 64], BF16) for i in range(2)]
            Bb = [s1(f"Bb{i}", [128, 2, 4, 64], BF16) for i in range(2)]
            PTt = s1("PTt", [128, 2, 4, 64], BF16)
            Btk = s1("Btk", [128, 2, 4, 64], BF16)
            Ktk = s1("Ktk", [128, 2, 4, 64], BF16)
            Vtk = s1("Vtk", [128, 2, 4, 64], BF16)
            Utk = s1("Utk", [128, 4, 64], BF16)
            Zs = s1("Zs", [128, 4, 64], BF16)
            Zf = s1("Zf", [128, 4, 64])
            STb = s1("STb", [128, 4, 64], BF16)
            vb = s1("vb", [128, 4, TB], BF16)
            identd = s1("identd", [128, 64])
            yT = s1("yT", [128, 4, TB])
            ST = [s1(f"ST{i}", [128, 4, 64]) for i in range(2)]
            QT = s1("QT", [128, 4, TB], BF16)
            KTr = s1("KTr", [128, 3, 128], BF16)
            KTf = s1("KTf", [128, TB])
            Vs = s1("Vs", [128, 3, 128], BF16)
            Vf = s1("Vf", [128, 128])
            tmpS = s1("tmpS", [128, 1024])
            PTb = s1("PTb", [128, 1024], BF16)
            den = s1("den", [128, 256])
            mkT = s1("mkT", [128, 2, 256], BF16)
            mv = s1("mv", [128, 2, 256], BF16)
            mixT = s1("mixT", [128, 8, TB], BF16)

            for kc in range(8):
                P.dma("pool", win[:, kc, :], w_in[kc * 128:(kc + 1) * 128, :], w=[("win", kc)], max_dma_last_dim=4096)
            P.dma("pool", lwa[:], lwa_d, w=["lwa"])
            P.dma("pool", lwg[:], lwg_d, w=["lwg"])
            P.dma("pool", ones[:], ones_d, w=["ones"])
            P.dma("sp", maskg[:], maskg_d, w=["maskg"])
            P.dma("sp", maskn[:], maskn_d, w=["maskn"])
            P.dma("sp", identd[:], identd_d, w=["identd"])
            P.dma("sp", bavg[:], bavg_d, w=["bavg"])
            P.dma("sp", rmask[:], rmask_d, w=["rmask"])

            P.enabled = stage >= 1
            fpad = tmp[0]
            relb_s = tmp[1]
            oneh_s = tmp[2]
            P.dma("sp", relb_s[0:32, 0:4], relb, w=["t1"])
            P.dma("sp", oneh_s[0:32, 0:129], oneh_d, w=["t2"])
            P.op("dve", lambda e: e.memset(tmpS[0:4, 0:512], -30000.0), w=["tmpS"])
            P.mm(ps[0][0:4, 0:129], relb_s[0:32, 0:4], oneh_s[0:32, 0:129], r=["t1", "t2"], w=["b0"])
            P.copy("dve", tmpS[0:4, 128:257], ps[0][0:4, 0:129], r=["b0"], w=["tmpS"])
            P.dma("sp", fscr, tmpS[0:4, 0:512], r=["tmpS"], w=["fscr"])
            aid = tmp[3]
            P.dma("sp", aid[:, 0:128], aident_d, w=["t3"])
            for j in range(2):
                for hh in range(2):
                    qh = 2 * hh + j
                    for blk in range(2):
                        idx = (hh * 2 + j) * 2 + blk
                        base = 256 if blk == 0 else 128
                        src = bass.AP(fscr.tensor, qh * 512 + base - 127, [[1, 128], [1, 128]])
                        P.dma("sp", PTb[:, 0:256].bitcast(F32)[:, 0:128] if False else tmp[4 + (idx % 2)][:, 0:128], src, r=["fscr"], w=[f"t{4 + idx % 2}"])
                        bank = idx // 4
                        P.mm(ps[bank][:, (idx % 4) * 128:(idx % 4 + 1) * 128], aid[:, 0:128], tmp[4 + (idx % 2)][:, 0:128], r=["t3", f"t{4 + idx % 2}"], w=[f"b{bank}"])
            for bank in range(2):
                P.copy("act", biasT[:, bank * 512:(bank + 1) * 512], ps[bank][:, :], r=[f"b{bank}"], w=["biasT"])
            P.enabled = stage >= 2
            P.dma("sp", gbc[:], gmbc_d, w=["gbc"])
            phM = ExitStack()
            knmbc = sb("knmbc", [128, 256], F32, phM)
            P.dma("sp", knmbc[:], knmbc_d, w=["knmbc"])
            mhT = sb("mhT", [128, 8, 256], BF16, phM)
            wmem = sb("wmemb", [128, 8, 512], BF16, phM)
            for kc in range(8):
                P.dma("pool", wmem[:, kc, :], w_mem[kc * 128:(kc + 1) * 128, :], w=[("wmem", kc)])
            for mt in range(2):
                xt = xbuf[mt]
                P.dma("sp", xt[:], mem[mt * 128:(mt + 1) * 128, :], w=[("xb", mt)])
                norm_rows(xt, xn, gbc, ("xb", mt), "xn", ss)
                for kc in range(8):
                    P.tr(psT[:, kc * 128:(kc + 1) * 128], xn[:, kc * 128:(kc + 1) * 128], identb[:], r=["xn", "identb"], w=["bT"])
                P.mm(ps[6][:, 0:16], identb[:, 0:128], identb[:, 0:16], r=["identb"], w=["bT", "b6"])
                P.copy("act", mhT[:, :, mt * 128:(mt + 1) * 128], psT[:].rearrange("p (k t) -> p k t", t=128), r=["bT"], w=["mhT"])
            P.enabled = stage >= 2.2
            for j in range(2):
                for kc in range(8):
                    P.mm(ps[0][:, 0:256], wmem[:, kc, j * 128:(j + 1) * 128], mhT[:, kc, :], start=(kc == 0), stop=(kc == 7),
                         r=[("wmem", kc), "mhT"], w=["b0"])
                headnorm(ps[0][:, 0:256], 256, pcol("knm"), [(mkT[:, j, :], "mkT")], (hnA, hnB), "b0")
            P.enabled = stage >= 2.4
            for mt in range(2):
                for kc in range(8):
                    P.mm(ps[1][:, 0:512], mhT[:, kc, mt * 128:(mt + 1) * 128], wmem[:, kc, :], start=(kc == 0), stop=(kc == 7),
                         r=[("wmem", kc), "mhT"], w=["b1"])
                P.copy("act", mv[:, mt, :], ps[1][:, 256:512], r=["b1"], w=["mv"])
                P.copy("act", tmp[5][:, 0:256], ps[1][:, 256:512], r=["b1"], w=["t5"])
                P.dma("sp", mvp[mt * 128:(mt + 1) * 128, :], tmp[5][:, 0:256], r=["t5"])
                P.act(tmp[6][:, 0:256], ps[1][:, 0:256], AF.Square, r=["b1"], w=["t6"])
                P.op("dve", lambda e: e.tensor_reduce(out=ss[:, 0:4], in_=tmp[6][:, 0:256].rearrange("p (h d) -> p h d", d=64), axis=AX.X, op=ALU.add),
                     r=["t6"], w=["ss"])
                P.act(ss[:, 0:4], ss[:, 0:4], AF.Sqrt, r=["ss"], w=["ss"], scale=1.0 / 64, bias=1e-6)
                P.recip(ss[:, 0:4], ss[:, 0:4], r=["ss"], w=["ss"])
                P.tt("dve", tmp[7][:, 0:256].rearrange("p (h d) -> p h d", d=64), ps[1][:, 0:256].rearrange("p (h d) -> p h d", d=64),
                     ss[:, 0:4].unsqueeze(2).to_broadcast([128, 4, 64]), ALU.mult, r=["b1", "ss"], w=["t7"])
                P.tt("dve", tmp[7][:, 0:256], tmp[7][:, 0:256], knmbc[:], ALU.mult, r=["t7", "knmbc"], w=["t7"])
                P.dma("sp", mkp[mt * 128:(mt + 1) * 128, :], tmp[7][:, 0:256], r=["t7"])

            P.enabled = True
            P.barrier()
            P.emit()
            phM.close()
            tmp = tmp + [s1(f"t{i}", [128, TB]) for i in range(12, 40)]
            P.enabled = stage >= 3
            P.dma("sp", gbc[:], g1bc_d, w=["gbc"])
            P.op("dve", lambda e: e.memset(pT[:, :, 0:1], 0.0), w=["pT"])
            P.op("dve", lambda e: e.memset(ST[0][:], 0.0), w=["ST0"])
            P.op("pool", lambda e: e.memset(KTr[:], 0.0), w=[("KTr", 0), ("KTr", 1), ("KTr", 2)])
            P.op("pool", lambda e: e.memset(Vs[:], 0.0), w=[("Vs", 0), ("Vs", 1), ("Vs", 2)])
            xs_flat = xs[:].rearrange("p a t -> p (a t)")

            def A0(b):
                hT_ = hTb[b % 2]
                for ti in range(2):
                    t = b * 2 + ti
                    xt = xbuf[ti]
                    P.dma("sp", xt[:], xseq[t * 128:(t + 1) * 128, :], w=[("xb", ti)])
                    norm_rows(xt, xn, gbc, ("xb", ti), "xn", ss)
                    for kc in range(8):
                        P.tr(psT[:, kc * 128:(kc + 1) * 128], xn[:, kc * 128:(kc + 1) * 128], identb[:], r=["xn", "identb"], w=["bT"])
                    P.copy("act", hT_[:, :, ti * 128:(ti + 1) * 128], psT[:].rearrange("p (k t) -> p k t", t=128), r=["bT"], w=[f"hT{b % 2}"])

            A0(0)
            for blk in range(nblk):
                hT = hTb[blk % 2]
                kh = f"hT{blk % 2}"
                def proj(oc, bank):
                    for kc in range(8):
                        P.mm(ps[bank][:, 0:TB], win[:, kc, oc * 128:(oc + 1) * 128], hT[:, kc, :], start=(kc == 0), stop=(kc == 7),
                             r=[("win", kc), kh], w=[f"b{bank}"])

                for oc in range(14):
                    bank = 5 + (oc % 2)
                    proj(oc, bank)
                    P.copy("act", pT[:, oc, 1:TB + 1], ps[bank][:, 0:TB], r=[f"b{bank}"], w=[("pT", oc)])
                    P.act(tmp[0][:], pT[:, oc, 0:TB], AF.Copy, r=[("pT", oc), "pT", "pc"], w=["t0"], scale=pcol("mu", oc))
                    P.stt(xs[:, oc, :], pT[:, oc, 1:TB + 1], pcol("omu", oc), tmp[0][:], ALU.mult, ALU.add, r=[("pT", oc), "t0", "pc"], w=[("xs", oc)])
                    P.copy("pool", pT[:, oc, 0:1], pT[:, oc, TB:TB + 1], r=[("pT", oc)], w=[("pT", oc)])

                P.enabled = stage >= 4
                for j in range(2):
                    proj(14 + j, 5)
                    headnorm(ps[5][:, 0:TB], TB, pcol("qns"), [(QT[:, j, :], "QT")], (hnA, hnB), "b5")
                proj(16, 5)
                for ti in range(2):
                    t = blk * 2 + ti
                    headnorm(ps[5][:, ti * 128:(ti + 1) * 128], 128, pcol("kns"),
                             [(KTr[:, t % 3, :], ("KTr", t % 3)), (KTf[:, ti * 128:(ti + 1) * 128], "KTf")], (hnA, hnB), "b5")
                for j in range(2):
                    proj(18 + j, 5)
                    headnorm(ps[5][:, 0:TB], TB, pcol("qnm"), [(QT[:, 2 + j, :], "QT")], (hnA, hnB), "b5")
                for ti in range(2):
                    t = blk * 2 + ti
                    for kc in range(8):
                        P.mm(ps[6][:, 0:128], hT[:, kc, ti * 128:(ti + 1) * 128], win[:, kc, 2176:2304], start=(kc == 0), stop=(kc == 7),
                             r=[("win", kc), kh], w=["b6"])
                    P.copy("act", Vs[:, t % 3, :], ps[6][:, 0:128], r=["b6"], w=[("Vs", t % 3)])
                    if blk == nblk - 1 and ti == 1:
                        P.copy("dve", Vf[:], ps[6][:, 0:128], r=["b6"], w=["Vf"])

                def attn_tail(o_lhs, nblkk, has_sink, mixc, tcols, okeys):
                    for j in range(2):
                        for hh in range(2):
                            for b in range(nblkk[0], 2):
                                idx = (hh * 2 + j) * 2 + b
                                P.mm(ps[2][hh * 64:(hh + 1) * 64, j * 128:(j + 1) * 128], o_lhs(j, hh, b), PTb[:, idx * 128:(idx + 1) * 128],
                                     start=(b == nblkk[0]), stop=(b == 1), r=["PTb"] + okeys, w=["b2"])
                            for b in range(nblkk[0], 2):
                                idx = (hh * 2 + j) * 2 + b
                                P.mm(ps[2][hh * 64:(hh + 1) * 64, 256 + j * 128:256 + (j + 1) * 128], ones[:, 0:64], PTb[:, idx * 128:(idx + 1) * 128],
                                     start=(b == nblkk[0]), stop=(b == 1), r=["PTb", "ones"], w=["b2"])
                    if stage < 5.3:
                        return
                    if has_sink:
                        P.tt("dve", den[:].rearrange("p (j q) -> p j q", q=128), ps[2][:, 256:512].rearrange("p (j q) -> p j q", q=128),
                             pc[:, PC["esk"]:PC["esk"] + 2].unsqueeze(2).to_broadcast([128, 2, 128]), ALU.add, r=["b2", "pc"], w=["den"])
                        P.act(den[:], den[:], AF.Ln, r=["den"], w=["den"])
                    else:
                        P.act(den[:], ps[2][:, 256:512], AF.Ln, r=["b2"], w=["den"])
                    P.act(den[:], den[:], AF.Exp, r=["den"], w=["den"], scale=-1.0)
                    P.tt("dve", mixT[:, mixc:mixc + 2, tcols], ps[2][:, 0:256].rearrange("p (j q) -> p j q", q=128),
                         den[:].rearrange("p (j q) -> p j q", q=128), ALU.mult, r=["b2", "den"], w=["mixT"])

                def attn_gen():
                    for ti in range(2):
                        t = blk * 2 + ti
                        tcols = slice(ti * 128, (ti + 1) * 128)
                        b0 = 1 if t == 0 else 0
                        for j in range(2):
                            for hh in range(2):
                                for b in range(0, 2):
                                    idx = (hh * 2 + j) * 2 + b
                                    slot = (t - 1 + b) % 3
                                    bank = idx // 4
                                    P.mm(ps[bank][:, (idx % 4) * 128:(idx % 4 + 1) * 128], KTr[hh * 64:(hh + 1) * 64, slot, :],
                                         QT[hh * 64:(hh + 1) * 64, j, tcols], r=[("KTr", slot), "QT"], w=[f"b{bank}"])
                        for bank in range(2):
                            P.stt(tmpS[:, bank * 512:(bank + 1) * 512], ps[bank][:, :], SCALE, biasT[:, bank * 512:(bank + 1) * 512], ALU.mult, ALU.add,
                                  r=[f"b{bank}", "biasT"], w=["tmpS"])
                        yield
                        P.act(PTb[:], tmpS[:], AF.Exp, r=["tmpS"], w=["PTb"])
                        yield
                        attn_tail(lambda j, hh, b: Vs[:, (t - 1 + b) % 3, hh * 64:(hh + 1) * 64], (b0,), True, 4, tcols, [("Vs", (t - 1) % 3), ("Vs", t % 3)])
                        yield
                        for j in range(2):
                            for hh in range(2):
                                for b in range(2):
                                    idx = (hh * 2 + j) * 2 + b
                                    bank = idx // 4
                                    P.mm(ps[bank][:, (idx % 4) * 128:(idx % 4 + 1) * 128], mkT[hh * 64:(hh + 1) * 64, j, b * 128:(b + 1) * 128],
                                         QT[hh * 64:(hh + 1) * 64, 2 + j, tcols], r=["mkT", "QT"], w=[f"b{bank}"])
                        for bank in range(2):
                            P.act(PTb[:, bank * 512:(bank + 1) * 512], ps[bank][:, :], AF.Exp, r=[f"b{bank}"], w=["PTb"], scale=SCALE)
                        yield
                        attn_tail(lambda j, hh, b: mv[:, b, (2 * j + hh) * 64:(2 * j + hh + 1) * 64], (0,), False, 6, tcols, ["mv"])
                        yield


                T = lambda cc, k: tmp[cc * 10 + k]
                tk = lambda cc, k: f"t{cc * 10 + k}"
                r_ = lambda cc: xs[:, cc, :]
                k_ = lambda cc: xs[:, 4 + cc, :]
                v_ = lambda cc: xs[:, 8 + cc, :]
                rk = lambda cc: [("xs", cc), ("xs", 4 + cc), ("xs", 8 + cc)]
                hb = lambda cc: slice((cc % 2) * TB, (cc % 2 + 1) * TB)
                pW = lambda cc: ps[4][:, hb(cc)]
                pA = lambda cc: ps[5][:, hb(cc)]
                v4 = lambda a: a.rearrange("p (c t) -> p c t", t=C)
                steps = [
                    lambda cc: P.mm(pW(cc), lwa[0:64, cc * 128:(cc + 1) * 128], lin[0:64, :], r=["lwa", "lin"], w=["b4"]),
                    lambda cc: P.mm(pA(cc), lwa[64:128, cc * 128:(cc + 1) * 128], lin[64:128, :], r=["lwa", "lin"], w=["b5"]),
                    lambda cc: P.mm(ps[6][:, hb(cc)], lwg[:, cc * 128:(cc + 1) * 128], sg[:], r=["lwg", "sg"], w=["b6"]),
                    lambda cc: P.act(T(cc, 0)[:], pW(cc), AF.Sigmoid, r=["b4", "pc"], w=[tk(cc, 0)], bias=pcol("w0", cc)),
                    lambda cc: P.act(T(cc, 1)[:], pA(cc), AF.Sigmoid, r=["b5", "pc"], w=[tk(cc, 1)], bias=pcol("a0", cc)),
                    lambda cc: P.copy("act", gT[:, cc, :], ps[6][:, hb(cc)], r=["b6"], w=["gT"]),
                    lambda cc: P.ts("dve", T(cc, 2)[:], k_(cc), pcol("kk", cc), ALU.mult, r=rk(cc) + ["pc"], w=[tk(cc, 2)]),
                    lambda cc: P.act(T(cc, 3)[:], T(cc, 2)[:], AF.Square, r=[tk(cc, 2)], w=[tk(cc, 3)]),
                    lambda cc: P.mm(ps[3][:, hb(cc)], bavg[:], T(cc, 3)[:], r=["bavg", tk(cc, 3)], w=["b3"]),
                    lambda cc: P.act(T(cc, 3)[:], ps[3][:, hb(cc)], AF.Ln, r=["b3"], w=[tk(cc, 3)], scale=64.0, bias=1e-18),
                    lambda cc: P.act(T(cc, 3)[:], T(cc, 3)[:], AF.Exp, r=[tk(cc, 3)], w=[tk(cc, 3)], scale=-0.5),
                    lambda cc: P.tt("dve", T(cc, 2)[:], T(cc, 2)[:], T(cc, 3)[:], ALU.mult, r=[tk(cc, 2), tk(cc, 3)], w=[tk(cc, 2)]),
                    lambda cc: P.ts("dve", T(cc, 3)[:], T(cc, 1)[:], pcol("ka", cc), ALU.mult, pcol("omka", cc), ALU.add, r=[tk(cc, 1), "pc"], w=[tk(cc, 3)]),
                    lambda cc: P.tt("pool", T(cc, 4)[:], k_(cc), T(cc, 3)[:], ALU.mult, r=rk(cc) + [tk(cc, 3)], w=[tk(cc, 4)]),
                    lambda cc: P.tt("pool", T(cc, 5)[:], T(cc, 2)[:], T(cc, 1)[:], ALU.mult, r=[tk(cc, 2), tk(cc, 1)], w=[tk(cc, 5)]),
                    lambda cc: P.ts("dve", T(cc, 6)[:], T(cc, 0)[:], -EXPM05, ALU.mult, r=[tk(cc, 0)], w=[tk(cc, 6)]),
                    lambda cc: P.op("dve", lambda e, o=T(cc, 7), l=T(cc, 6): e.tensor_tensor_scan(out=o[:], data0=rmask[:], data1=l[:], initial=0.0, op0=ALU.mult, op1=ALU.add),
                                    r=["rmask", tk(cc, 6)], w=[tk(cc, 7)]),
                    lambda cc: P.act(epos[:, cc, :], T(cc, 7)[:], AF.Exp, r=[tk(cc, 7)], w=["epos"]),
                    lambda cc: P.act(T(cc, 8)[:], T(cc, 7)[:], AF.Exp, r=[tk(cc, 7)], w=[tk(cc, 8)], scale=-1.0),
                    lambda cc: P.tt("dve", T(cc, 6)[:], T(cc, 7)[:], T(cc, 6)[:], ALU.subtract, r=[tk(cc, 7), tk(cc, 6)], w=[tk(cc, 6)]),
                    lambda cc: P.act(T(cc, 9)[:], T(cc, 6)[:], AF.Exp, r=[tk(cc, 6)], w=[tk(cc, 9)]),
                    lambda cc: P.stt(AR[:, cc, :, 0, :], v4(T(cc, 2)[:]), -1.0, v4(T(cc, 9)[:]), ALU.mult, ALU.mult, r=[tk(cc, 2), tk(cc, 9)], w=["AR"]),
                    lambda cc: P.tt("pool", AR[:, cc, :, 1, :], v4(r_(cc)), v4(epos[:, cc, :]), ALU.mult, r=rk(cc) + ["epos"], w=["AR"]),
                    lambda cc: P.tt("dve", BK[:, cc, :, 0, :], v4(T(cc, 5)[:]), v4(T(cc, 8)[:]), ALU.mult, r=[tk(cc, 5), tk(cc, 8)], w=["BK"]),
                    lambda cc: P.tt("pool", BK[:, cc, :, 1, :], v4(T(cc, 4)[:]), v4(T(cc, 8)[:]), ALU.mult, r=[tk(cc, 4), tk(cc, 8)], w=["BK"]),
                    lambda cc: P.copy("act", vb[:, cc, :], v_(cc), r=rk(cc), w=["vb"]),
                    lambda cc: P.stt(T(cc, 3)[:], r_(cc), pcol("rk", cc), T(cc, 4)[:], ALU.mult, ALU.mult, r=rk(cc) + [tk(cc, 4), "pc"], w=[tk(cc, 3)]),
                    lambda cc: P.mm(ps[6][:, hb(cc)], bavg[:], T(cc, 3)[:], r=["bavg", tk(cc, 3)], w=["b6"]),
                    lambda cc: P.stt(bonus[:, cc, :], ps[6][:, hb(cc)], 64.0, v_(cc), ALU.mult, ALU.mult, r=["b6"] + rk(cc), w=["bonus"]),
                ]
                def elem_gen():
                    P.act(lin[0:64, :], xs[0:64, 12, :], AF.Tanh, r=[("xs", 12)], w=["lin"])
                    P.copy("act", lin[64:128, :], xs[64:128, 12, :], r=[("xs", 12)], w=["lin"])
                    P.act(sg[:], xs[:, 13, :], AF.Sigmoid, r=[("xs", 13)], w=["sg"])
                    yield
                    for grp in ((0, 1), (2, 3)):
                        for st_ in steps:
                            for cc in grp:
                                st_(cc)
                            yield

                ga, ge = attn_gen(), elem_gen()
                live = [ga, ge]
                while live:
                    for g_, n_ in ((ge, 5), (ga, 1)):
                        if g_ in live:
                            for _ in range(n_):
                                try:
                                    next(g_)
                                except StopIteration:
                                    live.remove(g_)
                                    break

                P.enabled = stage >= 3
                if blk + 1 < nblk:
                    A0(blk + 1)
                P.enabled = stage >= 7
                bkn = lambda n: f"b{n}"
                hs = [(cc, hh) for hh in range(2) for cc in range(4)]
                sl = lambda hh: slice(hh * 64, (hh + 1) * 64)
                v3 = lambda ap, t: ap.rearrange("p (a t) -> p a t", t=t)
                at_ = lambda c, cc, hh: AR[sl(hh), cc, c, 0, :]
                rt_ = lambda c, cc, hh: AR[sl(hh), cc, c, 1, :]
                ar_ = lambda c, cc, hh: AR[sl(hh), cc, c, :, :].rearrange("p a t -> p (a t)")
                bt_ = lambda c, cc, hh: BK[sl(hh), cc, c, 0, :]
                kt_ = lambda c, cc, hh: BK[sl(hh), cc, c, 1, :]
                vt_ = lambda c, cc, hh: vb[sl(hh), cc, c * C:(c + 1) * C]
                idq = lambda hh: identb[sl(hh), sl(hh)]
                f8 = lambda t, hh: t[sl(hh), :, :, :].rearrange("p a b t -> p (a b) t")
                for pair in range(NCH // 2):
                    cs = [(0, 2 * pair), (1, 2 * pair + 1)]
                    for G, lt, gk in ((GB, bt_, "GB"), (GK, kt_, "GK")):
                        for ci, c in cs:
                            for cc, hh in hs:
                                P.mm(ps[2 * ci + hh][sl(hh), cc * 128:(cc + 1) * 128], lt(c, cc, hh), ar_(c, cc, hh), r=["BK", "AR"], w=[bkn(2 * ci + hh)])
                        for ci, c in cs:
                            for hh in range(2):
                                P.tt("dve", G[sl(hh), ci, :, :], v3(ps[2 * ci + hh][sl(hh), :], 128), maskg[sl(hh), :].unsqueeze(1).to_broadcast([64, 4, 128]),
                                     ALU.mult, r=[bkn(2 * ci + hh), "maskg"], w=[gk])
                    for ci, c in cs:
                        for cc, hh in hs:
                            P.mm(ps[4 + hh][sl(hh), (ci * 4 + cc) * 64:(ci * 4 + cc + 1) * 64], at_(c, cc, hh), bt_(c, cc, hh), r=["BK", "AR"], w=[bkn(4 + hh)])
                    for hh in range(2):
                        P.tt("dve", f8(Ab[0], hh), v3(ps[4 + hh][sl(hh), :], 64), maskn[sl(hh), :].unsqueeze(1).to_broadcast([64, 8, 64]), ALU.mult,
                             r=[bkn(4 + hh), "maskn"], w=["Ab0"])
                    for ci, c in cs:
                        P.tt("pool", PTt[:, ci, :, :], GB[:, ci, :, 0:64], identd[:].unsqueeze(1).to_broadcast([128, 4, 64]), ALU.add, r=["GB", "identd"], w=["PTt"])
                    for bb, lt, key in ((0, bt_, "BK"), (2, kt_, "BK"), (4, vt_, "vb")):
                        for ci, c in cs:
                            for cc, hh in hs:
                                P.mm(ps[bb + hh][sl(hh), (ci * 4 + cc) * 64:(ci * 4 + cc + 1) * 64], lt(c, cc, hh), idq(hh), r=[key, "identb"], w=[bkn(bb + hh)])
                    for hh in range(2):
                        P.copy("act", f8(Btk, hh), v3(ps[0 + hh][sl(hh), :], 64), r=[bkn(0 + hh)], w=["Btk"])
                        P.copy("act", f8(Ktk, hh), v3(ps[2 + hh][sl(hh), :], 64), r=[bkn(2 + hh)], w=["Ktk"])
                        P.copy("act", f8(Vtk, hh), v3(ps[4 + hh][sl(hh), :], 64), r=[bkn(4 + hh)], w=["Vtk"])
                    A, B, ka, kb = Ab[0], GB[:, :, :, 0:64], "Ab0", "GB"
                    for lev in range(1, 6):
                        An, Bn = Ab[lev % 2], Bb[lev % 2]
                        kan, kbn = f"Ab{lev % 2}", f"Bb{lev % 2}"
                        for ci, c in cs:
                            for cc, hh in hs:
                                P.mm(ps[0 + hh][sl(hh), (ci * 4 + cc) * 64:(ci * 4 + cc + 1) * 64], B[sl(hh), ci, cc, :], A[sl(hh), ci, cc, :], r=[ka, kb], w=[bkn(0 + hh)])
                        if lev < 5:
                            for ci, c in cs:
                                for cc, hh in hs:
                                    P.mm(ps[2 + hh][sl(hh), (ci * 4 + cc) * 64:(ci * 4 + cc + 1) * 64], A[sl(hh), ci, cc, :], B[sl(hh), ci, cc, :], r=[ka, kb], w=[bkn(2 + hh)])
                        for hh in range(2):
                            P.copy("act", f8(An, hh), v3(ps[0 + hh][sl(hh), :], 64), r=[bkn(0 + hh)], w=[kan])
                        if lev < 5:
                            for hh in range(2):
                                P.copy("dve", f8(Bn, hh), v3(ps[2 + hh][sl(hh), :], 64), r=[bkn(2 + hh)], w=[kbn])
                        for ci, c in cs:
                            for cc, hh in hs:
                                P.mm(ps[4 + hh][sl(hh), (ci * 4 + cc) * 64:(ci * 4 + cc + 1) * 64], An[sl(hh), ci, cc, :], PTt[sl(hh), ci, cc, :], r=[kan, "PTt"], w=[bkn(4 + hh)])
                        for hh in range(2):
                            P.tt("dve", f8(PTt, hh), v3(ps[4 + hh][sl(hh), :], 64), f8(PTt, hh), ALU.add, r=[bkn(4 + hh), "PTt"], w=["PTt"])
                        A, B, ka, kb = An, Bn, kan, kbn
                    for ci, c in cs:
                        gc = blk * NCH + c
                        S0 = ST[gc % 2]
                        S1 = ST[(gc + 1) % 2]
                        k0, k1 = f"ST{gc % 2}", f"ST{(gc + 1) % 2}"
                        for hh in range(2):
                            P.copy("act", STb[sl(hh), :, :], S0[sl(hh), :, :], r=[k0], w=["STb"])
                        for cc, hh in hs:
                            o = ps[0 + hh][sl(hh), cc * 64:(cc + 1) * 64]
                            P.mm(o, GK[sl(hh), ci, cc, 0:64], Vtk[sl(hh), ci, cc, :], start=True, stop=False, r=["GK", "Vtk"], w=[bkn(0 + hh)])
                            P.mm(o, at_(c, cc, hh), STb[sl(hh), cc, :], start=False, stop=True, r=["AR", "STb"], w=[bkn(0 + hh)])
                        for hh in range(2):
                            P.copy("act", Zs[sl(hh), :, :], v3(ps[0 + hh][sl(hh), 0:256], 64), r=[bkn(0 + hh)], w=["Zs"])
                        for cc, hh in hs:
                            P.mm(ps[0 + hh][sl(hh), 256 + cc * 64:256 + (cc + 1) * 64], PTt[sl(hh), ci, cc, :], Zs[sl(hh), cc, :], r=["PTt", "Zs"], w=[bkn(0 + hh)])
                        for hh in range(2):
                            P.copy("dve", Utk[sl(hh), :, :], v3(ps[0 + hh][sl(hh), 256:512], 64), r=[bkn(0 + hh)], w=["Utk"])
                        for cc, hh in hs:
                            o = ps[2 + hh][sl(hh), cc * 64:(cc + 1) * 64]
                            P.mm(o, STb[sl(hh), cc, :], rt_(c, cc, hh), start=True, stop=False, r=["STb", "AR"], w=[bkn(2 + hh)])
                            P.mm(o, Utk[sl(hh), cc, :], GB[sl(hh), ci, cc, 64:128], start=False, stop=False, r=["Utk", "GB"], w=[bkn(2 + hh)])
                            P.mm(o, Vtk[sl(hh), ci, cc, :], GK[sl(hh), ci, cc, 64:128], start=False, stop=True, r=["Vtk", "GK"], w=[bkn(2 + hh)])
                        for cc, hh in hs:
                            o = ps[4 + hh][sl(hh), cc * 64:(cc + 1) * 64]
                            P.mm(o, Btk[sl(hh), ci, cc, :], Utk[sl(hh), cc, :], start=True, stop=False, r=["Btk", "Utk"], w=[bkn(4 + hh)])
                            P.mm(o, Ktk[sl(hh), ci, cc, :], Vtk[sl(hh), ci, cc, :], start=False, stop=True, r=["Ktk", "Vtk"], w=[bkn(4 + hh)])
                        for hh in range(2):
                            P.tt("dve", Zf[sl(hh), :, :], v3(ps[4 + hh][sl(hh), 0:256], 64), S0[sl(hh), :, :], ALU.add, r=[bkn(4 + hh), k0], w=["Zf"])
                            P.tt("pool", S1[sl(hh), :, :], Zf[sl(hh), :, :],
                                 epos[sl(hh), :, c * C + C - 1:c * C + C].to_broadcast([64, 4, 64]), ALU.mult, r=["Zf", "epos"], w=[k1])
                        for hh in range(2):
                            P.copy("act", yT[sl(hh), :, c * C:(c + 1) * C], v3(ps[2 + hh][sl(hh), 0:256], 64), r=[bkn(2 + hh)], w=["yT"])

                P.enabled = stage >= 8
                gb = lambda cc: (ps[cc], f"b{cc}")
                gsteps = [
                    lambda cc: P.mm(gb(cc)[0][:, 0:TB], bavg[:], yT[:, cc, :], r=["bavg", "yT"], w=[gb(cc)[1]]),
                    lambda cc: P.tt("dve", T(cc, 0)[:], yT[:, cc, :], gb(cc)[0][:, 0:TB], ALU.subtract, r=["yT", gb(cc)[1]], w=[tk(cc, 0)]),
                    lambda cc: P.act(T(cc, 1)[:], T(cc, 0)[:], AF.Square, r=[tk(cc, 0)], w=[tk(cc, 1)]),
                    lambda cc: P.mm(gb(cc)[0][:, TB:2 * TB], bavg[:], T(cc, 1)[:], r=["bavg", tk(cc, 1)], w=[gb(cc)[1]]),
                    lambda cc: P.act(T(cc, 1)[:], gb(cc)[0][:, TB:2 * TB], AF.Ln, r=[gb(cc)[1]], w=[tk(cc, 1)], bias=64e-5),
                    lambda cc: P.act(T(cc, 1)[:], T(cc, 1)[:], AF.Exp, r=[tk(cc, 1)], w=[tk(cc, 1)], scale=-0.5),
                    lambda cc: P.tt("dve", T(cc, 0)[:], T(cc, 0)[:], T(cc, 1)[:], ALU.mult, r=[tk(cc, 0), tk(cc, 1)], w=[tk(cc, 0)]),
                    lambda cc: P.ts("dve", T(cc, 0)[:], T(cc, 0)[:], pcol("lnw", cc), ALU.mult, pcol("lnb", cc), ALU.add, r=[tk(cc, 0), "pc"], w=[tk(cc, 0)]),
                    lambda cc: P.tt("pool", T(cc, 0)[:], T(cc, 0)[:], bonus[:, cc, :], ALU.add, r=[tk(cc, 0), "bonus"], w=[tk(cc, 0)]),
                    lambda cc: P.tt("pool", mixT[:, cc, :], T(cc, 0)[:], gT[:, cc, :], ALU.mult, r=[tk(cc, 0), "gT"], w=["mixT"]),
                ]
                for st_ in gsteps:
                    for cc in range(4):
                        st_(cc)
                P.enabled = stage >= 3
                for kc in range(8):
                    P.dma("sp", mixD[kc, :, blk * TB:(blk + 1) * TB], mixT[:, kc, :], r=["mixT"], w=["mixD"])

            P.enabled = stage >= 9
            Sf = ST[(nblk * NCH) % 2]
            kf = f"ST{(nblk * NCH) % 2}"
            for hh in range(2):
                for cc in range(4):
                    P.mm(ps[hh][hh * 64:(hh + 1) * 64, cc * 64:(cc + 1) * 64], Sf[hh * 64:(hh + 1) * 64, cc, :],
                         ident[hh * 64:(hh + 1) * 64, hh * 64:(hh + 1) * 64], r=[kf, "ident"], w=[f"b{hh}"])
                P.copy("act", Zf[hh * 64:(hh + 1) * 64, :, :], ps[hh][hh * 64:(hh + 1) * 64, 0:256].rearrange("p (a t) -> p a t", t=64), r=[f"b{hh}"], w=["Zf"])
                P.dma("sp", srp.rearrange("(c two) v k -> two v c k", two=2)[hh], Zf[hh * 64:(hh + 1) * 64, :, :], r=["Zf"])
            P.dma("sp", shp.rearrange("(c p) -> p c", p=128), pT[:, :, 0], r=[("pT", i) for i in range(14)] + ["pT"], allow_slow_non_contiguous=True)
            P.tr(ps[1][:, 0:128], KTf[:, 128:256], ident[:], r=["KTf", "ident"], w=["b1"])
            P.copy("act", tmp[0][:, 0:128], ps[1][:, 0:128], r=["b1"], w=["t0"])
            P.dma("sp", kbp, tmp[0][:, 0:128], r=["t0"])
            P.dma("sp", vbp, Vf[:], r=["Vf"])
            P.enabled = True
            P.barrier()
            P.emit()

        if do_samp:
            with ExitStack() as phS:
                def sS(name, shape, dt=F32):
                    return sb(name, shape, dt, phS)
                win = sS("winS", [128, 8, 2560], BF16)
                lwa = sS("lwaS", [128, 512], BF16)
                lwg = sS("lwgS", [128, 512], BF16)
                bavg = sS("bavgS", [128, 128])
                gbc = sS("gbcS", [128, D])
                x16 = sS("x16", [128, D])
                xn16 = sS("xn16", [128, D], BF16)
                ss16 = sS("ss16", [128, 4])
                hTs = sS("hTs", [128, 8, 16], BF16)
                pTs = sS("pTs", [128, 20, 16])
                shl = sS("shl", [16, 1792])
                prevT = sS("prevT", [128, 14, 16])
                xss = sS("xss", [128, 14, 16])
                tm = sS("tm", [16, 1792])
                TMv = sS("TMv", [16, 8, 512])
                lin16 = sS("lin16", [128, 16], BF16)
                sg16 = sS("sg16", [128, 16], BF16)
                q = [sS(f"q{i}", [128, 16]) for i in range(10)]
                Fv = sS("Fv", [128, 4, 8, 16])
                VS = sS("VS", [128, 8, 64])
                big = sS("big", [128, 8256])
                Sst = big[:, 0:4096].rearrange("p (v k) -> p v k", k=64)
                tmpA = big[:, 4096:8192].rearrange("p (v k) -> p v k", k=64)
                sm = [sS(f"sm{i}", [128, 64]) for i in range(5)]
                st4 = sS("st4", [128, 4])
                lnwbh = sS("lnwbh", [128, 64])
                lnbbh = sS("lnbbh", [128, 64])
                QTs = sS("QTs", [128, 4, 16])
                KTs = sS("KTs", [128, 16])
                Kc = sS("Kc", [64, 129, 64])
                Vc = sS("Vc", [64, 129, 64])
                prod = big[0:64, :].rearrange("p (j d) -> p j d", d=64)
                qd = sS("qd", [64, 64])
                sc = sS("sc", [64, 256])
                bdec = sS("bdec", [64, 129])
                od = sS("od", [64, 64])
                o2 = sS("o2", [64, 64])
                skd = sS("skd", [64, 4])

                P.dma("sp", gbc[:], g1bc_d, w=["gbc"])
                P.dma("sp", bavg[:], bavg_d, w=["bavg"])
                for kc in range(8):
                    P.dma("pool", win[:, kc, :], w_in[kc * 128:(kc + 1) * 128, :], w=[("win", kc)], max_dma_last_dim=4096)
                P.dma("pool", lwa[:], lwa_d, w=["lwa"])
                P.dma("pool", lwg[:], lwg_d, w=["lwg"])
                P.dma("sp", lnwbh[:], lnwbh_d, w=["lnwbh"])
                P.dma("sp", lnbbh[:], lnbbh_d, w=["lnbbh"])
                P.dma("sp", skd[:, 0:1], skd_d, w=["skd"])
                P.dma("sp", shl[:], shift_s, w=["shl"])
                P.dma("sp", Sst[:], srs_in.rearrange("b h v k -> (b h) v k"), w=["Sst"])
                for g in range(2):
                    for kvh in range(2):
                        rows = slice(g * 32 + kvh * 16, g * 32 + kvh * 16 + 16)
                        P.dma("pool", Kc[rows, 0:128, :], ck_in[:, :, kvh, :], w=["Kc"])
                        P.dma("pool", Vc[rows, 0:128, :], cv_in[:, :, kvh, :], w=["Vc"])

                P.dma("sp", q[0][0:32, 0:4], relb, w=["q0"])
                P.dma("sp", sc[0:32, 0:129], onehr_d, w=["sc"])
                P.mm(ps[0][0:4, 0:129], q[0][0:32, 0:4], sc[0:32, 0:129], r=["q0", "sc"], w=["b0"])
                P.copy("act", bdec[0:4, 0:129], ps[0][0:4, 0:129], r=["b0"], w=["bdec"])
                P.dma("sp", fdscr, bdec[0:4, 0:129], r=["bdec"], w=["fdscr"])

                P.dma("sp", x16[0:16, :], xsamp, w=["x16"])
                norm_rows(x16, xn16, gbc, "x16", "xn16", ss16, nrows=16)
                for kc in range(8):
                    P.tr(psT[:, kc * 128:kc * 128 + 16], xn16[0:16, kc * 128:(kc + 1) * 128], identb[0:16, 0:16], r=["xn16", "identb"], w=["bT"])
                P.copy("act", hTs[:, :, :], psT[:].rearrange("p (k t) -> p k t", t=128)[:, :, 0:16], r=["bT"], w=["hTs"])
                for oc in range(20):
                    if oc == 17:
                        continue
                    bank = 5 + oc % 2
                    for kc in range(8):
                        P.mm(ps[bank][:, 0:16], win[:, kc, oc * 128:(oc + 1) * 128], hTs[:, kc, :], start=(kc == 0), stop=(kc == 7),
                             r=[("win", kc), "hTs"], w=[f"b{bank}"])
                    P.copy("act", pTs[:, oc, :], ps[bank][:, 0:16], r=[f"b{bank}"], w=[("pTs", oc)])
                for kc in range(8):
                    P.mm(ps[4][0:16, 0:128], hTs[:, kc, :], win[:, kc, 2176:2304], start=(kc == 0), stop=(kc == 7), r=[("win", kc), "hTs"], w=["b4"])
                P.copy("act", x16[0:16, 0:128], ps[4][0:16, 0:128], r=["b4"], w=["x16v"])
                P.dma("sp", scrVn, x16[0:16, 0:128], r=["x16v"], w=["scrVn"])
                for g4 in range(4):
                    ocs = list(range(g4 * 4, min(14, g4 * 4 + 4)))
                    for i, oc in enumerate(ocs):
                        P.tr(ps[g4 % 2][0:16, i * 128:(i + 1) * 128], pTs[:, oc, :], ident[:], r=[("pTs", oc), "ident"], w=[f"b{g4 % 2}"])
                    n = len(ocs) * 128
                    P.copy("act", tm[0:16, g4 * 512:g4 * 512 + n], ps[g4 % 2][0:16, 0:n], r=[f"b{g4 % 2}"], w=["tm"])
                P.dma("sp", shs, tm[0:16, 0:1792], r=["tm"])
                for oc in range(14):
                    P.tr(ps[2][:, oc * 16:(oc + 1) * 16], shl[0:16, oc * 128:(oc + 1) * 128], ident[0:16, 0:16], r=["shl", "ident"], w=["b2"])
                P.copy("act", prevT[:].rearrange("p a t -> p (a t)"), ps[2][:, 0:224], r=["b2"], w=["prevT"])
                for oc in range(14):
                    P.ts("pool", q[0][:], prevT[:, oc, :], pcol("mu", oc), ALU.mult, r=["prevT", "pc"], w=["q0"])
                    P.stt(xss[:, oc, :], pTs[:, oc, :], pcol("omu", oc), q[0][:], ALU.mult, ALU.add, r=[("pTs", oc), "q0", "pc"], w=["xss"])
                P.act(lin16[0:64, :], xss[0:64, 12, :], AF.Tanh, r=["xss"], w=["lin16"])
                P.copy("act", lin16[64:128, :], xss[64:128, 12, :], r=["xss"], w=["lin16"])
                P.act(sg16[:], xss[:, 13, :], AF.Sigmoid, r=["xss"], w=["sg16"])
                for cc in range(4):
                    r_, k_, v_ = xss[:, cc, :], xss[:, 4 + cc, :], xss[:, 8 + cc, :]
                    P.mm(ps[0][:, 0:16], lwa[0:64, cc * 128:(cc + 1) * 128], lin16[0:64, :], r=["lwa", "lin16"], w=["b0"])
                    P.mm(ps[2][:, 0:16], lwa[64:128, cc * 128:(cc + 1) * 128], lin16[64:128, :], r=["lwa", "lin16"], w=["b2"])
                    P.mm(ps[1][:, 0:16], lwg[:, cc * 128:(cc + 1) * 128], sg16[:], r=["lwg", "sg16"], w=["b1"])
                    P.act(q[0][:], ps[0][:, 0:16], AF.Sigmoid, r=["b0", "pc"], w=["q0"], bias=pcol("w0", cc))
                    P.act(q[1][:], ps[2][:, 0:16], AF.Sigmoid, r=["b2", "pc"], w=["q1"], bias=pcol("a0", cc))
                    P.copy("act", Fv[:, cc, 6, :], ps[1][:, 0:16], r=["b1"], w=["Fv"])
                    P.ts("dve", q[2][:], k_, pcol("kk", cc), ALU.mult, r=["xss", "pc"], w=["q2"])
                    P.act(q[3][:], q[2][:], AF.Square, r=["q2"], w=["q3"])
                    P.mm(ps[1][:, 16:32], bavg[:], q[3][:], r=["bavg", "q3"], w=["b1"])
                    P.act(q[3][:], ps[1][:, 16:32], AF.Sqrt, r=["b1"], w=["q3"], scale=64.0)
                    P.ts("dve", q[3][:], q[3][:], 1e-12, ALU.max, r=["q3"], w=["q3"])
                    P.recip(q[3][:], q[3][:], r=["q3"], w=["q3"])
                    P.tt("dve", q[2][:], q[2][:], q[3][:], ALU.mult, r=["q2", "q3"], w=["q2"])
                    P.ts("dve", q[3][:], q[1][:], pcol("ka", cc), ALU.mult, pcol("omka", cc), ALU.add, r=["q1", "pc"], w=["q3"])
                    P.tt("dve", Fv[:, cc, 2, :], k_, q[3][:], ALU.mult, r=["xss", "q3"], w=["Fv"])
                    P.tt("dve", Fv[:, cc, 5, :], q[2][:], q[1][:], ALU.mult, r=["q2", "q1"], w=["Fv"])
                    P.ts("dve", Fv[:, cc, 4, :], q[2][:], -1.0, ALU.mult, r=["q2"], w=["Fv"])
                    P.act(Fv[:, cc, 1, :], q[0][:], AF.Exp, r=["q0"], w=["Fv"], scale=-EXPM05)
                    P.copy("act", Fv[:, cc, 0, :], r_, r=["xss"], w=["Fv"])
                    P.copy("act", Fv[:, cc, 3, :], v_, r=["xss"], w=["Fv"])
                    P.stt(q[4][:], r_, pcol("rk", cc), Fv[:, cc, 2, :], ALU.mult, ALU.mult, r=["xss", "Fv", "pc"], w=["q4"])
                    P.mm(ps[1][:, 32:48], bavg[:], q[4][:], r=["bavg", "q4"], w=["b1"])
                    P.stt(Fv[:, cc, 7, :], ps[1][:, 32:48], 64.0, v_, ALU.mult, ALU.mult, r=["b1", "xss"], w=["Fv"])
                for vec in range(8):
                    bank = vec % 2
                    for cc in range(4):
                        P.tr(ps[bank][0:16, cc * 128:(cc + 1) * 128], Fv[:, cc, vec, :], ident[:], r=["Fv", "ident"], w=[f"b{bank}"])
                    P.copy("act", TMv[:, vec, :], ps[bank][0:16, :], r=[f"b{bank}"], w=["TMv"])
                for vec in range(8):
                    P.dma("sp", scrV[:, :, vec, :], TMv[:, vec, :].rearrange("b (h c) -> b h c", c=64), r=["TMv"], w=["scrV"])
                P.dma("sp", VS[:], scrV.rearrange("b h v c -> (b h) v c"), r=["scrV"], w=["VS"])
                bc_v = lambda i: VS[:, i, :].unsqueeze(1).to_broadcast([128, 64, 64])
                bc_k = lambda ap: ap.unsqueeze(2).to_broadcast([128, 64, 64])
                P.tt("dve", tmpA[:], Sst[:], bc_v(4), ALU.mult, r=["Sst", "VS"], w=["tmpA"])
                P.op("dve", lambda e: e.tensor_reduce(out=sm[0][:], in_=tmpA[:], axis=AX.X, op=ALU.add), r=["tmpA"], w=["sm0"])
                P.tt("dve", Sst[:], Sst[:], bc_v(1), ALU.mult, r=["Sst", "VS", "tmpA"], w=["Sst"])
                P.tt("dve", tmpA[:], bc_k(sm[0][:]), bc_v(5), ALU.mult, r=["sm0", "VS"], w=["tmpA"])
                P.tt("pool", Sst[:], Sst[:], tmpA[:], ALU.add, r=["Sst", "tmpA"], w=["Sst"])
                P.tt("dve", tmpA[:], bc_k(VS[:, 3, :]), bc_v(2), ALU.mult, r=["VS", "Sst"], w=["tmpA"])
                P.tt("pool", Sst[:], Sst[:], tmpA[:], ALU.add, r=["Sst", "tmpA"], w=["Sst"])
                P.dma("sp", srs.rearrange("b h v k -> (b h) v k"), Sst[:], r=["Sst"])
                P.tt("dve", tmpA[:], Sst[:], bc_v(0), ALU.mult, r=["Sst", "VS"], w=["tmpA"])
                P.op("dve", lambda e: e.tensor_reduce(out=sm[1][:], in_=tmpA[:], axis=AX.X, op=ALU.add), r=["tmpA"], w=["sm1"])
                P.op("dve", lambda e: e.tensor_reduce(out=st4[:, 0:1], in_=sm[1][:], axis=AX.X, op=ALU.add), r=["sm1"], w=["st4"])
                P.ts("dve", st4[:, 0:1], st4[:, 0:1], 1.0 / 64, ALU.mult, r=["st4"], w=["st4"])
                P.ts("dve", sm[2][:], sm[1][:], st4[:, 0:1], ALU.subtract, r=["sm1", "st4"], w=["sm2"])
                P.tt("dve", sm[3][:], sm[2][:], sm[2][:], ALU.mult, r=["sm2"], w=["sm3"])
                P.op("dve", lambda e: e.tensor_reduce(out=st4[:, 1:2], in_=sm[3][:], axis=AX.X, op=ALU.add), r=["sm3"], w=["st4"])
                P.act(st4[:, 2:3], st4[:, 1:2], AF.Sqrt, r=["st4"], w=["st4"], scale=1.0 / 64, bias=64e-5)
                P.recip(st4[:, 2:3], st4[:, 2:3], r=["st4"], w=["st4"])
                P.ts("dve", sm[2][:], sm[2][:], st4[:, 2:3], ALU.mult, r=["sm2", "st4"], w=["sm2"])
                P.tt("dve", sm[2][:], sm[2][:], lnwbh[:], ALU.mult, r=["sm2", "lnwbh"], w=["sm2"])
                P.tt("dve", sm[2][:], sm[2][:], lnbbh[:], ALU.add, r=["sm2", "lnbbh"], w=["sm2"])
                P.tt("dve", sm[2][:], sm[2][:], VS[:, 7, :], ALU.add, r=["sm2", "VS"], w=["sm2"])
                P.tt("dve", sm[2][:], sm[2][:], VS[:, 6, :], ALU.mult, r=["sm2", "VS"], w=["sm2"])
                P.dma("sp", scrMix.rearrange("b (h c) -> (b h) c", c=64), sm[2][:], r=["sm2"], w=["scrMix"])

                def to_mix(scr, key, ncols, mixc):
                    P.dma("sp", tm[0:16, 0:ncols], scr, r=[key], w=["tm"])
                    for i in range(ncols // 128):
                        P.tr(ps[0][:, i * 16:(i + 1) * 16], tm[0:16, i * 128:(i + 1) * 128], ident[0:16, 0:16], r=["tm", "ident"], w=["b0"])
                    n = ncols // 128
                    P.copy("act", mixTs[:, mixc:mixc + n, :], ps[0][:, 0:n * 16].rearrange("p (a t) -> p a t", t=16), r=["b0"], w=["mixTs"])

                to_mix(scrMix, "scrMix", 512, 0)

                for j in range(2):
                    headnorm(pTs[:, 14 + j, :], 16, pcol("qns"), [(QTs[:, j, :], "QTs")], (q[8], q[9]), ("pTs", 14 + j))
                    headnorm(pTs[:, 18 + j, :], 16, pcol("qnm"), [(QTs[:, 2 + j, :], "QTs")], (q[8], q[9]), ("pTs", 18 + j))
                headnorm(pTs[:, 16, :], 16, pcol("kns"), [(KTs[:, :], "KTs")], (q[8], q[9]), ("pTs", 16))
                for j in range(4):
                    P.tr(ps[1][0:16, j * 128:(j + 1) * 128], QTs[:, j, :], ident[:], r=["QTs", "ident"], w=["b1"])
                P.copy("act", tm[0:16, 0:512], ps[1][0:16, 0:512], r=["b1"], w=["tm"])
                P.dma("sp", scrQ, tm[0:16, 0:512], r=["tm"], w=["scrQ"])
                P.tr(ps[1][0:16, 0:128], KTs[:, :], ident[:], r=["KTs", "ident"], w=["b1"])
                P.copy("act", x16[0:16, 128:256], ps[1][0:16, 0:128], r=["b1"], w=["x16k"])
                P.dma("sp", scrK, x16[0:16, 128:256], r=["x16k"], w=["scrK"])
                ck2 = ck_in.rearrange("b j h d -> b j (h d)")
                cv2 = cv_in.rearrange("b j h d -> b j (h d)")
                P.dma("sp", kbs[:, 0:127, :], ck2[:, 1:128, :])
                P.dma("sp", vbs[:, 0:127, :], cv2[:, 1:128, :])
                P.dma("sp", kbs[:, 127, :], scrK, r=["scrK"])
                P.dma("sp", vbs[:, 127, :], scrVn, r=["scrVn"])

                for g in range(2):
                    for kvh in range(2):
                        rows = slice(g * 32 + kvh * 16, g * 32 + kvh * 16 + 16)
                        P.dma("sp", Kc[rows, 128, :], scrK[:, kvh * 64:(kvh + 1) * 64], r=["scrK"], w=["Kc"])
                        P.dma("sp", Vc[rows, 128, :], scrVn[:, kvh * 64:(kvh + 1) * 64], r=["scrVn"], w=["Vc"])
                        P.dma("sp", qd[rows, :], scrQ[:, g * 128 + kvh * 64:g * 128 + (kvh + 1) * 64], r=["scrQ"], w=["qd"])
                        P.dma("sp", bdec[rows, :], bass.AP(fdscr.tensor, (2 * kvh + g) * 129, [[0, 16], [1, 129]]), r=["fdscr"], w=["bdec"])
                P.act(skd[:, 1:2], skd[:, 0:1], AF.Exp, r=["skd"], w=["skd"])
                P.tt("dve", prod[:], Kc[:], qd[:].unsqueeze(1).to_broadcast([64, 129, 64]), ALU.mult, r=["Kc", "qd"], w=["prod", "Sst", "tmpA"])
                P.op("dve", lambda e: e.tensor_reduce(out=sc[:, 0:129], in_=prod[:], axis=AX.X, op=ALU.add), r=["prod"], w=["sc"])
                P.stt(sc[:, 0:129], sc[:, 0:129], SCALE, bdec[:], ALU.mult, ALU.add, r=["sc", "bdec"], w=["sc"])
                P.act(sc[:, 0:129], sc[:, 0:129], AF.Exp, r=["sc"], w=["sc", "skd"], accum_out=skd[:, 2:3])
                P.tt("dve", skd[:, 2:3], skd[:, 2:3], skd[:, 1:2], ALU.add, r=["skd"], w=["skd"])
                P.recip(skd[:, 2:3], skd[:, 2:3], r=["skd"], w=["skd"])
                P.tt("dve", prod[:], Vc[:], sc[:, 0:129].unsqueeze(2).to_broadcast([64, 129, 64]), ALU.mult, r=["Vc", "sc"], w=["prod"])
                P.op("dve", lambda e: e.tensor_reduce(out=od[:], in_=prod[:].rearrange("p j d -> p d j"), axis=AX.X, op=ALU.add), r=["prod"], w=["od"])
                P.ts("dve", od[:], od[:], skd[:, 2:3], ALU.mult, r=["od", "skd"], w=["od"])
                for g in range(2):
                    for kvh in range(2):
                        rows = slice(g * 32 + kvh * 16, g * 32 + kvh * 16 + 16)
                        P.dma("sp", scrO[:, g * 128 + kvh * 64:g * 128 + (kvh + 1) * 64], od[rows, :], r=["od"], w=["scrO"])
                to_mix(scrO, "scrO", 256, 4)

                for mh in range(4):
                    rows = slice(mh * 16, (mh + 1) * 16)
                    P.dma("sp", qd[rows, :], scrQ[:, 256 + mh * 64:256 + (mh + 1) * 64], r=["scrQ"], w=["qd"])
                for half in range(2):
                    for mh in range(4):
                        rows = slice(mh * 16, (mh + 1) * 16)
                        P.dma("pool", Kc[rows, 0:128, :], cmk_in[:, half * 128:(half + 1) * 128, mh, :], w=["Kc"])
                        P.dma("pool", Vc[rows, 0:128, :], cmv_in[:, half * 128:(half + 1) * 128, mh, :], w=["Vc"])
                    P.tt("dve", prod[:, 0:128, :], Kc[:, 0:128, :], qd[:].unsqueeze(1).to_broadcast([64, 128, 64]), ALU.mult, r=["Kc", "qd"], w=["prod"])
                    P.op("dve", lambda e, h=half: e.tensor_reduce(out=sc[:, h * 128:(h + 1) * 128], in_=prod[:, 0:128, :], axis=AX.X, op=ALU.add),
                         r=["prod"], w=["sc"])
                    P.act(sc[:, half * 128:(half + 1) * 128], sc[:, half * 128:(half + 1) * 128], AF.Exp, r=["sc"], w=["sc", "skd"],
                          scale=SCALE, accum_out=skd[:, 2 + half:3 + half])
                    P.tt("dve", prod[:, 0:128, :], Vc[:, 0:128, :], sc[:, half * 128:(half + 1) * 128].unsqueeze(2).to_broadcast([64, 128, 64]), ALU.mult,
                         r=["Vc", "sc"], w=["prod"])
                    P.op("dve", lambda e, o=(od if half == 0 else o2): e.tensor_reduce(out=o[:], in_=prod[:, 0:128, :].rearrange("p j d -> p d j"), axis=AX.X, op=ALU.add),
                         r=["prod"], w=["od" if half == 0 else "o2"])
                P.tt("dve", od[:], od[:], o2[:], ALU.add, r=["od", "o2"], w=["od"])
                P.tt("dve", skd[:, 2:3], skd[:, 2:3], skd[:, 3:4], ALU.add, r=["skd"], w=["skd"])
                P.recip(skd[:, 2:3], skd[:, 2:3], r=["skd"], w=["skd"])
                P.ts("dve", od[:], od[:], skd[:, 2:3], ALU.mult, r=["od", "skd"], w=["od"])
                for mh in range(4):
                    rows = slice(mh * 16, (mh + 1) * 16)
                    P.dma("sp", scrOm[:, mh * 64:(mh + 1) * 64], od[rows, :], r=["od"], w=["scrOm"])
                to_mix(scrOm, "scrOm", 256, 6)
                P.barrier()
                P.emit()

        P.enabled = True
        if do_ffn:
            with ExitStack() as ph2:
                def s2(name, shape, dt=F32):
                    return sb(name, shape, dt, ph2)
                wout = s2("wout", [128, 8, D], BF16)
                wff1 = s2("wff1", [128, 8, 4096], BF16)
                wff2 = s2("wff2", [128, 32, D], BF16)
                g2 = s2("g2", [128, D])
                mixb = [s2(f"mixb{i}", [128, 8, TB], BF16) for i in range(2)]
                xb2 = [s2(f"x2_{i}", [128, D]) for i in range(4)]
                xn2l = [s2(f"xn2_{i}", [128, D], BF16) for i in range(2)]
                h2Tl = [s2(f"h2T{i}", [128, 8, TB], BF16) for i in range(2)]
                aT = s2("aT", [128, 32, TB], BF16)
                rl = [s2(f"rl{i}", [128, TB]) for i in range(2)]
                ss2 = s2("ss2", [128, 4])
                P.dma("sp", g2[:], g2bc_d, w=["gbc"])
                for kc in range(8):
                    P.dma("pool", wout[:, kc, :], w_out[kc * 128:(kc + 1) * 128, :], w=[("wout", kc)])
                for kc in range(8):
                    for q in range(2):
                        P.dma("pool", wff1[:, kc, q * 2048:(q + 1) * 2048], w_ff1[kc * 128:(kc + 1) * 128, q * 2048:(q + 1) * 2048], w=[("wff1", kc)])
                for kc in range(32):
                    P.dma("pool", wff2[:, kc, :], w_ff2[kc * 128:(kc + 1) * 128, :], w=[("wff2", kc)])
                def partA(mb, km, tiles):
                    for (xin, yout, nr, c0, ti) in tiles:
                        kx = ("x2", ti)
                        P.dma("pool", xb2[ti][0:nr, :], xin, w=[kx])
                        for half in range(2):
                            for kc in range(8):
                                P.mm(ps[half][0:nr, :], mb[:, kc, c0:c0 + nr], wout[:, kc, half * 512:(half + 1) * 512],
                                     start=(kc == 0), stop=(kc == 7), r=[km, ("wout", kc)], w=[f"b{half}"])
                            P.tt("dve", xb2[ti][0:nr, half * 512:(half + 1) * 512], ps[half][0:nr, :], xb2[ti][0:nr, half * 512:(half + 1) * 512], ALU.add,
                                 r=[f"b{half}", kx], w=[kx])
                        norm_rows(xb2[ti], xn2l[ti % 2], g2, kx, f"xn2_{ti % 2}", ss2, nrows=nr)

                def partT(tiles, hb_):
                    h2 = h2Tl[hb_]
                    for (xin, yout, nr, c0, ti) in tiles:
                        for kc in range(8):
                            P.tr(psT[:, kc * 128:kc * 128 + nr], xn2l[ti % 2][0:nr, kc * 128:(kc + 1) * 128], identb[0:nr, 0:nr],
                                 r=[f"xn2_{ti % 2}", "identb"], w=["bT"])
                        P.copy("act", h2[:, :, c0:c0 + nr], psT[:].rearrange("p (k t) -> p k t", t=128)[:, :, 0:nr], r=["bT"], w=[f"h2T{hb_}"])

                def ff1(W, hb_):
                    h2 = h2Tl[hb_]
                    for oc in range(32):
                        bank = 2 + oc % 2
                        for kc in range(8):
                            P.mm(ps[bank][:, 0:W], wff1[:, kc, oc * 128:(oc + 1) * 128], h2[:, kc, 0:W], start=(kc == 0), stop=(kc == 7),
                                 r=[("wff1", kc), f"h2T{hb_}"], w=[f"b{bank}"])
                        P.act(rl[oc % 2][:, 0:W], ps[bank][:, 0:W], AF.Relu, r=[f"b{bank}"], w=[f"rl{oc % 2}"])
                        P.tt("pool", aT[:, oc, 0:W], rl[oc % 2][:, 0:W], rl[oc % 2][:, 0:W], ALU.mult, r=[f"rl{oc % 2}"], w=[("aT", oc)])

                def ff2(tiles):
                    for (xin, yout, nr, c0, ti) in tiles:
                        kx = ("x2", ti)
                        for half in range(2):
                            bank = 4 + half
                            for kc in range(32):
                                P.mm(ps[bank][0:nr, :], aT[:, kc, c0:c0 + nr], wff2[:, kc, half * 512:(half + 1) * 512],
                                     start=(kc == 0), stop=(kc == 31), r=[("aT", kc), ("wff2", kc)], w=[f"b{bank}"])
                            P.tt("dve", xb2[ti][0:nr, half * 512:(half + 1) * 512], ps[bank][0:nr, :], xb2[ti][0:nr, half * 512:(half + 1) * 512], ALU.add,
                                 r=[f"b{bank}", kx], w=[kx])
                        P.dma("sp", yout, xb2[ti][0:nr, :], r=[kx])

                work = []
                for blk in range(nblk):
                    tiles = [(xseq[(blk * 2 + ti) * 128:(blk * 2 + ti + 1) * 128, :], y_p[(blk * 2 + ti) * 128:(blk * 2 + ti + 1) * 128, :], 128, ti * 128,
                              (blk * 2 + ti) % 4) for ti in range(2)]
                    work.append(("p", blk, tiles, TB))
                if do_samp:
                    work.append(("s", nblk, [(xsamp, y_s, 16, 0, (nblk * 2) % 4)], 16))

                def mix_for(item):
                    kind, blk, tiles, W = item
                    if kind == "s":
                        return mixTs, "mixTs"
                    mb = mixb[blk % 2]
                    km = f"mixb{blk % 2}"
                    for kc in range(8):
                        P.dma("pool", mb[:, kc, :], mixD[kc, :, blk * TB:(blk + 1) * TB], r=["mixD"], w=[km])
                    return mb, km

                mb0, km0 = mix_for(work[0])
                partA(mb0, km0, work[0][2])
                partT(work[0][2], 0)
                for wi, item in enumerate(work):
                    kind, blk, tiles, W = item
                    ff1(W, wi % 2)
                    if wi + 1 < len(work):
                        mbn, kmn = mix_for(work[wi + 1])
                        partA(mbn, kmn, work[wi + 1][2])
                    ff2(tiles)
                    if wi + 1 < len(work):
                        partT(work[wi + 1][2], (wi + 1) % 2)
                P.emit()
        else:
            P.emit()
    return nc


def _consts():
    ident = np.eye(128, dtype=np.float32)
    maskg = np.zeros((128, 128), np.float32)
    for r in range(128):
        j = r % 64
        for c in range(128):
            t = c % 64
            maskg[r, c] = 1.0 if ((c < 64 and j < t) or (c >= 64 and j <= t)) else 0.0
    maskn = np.tile(np.tril(np.ones((64, 64), np.float32), -1), (2, 1))
    identd = np.tile(np.eye(64, dtype=np.float32), (2, 1))
    bavg = np.zeros((128, 128), np.float32)
    bavg[:64, :64] = 1.0 / 64
    bavg[64:, 64:] = 1.0 / 64
    ones = np.ones((128, 64), np.float32)
    rmask = np.ones((128, TB), np.float32)
    rmask[:, ::C] = 0.0
    bk = t5_bucket_np(np.arange(129))
    oneh = np.zeros((32, 129), np.float32)
    oneh[bk, np.arange(129)] = 1.0
    bkr = t5_bucket_np(128 - np.arange(129))
    onehr = np.zeros((32, 129), np.float32)
    onehr[bkr, np.arange(129)] = 1.0
    return dict(onehr=onehr, identd=identd, aident=np.ascontiguousarray(ident[::-1]), ident=ident, maskg=maskg, maskn=maskn, bavg=bavg, ones=ones, rmask=rmask, oneh=oneh)


def _prep_shared(inp):
    f = lambda a: np.ascontiguousarray(np.asarray(a, dtype=np.float32))
    sh = _consts()
    w_in = f(inp["w_in"][0])
    perm = np.arange(2560)
    for j in range(2):
        for hh in range(2):
            dst = 1792 + j * 128 + hh * 64
            src = 1792 + (2 * hh + j) * 64
            perm[dst:dst + 64] = np.arange(src, src + 64)
    sh["w_in"] = np.ascontiguousarray(w_in[:, perm])
    w_out = f(inp["w_out"][0])
    rperm = np.arange(1024)
    for j in range(2):
        for hh in range(2):
            dst = 512 + j * 128 + hh * 64
            src = 512 + (2 * hh + j) * 64
            rperm[dst:dst + 64] = np.arange(src, src + 64)
    sh["w_out"] = np.ascontiguousarray(w_out[rperm, :])
    sh["w_ff1"] = f(inp["w_ff1"][0])
    sh["w_ff2"] = f(inp["w_ff2"][0])
    sh["w_mem"] = f(inp["w_mem_kv"][0])
    sh["lwa"] = np.ascontiguousarray(np.concatenate([f(inp["w_up_w"][0]), f(inp["w_up_a"][0])], 0))
    sh["lwg"] = f(inp["w_up_g"][0])
    sh["relb"] = f(inp["rel_bias"])
    pc = np.zeros((128, NPC), np.float32)
    col = lambda v, n: np.asarray(v, np.float32).reshape(n, 128).T
    pc[:, PC["mu"]:PC["mu"] + 14] = col(inp["mu_shift"][0], 14)
    pc[:, PC["w0"]:PC["w0"] + 4] = col(inp["w0"][0], 4)
    pc[:, PC["a0"]:PC["a0"] + 4] = col(inp["a0"][0], 4)
    pc[:, PC["kk"]:PC["kk"] + 4] = col(inp["k_k"][0], 4)
    pc[:, PC["ka"]:PC["ka"] + 4] = col(inp["k_a"][0], 4)
    pc[:, PC["rk"]:PC["rk"] + 4] = col(np.asarray(inp["r_k"][0]).reshape(512), 4)
    pc[:, PC["lnw"]:PC["lnw"] + 4] = col(inp["lnx_w"][0], 4)
    pc[:, PC["lnb"]:PC["lnb"] + 4] = col(inp["lnx_b"][0], 4)
    t2 = lambda v: np.tile(np.asarray(v, np.float32).reshape(64), 2)
    pc[:, PC["qns"]] = t2(inp["q_norm_swa"][0])
    pc[:, PC["kns"]] = t2(inp["k_norm_swa"][0])
    pc[:, PC["qnm"]] = t2(inp["q_norm_mem"][0])
    pc[:, PC["knm"]] = t2(inp["k_norm_mem"][0])
    sk = np.asarray(inp["sinks"][0], np.float32)
    for j in range(2):
        for hh in range(2):
            pc[hh * 64:(hh + 1) * 64, PC["sink"] + j] = sk[2 * hh + j]
    sh["pc"] = pc
    bc = lambda v: np.ascontiguousarray(np.broadcast_to(np.asarray(v, np.float32).reshape(1, -1), (128, np.asarray(v).size)))
    sh["g1bc"] = bc(inp["norm1_g"][0])
    sh["g2bc"] = bc(inp["norm2_g"][0])
    sh["gmbc"] = bc(inp["mem_norm_g"][0])
    sh["knmbc"] = bc(np.tile(np.asarray(inp["k_norm_mem"][0], np.float32), 4))
    sh["lnwbh"] = np.ascontiguousarray(np.tile(np.asarray(inp["lnx_w"][0], np.float32).reshape(8, 64), (16, 1)))
    sh["lnbbh"] = np.ascontiguousarray(np.tile(np.asarray(inp["lnx_b"][0], np.float32).reshape(8, 64), (16, 1)))
    skd = np.zeros((64, 1), np.float32)
    for g in range(2):
        for kvh in range(2):
            skd[g * 32 + kvh * 16:g * 32 + kvh * 16 + 16, 0] = sk[2 * kvh + g]
    sh["skd"] = skd
    return sh


def _core_inputs(inp, sh, c):
    f = lambda a: np.ascontiguousarray(np.asarray(a, dtype=np.float32))
    m = dict(sh)
    m["xseq"] = f(inp["x_prompt"][c % 2])
    m["mem"] = f(inp["mem_prompt"][c % 2])
    b = slice(16 * c, 16 * c + 16)
    m["xsamp"] = f(inp["x_sample"][b, 0, :])
    m["srs_in"] = f(inp["state_rwkv"][0, b])
    m["shift_s"] = f(inp["state_shift"][0, b])
    m["ck_in"] = f(inp["cache_swa_k"][0, b])
    m["cv_in"] = f(inp["cache_swa_v"][0, b])
    m["cmk_in"] = f(inp["cache_mem_k"][0, b])
    m["cmv_in"] = f(inp["cache_mem_v"][0, b])
    return m


def kernel(**inp):
    nc = build()
    sh = _prep_shared(inp)
    in_maps = [_core_inputs(inp, sh, c) for c in range(8)]
    res = run_bass_kernel_spmd(nc, in_maps, core_ids=list(range(8)))
    R = res.results
    yp = np.stack([R[b]["y_p"] for b in range(2)])
    srp = np.stack([R[b]["srp"] for b in range(2)])[None]
    shp = np.stack([R[b]["shp"] for b in range(2)])[None]
    kbp = np.stack([R[b]["kbp"].reshape(128, 2, 64) for b in range(2)])[None]
    vbp = np.stack([R[b]["vbp"].reshape(128, 2, 64) for b in range(2)])[None]
    mkp = np.stack([R[b]["mkp"].reshape(256, 4, 64) for b in range(2)])[None]
    mvp = np.stack([R[b]["mvp"].reshape(256, 4, 64) for b in range(2)])[None]
    cat = lambda k: np.concatenate([R[c][k] for c in range(8)], 0)
    ys = cat("y_s").reshape(128, 1, 1024)
    srs = cat("srs")[None]
    shs = cat("shs")[None]
    kbs = cat("kbs").reshape(128, 128, 2, 64)[None]
    vbs = cat("vbs").reshape(128, 128, 2, 64)[None]
    return tuple(np.ascontiguousarray(a, dtype=np.float32) for a in (yp, ys, srp, shp, kbp, vbp, mkp, mvp, srs, shs, kbs, vbs))
```

```python
import numpy as np
import concourse.bass as bass
import concourse.mybir as mybir
from concourse.bass_utils import run_bass_kernel_spmd

F32 = mybir.dt.float32
BF16 = mybir.dt.bfloat16
AF = mybir.ActivationFunctionType
ALU = mybir.AluOpType
AX = mybir.AxisListType


class Prog:
    CE = ("pe", "act", "dve", "pool")

    def __init__(self, nc, n_dma_sems=12):
        self.nc = nc
        self.sem = {e: nc.alloc_semaphore(name=f"s_{e}") for e in self.CE}
        self.cnt = {e: 0 for e in self.CE}
        self.dsem = {q: [nc.alloc_semaphore(name=f"d_{q}{i}") for i in range(n_dma_sems)]
                     for q in ("sp", "pool", "act")}
        self.dval = {q: [0] * n_dma_sems for q in ("sp", "pool", "act")}
        self.dnext = {q: 0 for q in ("sp", "pool", "act")}
        self.semobj = {}
        for e in self.CE:
            self.semobj[("c", e)] = self.sem[e]
        for q in self.dsem:
            for i, s in enumerate(self.dsem[q]):
                self.semobj[("d", q, i)] = s
        self.ops = {e: [] for e in ("pe", "act", "dve", "pool", "sp")}
        self.lastw = {}
        self.readers = {}
        self.seen = {e: {} for e in self.ops}
        self.nops = 0

    def _deps(self, eng, reads, writes):
        need = {}

        def add(tok):
            sid, val = tok
            if need.get(sid, 0) < val:
                need[sid] = val

        for k in reads:
            t = self.lastw.get(k)
            if t is not None:
                add(t)
        for k in writes:
            t = self.lastw.get(k)
            if t is not None:
                add(t)
            for sid, val in self.readers.get(k, {}).items():
                add((sid, val))
        out = []
        seen = self.seen[eng]
        for sid, val in need.items():
            if eng == "pe" and sid == ("c", "pe"):
                continue
            if seen.get(sid, 0) >= val:
                continue
            seen[sid] = val
            out.append((sid, val))
        return out

    def _commit(self, tok, reads, writes):
        sid, val = tok
        for k in writes:
            self.lastw[k] = tok
            self.readers[k] = {}
        for k in reads:
            r = self.readers.setdefault(k, {})
            if r.get(sid, 0) < val:
                r[sid] = val

    enabled = True

    def op(self, eng, fn, r=(), w=()):
        if not self.enabled:
            return
        if eng in ("act", "dve"):
            banks = [k for k in list(r) + list(w) if isinstance(k, str) and len(k) == 2 and k[0] == "b"]
            if banks:
                w = list(w) + [("psrd", k) for k in banks]
        waits = self._deps(eng, r, w)
        self.cnt[eng] += 1
        tok = (("c", eng), self.cnt[eng])
        self.ops[eng].append((waits, fn, ("c", eng), 1))
        self._commit(tok, r, w)
        self.nops += 1

    def dma(self, q, out, in_, r=(), w=(), **kw):
        if not self.enabled:
            return
        i = self.dnext[q]
        n = len(self.dsem[q])
        self.dnext[q] = (i + 1) % n
        sid = ("d", q, i)
        waits = self._deps(q, r, w)
        prev = self.dval[q][i]
        if prev > 0 and self.seen[q].get(sid, 0) < prev:
            self.seen[q][sid] = prev
            waits.append((sid, prev))
        self.dval[q][i] += 16
        tok = (sid, self.dval[q][i])
        self.ops[q].append((waits, (lambda e, o=out, s=in_, kw=kw: e.dma_start(out=o, in_=s, **kw)), sid, 16))
        self._commit(tok, r, w)
        self.nops += 1

    def barrier(self):
        toks = []
        for q in self.dsem:
            for i in range(len(self.dsem[q])):
                if self.dval[q][i] > 0:
                    toks.append((("d", q, i), self.dval[q][i]))
        for e in self.CE:
            if self.cnt[e] > 0:
                toks.append((("c", e), self.cnt[e]))
        for e in self.ops:
            w = [(sid, v) for sid, v in toks if self.seen[e].get(sid, 0) < v]
            for sid, v in w:
                self.seen[e][sid] = v
            self.ops[e].append((w, None, None, 0))

    def emit(self):
        nc = self.nc
        fin = []
        for q in self.dsem:
            for i in range(len(self.dsem[q])):
                if self.dval[q][i] > 0:
                    fin.append((("d", q, i), self.dval[q][i]))
        for e in self.CE:
            if self.cnt[e] > 0:
                fin.append((("c", e), self.cnt[e]))
        ops = self.ops
        semobj = self.semobj

        def run(eng, lst, final=None):
            for waits, fn, sid, inc in lst:
                for ws, wv in waits:
                    eng.wait_ge(semobj[ws], wv)
                if fn is not None:
                    if inc == 0:
                        fn(eng)
                    else:
                        fn(eng).then_inc(semobj[sid], inc)
            if final:
                for ws, wv in final:
                    eng.wait_ge(semobj[ws], wv)

        with nc.Block() as block:
            @block.sync
            def _(e):
                run(e, ops["sp"], fin)

            @block.tensor
            def _(e):
                run(e, ops["pe"])

            @block.scalar
            def _(e):
                run(e, ops["act"])

            @block.vector
            def _(e):
                run(e, ops["dve"])

            @block.gpsimd
            def _(e):
                run(e, ops["pool"])
        for e in self.ops:
            self.ops[e] = []

    pe_mode = None

    def _pe_mode(self, st):
        if not self.enabled:
            return
        ru = lambda n: 32 if n <= 32 else (64 if n <= 64 else 128)
        k = st.partition_size()
        m = st.free_size()
        mode = (ru(k), ru(m))
        if self.pe_mode is not None and mode != self.pe_mode:
            self.ops["pe"].append(([], (lambda e: e.drain()), None, 0))
        self.pe_mode = mode

    def mm(self, out, lhsT, rhs, start=True, stop=True, r=(), w=()):
        self._pe_mode(lhsT)
        self.op("pe", lambda e: e.matmul(out, lhsT, rhs, start=start, stop=stop), r, w)

    def tr(self, out, in_, ident, r=(), w=()):
        self._pe_mode(in_)
        self.op("pe", lambda e: e.transpose(out, in_, ident), r, w)

    def act(self, out, in_, func, r=(), w=(), **kw):
        self.op("act", lambda e: e.activation(out=out, in_=in_, func=func, **kw), r, w)

    def tt(self, eng, out, in0, in1, op, r=(), w=()):
        self.op(eng, lambda e: e.tensor_tensor(out=out, in0=in0, in1=in1, op=op), r, w)

    def ts(self, eng, out, in0, s1, op0, s2=None, op1=None, r=(), w=()):
        if op1 is None:
            self.op(eng, lambda e: e.tensor_scalar(out=out, in0=in0, scalar1=s1, scalar2=None, op0=op0), r, w)
        else:
            self.op(eng, lambda e: e.tensor_scalar(out=out, in0=in0, scalar1=s1, scalar2=s2, op0=op0, op1=op1), r, w)

    def stt(self, out, in0, scalar, in1, op0, op1, r=(), w=()):
        self.op("dve", lambda e: e.scalar_tensor_tensor(out=out, in0=in0, scalar=scalar, in1=in1, op0=op0, op1=op1), r, w)

    def copy(self, eng, out, in_, r=(), w=()):
        if eng == "act":
            self.op("act", lambda e: e.copy(out=out, in_=in_), r, w)
        else:
            self.op(eng, lambda e: e.tensor_scalar(out=out, in0=in_, scalar1=1.0, scalar2=None, op0=ALU.mult), r, w)

    def recip(self, out, in_, r=(), w=()):
        self.op("dve", lambda e: e.reciprocal(out=out, in_=in_), r, w)


L = 8192
D = 1024
TB = 256
C = 64
NCH = TB // C
EXPM05 = float(np.exp(-0.5))
SCALE = 0.125

PC = {}
_o = 0
for _n, _w in [("mu", 14), ("omu", 14), ("w0", 4), ("a0", 4), ("kk", 4), ("ka", 4), ("omka", 4), ("rk", 4),
               ("lnw", 4), ("lnb", 4), ("qns", 1), ("kns", 1), ("qnm", 1), ("knm", 1), ("sink", 2), ("esk", 2)]:
    PC[_n] = _o
    _o += _w
NPC = _o


def t5_bucket_np(dist):
    dist = np.asarray(dist)
    d = np.maximum(dist, 1).astype(np.float32)
    large = 16 + (np.log(d / np.float32(16)) / np.float32(np.log(128 / 16)) * np.float32(16)).astype(np.int32)
    large = np.minimum(large, 31)
    return np.where(dist < 16, dist, large)


def build(nblk=L // TB, do_ffn=True, stage=9, do_samp=True):
    nc = bass.Bass("TRN2", target_bir_lowering=False)
    NTOK = nblk * TB

    def din(name, shape, dt=F32):
        return nc.dram_tensor(name, list(shape), dt, kind="ExternalInput").ap()

    def dout(name, shape, dt=F32):
        return nc.dram_tensor(name, list(shape), dt, kind="ExternalOutput").ap()

    xseq = din("xseq", [L, D])
    mem = din("mem", [256, D])
    w_in = din("w_in", [D, 2560])
    w_out = din("w_out", [D, D])
    w_ff1 = din("w_ff1", [D, 4096])
    w_ff2 = din("w_ff2", [4096, D])
    w_mem = din("w_mem", [D, 512])
    lwa_d = din("lwa", [128, 512])
    lwg_d = din("lwg", [128, 512])
    relb = din("relb", [32, 4])
    pc_d = din("pc", [128, NPC])
    g1bc_d = din("g1bc", [128, D])
    g2bc_d = din("g2bc", [128, D])
    gmbc_d = din("gmbc", [128, D])
    ident_d = din("ident", [128, 128])
    maskg_d = din("maskg", [128, 128])
    maskn_d = din("maskn", [128, 64])
    identd_d = din("identd", [128, 64])
    bavg_d = din("bavg", [128, 128])
    ones_d = din("ones", [128, 64])
    rmask_d = din("rmask", [128, TB])
    oneh_d = din("oneh", [32, 129])
    knmbc_d = din("knmbc", [128, 256])
    aident_d = din("aident", [128, 128])

    xsamp = din("xsamp", [16, D])
    srs_in = din("srs_in", [16, 8, 64, 64])
    shift_s = din("shift_s", [16, 1792])
    ck_in = din("ck_in", [16, 128, 2, 64])
    cv_in = din("cv_in", [16, 128, 2, 64])
    cmk_in = din("cmk_in", [16, 256, 4, 64])
    cmv_in = din("cmv_in", [16, 256, 4, 64])
    lnwbh_d = din("lnwbh", [128, 64])
    lnbbh_d = din("lnbbh", [128, 64])
    skd_d = din("skd", [64, 1])
    onehr_d = din("onehr", [32, 129])
    y_s = dout("y_s", [16, D])
    srs = dout("srs", [16, 8, 64, 64])
    shs = dout("shs", [16, 1792])
    kbs = dout("kbs", [16, 128, 128])
    vbs = dout("vbs", [16, 128, 128])
    y_p = dout("y_p", [L, D])
    srp = dout("srp", [8, 64, 64])
    shp = dout("shp", [1792])
    kbp = dout("kbp", [128, 128])
    vbp = dout("vbp", [128, 128])
    mkp = dout("mkp", [256, 256])
    mvp = dout("mvp", [256, 256])

    mixD = nc.dram_tensor("mixD", [8, 128, L], BF16, kind="Internal").ap()
    fscr = nc.dram_tensor("fscr", [4, 512], F32, kind="Internal").ap()
    fdscr = nc.dram_tensor("fdscr", [4, 129], F32, kind="Internal").ap()
    scrV = nc.dram_tensor("scrV", [16, 8, 8, 64], F32, kind="Internal").ap()
    scrMix = nc.dram_tensor("scrMix", [16, 512], F32, kind="Internal").ap()
    scrQ = nc.dram_tensor("scrQ", [16, 512], F32, kind="Internal").ap()
    scrK = nc.dram_tensor("scrK", [16, 128], F32, kind="Internal").ap()
    scrVn = nc.dram_tensor("scrVn", [16, 128], F32, kind="Internal").ap()
    scrO = nc.dram_tensor("scrO", [16, 256], F32, kind="Internal").ap()
    scrOm = nc.dram_tensor("scrOm", [16, 256], F32, kind="Internal").ap()

    P = Prog(nc)

    def pcol(name, j=0):
        return pc[:, PC[name] + j:PC[name] + j + 1]

    from contextlib import ExitStack
    with ExitStack() as top:
        def sb(name, shape, dt=F32, st=top):
            return st.enter_context(nc.sbuf_tensor("s_" + name, list(shape), dt))

        ident = sb("ident", [128, 128])
        identb = sb("identb", [128, 128], BF16)
        pc = sb("pc", [128, NPC])
        mixTs = sb("mixTs", [128, 8, 16], BF16)
        ps = [top.enter_context(nc.psum_tensor(f"b{i}", [128, 512], F32)) for i in range(7)]
        psT = top.enter_context(nc.psum_tensor("bT", [128, 1024], BF16))
        P.dma("sp", ident[:], ident_d, w=["ident"])
        P.dma("pool", identb[:], ident_d, w=["identb"])
        P.dma("sp", pc[:], pc_d, w=["pc"])
        P.ts("dve", pc[:, PC["omu"]:PC["omu"] + 14], pc[:, PC["mu"]:PC["mu"] + 14], -1.0, ALU.mult, 1.0, ALU.add, r=["pc"], w=["pc"])
        P.ts("dve", pc[:, PC["omka"]:PC["omka"] + 4], pc[:, PC["ka"]:PC["ka"] + 4], -1.0, ALU.mult, 1.0, ALU.add, r=["pc"], w=["pc"])
        P.act(pc[:, PC["esk"]:PC["esk"] + 2], pc[:, PC["sink"]:PC["sink"] + 2], AF.Exp, r=["pc"], w=["pc"])

        def norm_rows(x_t, xn_t, gbc, key_x, key_xn, ss, nrows=128):
            P.act(xn_t[0:nrows, :], x_t[0:nrows, :], AF.Square, r=[key_x], w=[key_xn, "ss"], accum_out=ss[0:nrows, 0:1])
            P.act(ss[0:nrows, 1:2], ss[0:nrows, 0:1], AF.Sqrt, r=["ss"], w=["ss"], scale=1.0 / D, bias=1e-6)
            P.recip(ss[0:nrows, 2:3], ss[0:nrows, 1:2], r=["ss"], w=["ss"])
            P.stt(xn_t[0:nrows, :], x_t[0:nrows, :], ss[0:nrows, 2:3], gbc[0:nrows, :], ALU.mult, ALU.mult, r=[key_x, "ss", "gbc"], w=[key_xn])

        def headnorm(psx, ncols, gcol, outs, tmps, keyp, eps=1e-6):
            sq, sd = tmps
            P.act(sq[:, 0:ncols], psx, AF.Square, r=[keyp], w=["hn_sq"])
            P.mm(ps[3][:, 0:ncols], bavg[:], sq[:, 0:ncols], r=["bavg", "hn_sq"], w=["b3"])
            P.act(sd[:, 0:ncols], ps[3][:, 0:ncols], AF.Ln, r=["b3"], w=["hn_sd"], bias=eps)
            P.act(sd[:, 0:ncols], sd[:, 0:ncols], AF.Exp, r=["hn_sd"], w=["hn_sd"], scale=-0.5)
            for o, k in outs:
                P.stt(o, psx, gcol, sd[:, 0:ncols], ALU.mult, ALU.mult, r=[keyp, "hn_sd", "pc"], w=[k])

        with ExitStack() as ph1:
            def s1(name, shape, dt=F32):
                return sb(name, shape, dt, ph1)

            win = s1("win", [128, 8, 2560], BF16)
            lwa = s1("lwab", [128, 512], BF16)
            lwg = s1("lwgb", [128, 512], BF16)
            maskg = s1("maskg", [128, 128])
            maskn = s1("maskn", [128, 64])
            bavg = s1("bavg", [128, 128])
            ones = s1("onesb", [128, 64], BF16)
            rmask = s1("rmask", [128, TB])
            gbc = s1("gbc", [128, D])
            biasT = s1("biasT", [128, 1024])
            ss = s1("ss", [128, 4])
            xbuf = [s1(f"xb{i}", [128, D]) for i in range(2)]
            xn = s1("xn", [128, D], BF16)
            hTb = [s1("hT0", [128, 8, TB], BF16), s1("hT1", [128, 8, TB], BF16)]
            pT = s1("pT", [128, 14, TB + 1])
            xs = s1("xs", [128, 14, TB])
            lin = s1("lin", [128, TB], BF16)
            sg = s1("sg", [128, TB], BF16)
            tmp = [s1(f"t{i}", [128, TB]) for i in range(12)]
            hnA = s1("hnA", [128, TB])
            hnB = s1("hnB", [128, TB])
            gT = s1("gT", [128, 4, TB])
            bonus = s1("bonus", [128, 4, TB])
            epos = s1("epos", [128, 4, TB])
            AR = s1("AR", [128, 4, NCH, 2, C], BF16)
            BK = s1("BK", [128, 4, NCH, 2, C], BF16)
            GB = s1("GB", [128, 2, 4, 128], BF16)
            GK = s1("GK", [128, 2, 4, 128], BF16)
            Ab = [s1(f"Ab{i}", [128, 2, 4, 64], BF16) for i in range(2)]
            Bb = [s1(f"Bb{i}", [128, 2, 4, 64], BF16) for i in range(2)]
            PTt = s1("PTt", [128, 2, 4, 64], BF16)
            Btk = s1("Btk", [128, 2, 4, 64], BF16)
            Ktk = s1("Ktk", [128, 2, 4, 64], BF16)
            Vtk = s1("Vtk", [128, 2, 4, 64], BF16)
            Utk = s1("Utk", [128, 4, 64], BF16)
            Zs = s1("Zs", [128, 4, 64], BF16)
            Zf = s1("Zf", [128, 4, 64])
            STb = s1("STb", [128, 4, 64], BF16)
            vb = s1("vb", [128, 4, TB], BF16)
            identd = s1("identd", [128, 64])
            yT = s1("yT", [128, 4, TB])
            ST = [s1(f"ST{i}", [128, 4, 64]) for i in range(2)]
            QT = s1("QT", [128, 4, TB], BF16)
            KTr = s1("KTr", [128, 3, 128], BF16)
            KTf = s1("KTf", [128, TB])
            Vs = s1("Vs", [128, 3, 128], BF16)
            Vf = s1("Vf", [128, 128])
            tmpS = s1("tmpS", [128, 1024])
            PTb = s1("PTb", [128, 1024], BF16)
            den = s1("den", [128, 256])
            mkT = s1("mkT", [128, 2, 256], BF16)
            mv = s1("mv", [128, 2, 256], BF16)
            mixT = s1("mixT", [128, 8, TB], BF16)

            for kc in range(8):
                P.dma("pool", win[:, kc, :], w_in[kc * 128:(kc + 1) * 128, :], w=[("win", kc)], max_dma_last_dim=4096)
            P.dma("pool", lwa[:], lwa_d, w=["lwa"])
            P.dma("pool", lwg[:], lwg_d, w=["lwg"])
            P.dma("pool", ones[:], ones_d, w=["ones"])
            P.dma("sp", maskg[:], maskg_d, w=["maskg"])
            P.dma("sp", maskn[:], maskn_d, w=["maskn"])
            P.dma("sp", identd[:], identd_d, w=["identd"])
            P.dma("sp", bavg[:], bavg_d, w=["bavg"])
            P.dma("sp", rmask[:], rmask_d, w=["rmask"])

            P.enabled = stage >= 1
            fpad = tmp[0]
            relb_s = tmp[1]
            oneh_s = tmp[2]
            P.dma("sp", relb_s[0:32, 0:4], relb, w=["t1"])
            P.dma("sp", oneh_s[0:32, 0:129], oneh_d, w=["t2"])
            P.op("dve", lambda e: e.memset(tmpS[0:4, 0:512], -30000.0), w=["tmpS"])
            P.mm(ps[0][0:4, 0:129], relb_s[0:32, 0:4], oneh_s[0:32, 0:129], r=["t1", "t2"], w=["b0"])
            P.copy("dve", tmpS[0:4, 128:257], ps[0][0:4, 0:129], r=["b0"], w=["tmpS"])
            P.dma("sp", fscr, tmpS[0:4, 0:512], r=["tmpS"], w=["fscr"])
            aid = tmp[3]
            P.dma("sp", aid[:, 0:128], aident_d, w=["t3"])
            for j in range(2):
                for hh in range(2):
                    qh = 2 * hh + j
                    for blk in range(2):
                        idx = (hh * 2 + j) * 2 + blk
                        base = 256 if blk == 0 else 128
                        src = bass.AP(fscr.tensor, qh * 512 + base - 127, [[1, 128], [1, 128]])
                        P.dma("sp", PTb[:, 0:256].bitcast(F32)[:, 0:128] if False else tmp[4 + (idx % 2)][:, 0:128], src, r=["fscr"], w=[f"t{4 + idx % 2}"])
                        bank = idx // 4
                        P.mm(ps[bank][:, (idx % 4) * 128:(idx % 4 + 1) * 128], aid[:, 0:128], tmp[4 + (idx % 2)][:, 0:128], r=["t3", f"t{4 + idx % 2}"], w=[f"b{bank}"])
            for bank in range(2):
                P.copy("act", biasT[:, bank * 512:(bank + 1) * 512], ps[bank][:, :], r=[f"b{bank}"], w=["biasT"])
            P.enabled = stage >= 2
            P.dma("sp", gbc[:], gmbc_d, w=["gbc"])
            phM = ExitStack()
            knmbc = sb("knmbc", [128, 256], F32, phM)
            P.dma("sp", knmbc[:], knmbc_d, w=["knmbc"])
            mhT = sb("mhT", [128, 8, 256], BF16, phM)
            wmem = sb("wmemb", [128, 8, 512], BF16, phM)
            for kc in range(8):
                P.dma("pool", wmem[:, kc, :], w_mem[kc * 128:(kc + 1) * 128, :], w=[("wmem", kc)])
            for mt in range(2):
                xt = xbuf[mt]
                P.dma("sp", xt[:], mem[mt * 128:(mt + 1) * 128, :], w=[("xb", mt)])
                norm_rows(xt, xn, gbc, ("xb", mt), "xn", ss)
                for kc in range(8):
                    P.tr(psT[:, kc * 128:(kc + 1) * 128], xn[:, kc * 128:(kc + 1) * 128], identb[:], r=["xn", "identb"], w=["bT"])
                P.mm(ps[6][:, 0:16], identb[:, 0:128], identb[:, 0:16], r=["identb"], w=["bT", "b6"])
                P.copy("act", mhT[:, :, mt * 128:(mt + 1) * 128], psT[:].rearrange("p (k t) -> p k t", t=128), r=["bT"], w=["mhT"])
            P.enabled = stage >= 2.2
            for j in range(2):
                for kc in range(8):
                    P.mm(ps[0][:, 0:256], wmem[:, kc, j * 128:(j + 1) * 128], mhT[:, kc, :], start=(kc == 0), stop=(kc == 7),
                         r=[("wmem", kc), "mhT"], w=["b0"])
                headnorm(ps[0][:, 0:256], 256, pcol("knm"), [(mkT[:, j, :], "mkT")], (hnA, hnB), "b0")
            P.enabled = stage >= 2.4
            for mt in range(2):
                for kc in range(8):
                    P.mm(ps[1][:, 0:512], mhT[:, kc, mt * 128:(mt + 1) * 128], wmem[:, kc, :], start=(kc == 0), stop=(kc == 7),
                         r=[("wmem", kc), "mhT"], w=["b1"])
                P.copy("act", mv[:, mt, :], ps[1][:, 256:512], r=["b1"], w=["mv"])
                P.copy("act", tmp[5][:, 0:256], ps[1][:, 256:512], r=["b1"], w=["t5"])
                P.dma("sp", mvp[mt * 128:(mt + 1) * 128, :], tmp[5][:, 0:256], r=["t5"])
                P.act(tmp[6][:, 0:256], ps[1][:, 0:256], AF.Square, r=["b1"], w=["t6"])
                P.op("dve", lambda e: e.tensor_reduce(out=ss[:, 0:4], in_=tmp[6][:, 0:256].rearrange("p (h d) -> p h d", d=64), axis=AX.X, op=ALU.add),
                     r=["t6"], w=["ss"])
                P.act(ss[:, 0:4], ss[:, 0:4], AF.Sqrt, r=["ss"], w=["ss"], scale=1.0 / 64, bias=1e-6)
                P.recip(ss[:, 0:4], ss[:, 0:4], r=["ss"], w=["ss"])
                P.tt("dve", tmp[7][:, 0:256].rearrange("p (h d) -> p h d", d=64), ps[1][:, 0:256].rearrange("p (h d) -> p h d", d=64),
                     ss[:, 0:4].unsqueeze(2).to_broadcast([128, 4, 64]), ALU.mult, r=["b1", "ss"], w=["t7"])
                P.tt("dve", tmp[7][:, 0:256], tmp[7][:, 0:256], knmbc[:], ALU.mult, r=["t7", "knmbc"], w=["t7"])
                P.dma("sp", mkp[mt * 128:(mt + 1) * 128, :], tmp[7][:, 0:256], r=["t7"])

            P.enabled = True
            P.barrier()
            P.emit()
            phM.close()
            tmp = tmp + [s1(f"t{i}", [128, TB]) for i in range(12, 40)]
            P.enabled = stage >= 3
            P.dma("sp", gbc[:], g1bc_d, w=["gbc"])
            P.op("dve", lambda e: e.memset(pT[:, :, 0:1], 0.0), w=["pT"])
            P.op("dve", lambda e: e.memset(ST[0][:], 0.0), w=["ST0"])
            P.op("pool", lambda e: e.memset(KTr[:], 0.0), w=[("KTr", 0), ("KTr", 1), ("KTr", 2)])
            P.op("pool", lambda e: e.memset(Vs[:], 0.0), w=[("Vs", 0), ("Vs", 1), ("Vs", 2)])
            xs_flat = xs[:].rearrange("p a t -> p (a t)")

            def A0(b):
                hT_ = hTb[b % 2]
                for ti in range(2):
                    t = b * 2 + ti
                    xt = xbuf[ti]
                    P.dma("sp", xt[:], xseq[t * 128:(t + 1) * 128, :], w=[("xb", ti)])
                    norm_rows(xt, xn, gbc, ("xb", ti), "xn", ss)
                    for kc in range(8):
                        P.tr(psT[:, kc * 128:(kc + 1) * 128], xn[:, kc * 128:(kc + 1) * 128], identb[:], r=["xn", "identb"], w=["bT"])
                    P.copy("act", hT_[:, :, ti * 128:(ti + 1) * 128], psT[:].rearrange("p (k t) -> p k t", t=128), r=["bT"], w=[f"hT{b % 2}"])

            A0(0)
            for blk in range(nblk):
                hT = hTb[blk % 2]
                kh = f"hT{blk % 2}"
                def proj(oc, bank):
                    for kc in range(8):
                        P.mm(ps[bank][:, 0:TB], win[:, kc, oc * 128:(oc + 1) * 128], hT[:, kc, :], start=(kc == 0), stop=(kc == 7),
                             r=[("win", kc), kh], w=[f"b{bank}"])

                for oc in range(14):
                    bank = 5 + (oc % 2)
                    proj(oc, bank)
                    P.copy("act", pT[:, oc, 1:TB + 1], ps[bank][:, 0:TB], r=[f"b{bank}"], w=[("pT", oc)])
                    P.act(tmp[0][:], pT[:, oc, 0:TB], AF.Copy, r=[("pT", oc), "pT", "pc"], w=["t0"], scale=pcol("mu", oc))
                    P.stt(xs[:, oc, :], pT[:, oc, 1:TB + 1], pcol("omu", oc), tmp[0][:], ALU.mult, ALU.add, r=[("pT", oc), "t0", "pc"], w=[("xs", oc)])
                    P.copy("pool", pT[:, oc, 0:1], pT[:, oc, TB:TB + 1], r=[("pT", oc)], w=[("pT", oc)])

                P.enabled = stage >= 4
                for j in range(2):
                    proj(14 + j, 5)
                    headnorm(ps[5][:, 0:TB], TB, pcol("qns"), [(QT[:, j, :], "QT")], (hnA, hnB), "b5")
                proj(16, 5)
                for ti in range(2):
                    t = blk * 2 + ti
                    headnorm(ps[5][:, ti * 128:(ti + 1) * 128], 128, pcol("kns"),
                             [(KTr[:, t % 3, :], ("KTr", t % 3)), (KTf[:, ti * 128:(ti + 1) * 128], "KTf")], (hnA, hnB), "b5")
                for j in range(2):
                    proj(18 + j, 5)
                    headnorm(ps[5][:, 0:TB], TB, pcol("qnm"), [(QT[:, 2 + j, :], "QT")], (hnA, hnB), "b5")
                for ti in range(2):
                    t = blk * 2 + ti
                    for kc in range(8):
                        P.mm(ps[6][:, 0:128], hT[:, kc, ti * 128:(ti + 1) * 128], win[:, kc, 2176:2304], start=(kc == 0), stop=(kc == 7),
                             r=[("win", kc), kh], w=["b6"])
                    P.copy("act", Vs[:, t % 3, :], ps[6][:, 0:128], r=["b6"], w=[("Vs", t % 3)])
                    if blk == nblk - 1 and ti == 1:
                        P.copy("dve", Vf[:], ps[6][:, 0:128], r=["b6"], w=["Vf"])

                def attn_tail(o_lhs, nblkk, has_sink, mixc, tcols, okeys):
                    for j in range(2):
                        for hh in range(2):
                            for b in range(nblkk[0], 2):
                                idx = (hh * 2 + j) * 2 + b
                                P.mm(ps[2][hh * 64:(hh + 1) * 64, j * 128:(j + 1) * 128], o_lhs(j, hh, b), PTb[:, idx * 128:(idx + 1) * 128],
                                     start=(b == nblkk[0]), stop=(b == 1), r=["PTb"] + okeys, w=["b2"])
                            for b in range(nblkk[0], 2):
                                idx = (hh * 2 + j) * 2 + b
                                P.mm(ps[2][hh * 64:(hh + 1) * 64, 256 + j * 128:256 + (j + 1) * 128], ones[:, 0:64], PTb[:, idx * 128:(idx + 1) * 128],
                                     start=(b == nblkk[0]), stop=(b == 1), r=["PTb", "ones"], w=["b2"])
                    if stage < 5.3:
                        return
                    if has_sink:
                        P.tt("dve", den[:].rearrange("p (j q) -> p j q", q=128), ps[2][:, 256:512].rearrange("p (j q) -> p j q", q=128),
                             pc[:, PC["esk"]:PC["esk"] + 2].unsqueeze(2).to_broadcast([128, 2, 128]), ALU.add, r=["b2", "pc"], w=["den"])
                        P.act(den[:], den[:], AF.Ln, r=["den"], w=["den"])
                    else:
                        P.act(den[:], ps[2][:, 256:512], AF.Ln, r=["b2"], w=["den"])
                    P.act(den[:], den[:], AF.Exp, r=["den"], w=["den"], scale=-1.0)
                    P.tt("dve", mixT[:, mixc:mixc + 2, tcols], ps[2][:, 0:256].rearrange("p (j q) -> p j q", q=128),
                         den[:].rearrange("p (j q) -> p j q", q=128), ALU.mult, r=["b2", "den"], w=["mixT"])

                def attn_gen():
                    for ti in range(2):
                        t = blk * 2 + ti
                        tcols = slice(ti * 128, (ti + 1) * 128)
                        b0 = 1 if t == 0 else 0
                        for j in range(2):
                            for hh in range(2):
                                for b in range(0, 2):
                                    idx = (hh * 2 + j) * 2 + b
                                    slot = (t - 1 + b) % 3
                                    bank = idx // 4
                                    P.mm(ps[bank][:, (idx % 4) * 128:(idx % 4 + 1) * 128], KTr[hh * 64:(hh + 1) * 64, slot, :],
                                         QT[hh * 64:(hh + 1) * 64, j, tcols], r=[("KTr", slot), "QT"], w=[f"b{bank}"])
                        for bank in range(2):
                            P.stt(tmpS[:, bank * 512:(bank + 1) * 512], ps[bank][:, :], SCALE, biasT[:, bank * 512:(bank + 1) * 512], ALU.mult, ALU.add,
                                  r=[f"b{bank}", "biasT"], w=["tmpS"])
                        yield
                        P.act(PTb[:], tmpS[:], AF.Exp, r=["tmpS"], w=["PTb"])
                        yield
                        attn_tail(lambda j, hh, b: Vs[:, (t - 1 + b) % 3, hh * 64:(hh + 1) * 64], (b0,), True, 4, tcols, [("Vs", (t - 1) % 3), ("Vs", t % 3)])
                        yield
                        for j in range(2):
                            for hh in range(2):
                                for b in range(2):
                                    idx = (hh * 2 + j) * 2 + b
                                    bank = idx // 4
                                    P.mm(ps[bank][:, (idx % 4) * 128:(idx % 4 + 1) * 128], mkT[hh * 64:(hh + 1) * 64, j, b * 128:(b + 1) * 128],
                                         QT[hh * 64:(hh + 1) * 64, 2 + j, tcols], r=["mkT", "QT"], w=[f"b{bank}"])
                        for bank in range(2):
                            P.act(PTb[:, bank * 512:(bank + 1) * 512], ps[bank][:, :], AF.Exp, r=[f"b{bank}"], w=["PTb"], scale=SCALE)
                        yield
                        attn_tail(lambda j, hh, b: mv[:, b, (2 * j + hh) * 64:(2 * j + hh + 1) * 64], (0,), False, 6, tcols, ["mv"])
                        yield


                T = lambda cc, k: tmp[cc * 10 + k]
                tk = lambda cc, k: f"t{cc * 10 + k}"
                r_ = lambda cc: xs[:, cc, :]
                k_ = lambda cc: xs[:, 4 + cc, :]
                v_ = lambda cc: xs[:, 8 + cc, :]
                rk = lambda cc: [("xs", cc), ("xs", 4 + cc), ("xs", 8 + cc)]
                hb = lambda cc: slice((cc % 2) * TB, (cc % 2 + 1) * TB)
                pW = lambda cc: ps[4][:, hb(cc)]
                pA = lambda cc: ps[5][:, hb(cc)]
                v4 = lambda a: a.rearrange("p (c t) -> p c t", t=C)
                steps = [
                    lambda cc: P.mm(pW(cc), lwa[0:64, cc * 128:(cc + 1) * 128], lin[0:64, :], r=["lwa", "lin"], w=["b4"]),
                    lambda cc: P.mm(pA(cc), lwa[64:128, cc * 128:(cc + 1) * 128], lin[64:128, :], r=["lwa", "lin"], w=["b5"]),
                    lambda cc: P.mm(ps[6][:, hb(cc)], lwg[:, cc * 128:(cc + 1) * 128], sg[:], r=["lwg", "sg"], w=["b6"]),
                    lambda cc: P.act(T(cc, 0)[:], pW(cc), AF.Sigmoid, r=["b4", "pc"], w=[tk(cc, 0)], bias=pcol("w0", cc)),
                    lambda cc: P.act(T(cc, 1)[:], pA(cc), AF.Sigmoid, r=["b5", "pc"], w=[tk(cc, 1)], bias=pcol("a0", cc)),
                    lambda cc: P.copy("act", gT[:, cc, :], ps[6][:, hb(cc)], r=["b6"], w=["gT"]),
                    lambda cc: P.ts("dve", T(cc, 2)[:], k_(cc), pcol("kk", cc), ALU.mult, r=rk(cc) + ["pc"], w=[tk(cc, 2)]),
                    lambda cc: P.act(T(cc, 3)[:], T(cc, 2)[:], AF.Square, r=[tk(cc, 2)], w=[tk(cc, 3)]),
                    lambda cc: P.mm(ps[3][:, hb(cc)], bavg[:], T(cc, 3)[:], r=["bavg", tk(cc, 3)], w=["b3"]),
                    lambda cc: P.act(T(cc, 3)[:], ps[3][:, hb(cc)], AF.Ln, r=["b3"], w=[tk(cc, 3)], scale=64.0, bias=1e-18),
                    lambda cc: P.act(T(cc, 3)[:], T(cc, 3)[:], AF.Exp, r=[tk(cc, 3)], w=[tk(cc, 3)], scale=-0.5),
                    lambda cc: P.tt("dve", T(cc, 2)[:], T(cc, 2)[:], T(cc, 3)[:], ALU.mult, r=[tk(cc, 2), tk(cc, 3)], w=[tk(cc, 2)]),
                    lambda cc: P.ts("dve", T(cc, 3)[:], T(cc, 1)[:], pcol("ka", cc), ALU.mult, pcol("omka", cc), ALU.add, r=[tk(cc, 1), "pc"], w=[tk(cc, 3)]),
                    lambda cc: P.tt("pool", T(cc, 4)[:], k_(cc), T(cc, 3)[:], ALU.mult, r=rk(cc) + [tk(cc, 3)], w=[tk(cc, 4)]),
                    lambda cc: P.tt("pool", T(cc, 5)[:], T(cc, 2)[:], T(cc, 1)[:], ALU.mult, r=[tk(cc, 2), tk(cc, 1)], w=[tk(cc, 5)]),
                    lambda cc: P.ts("dve", T(cc, 6)[:], T(cc, 0)[:], -EXPM05, ALU.mult, r=[tk(cc, 0)], w=[tk(cc, 6)]),
                    lambda cc: P.op("dve", lambda e, o=T(cc, 7), l=T(cc, 6): e.tensor_tensor_scan(out=o[:], data0=rmask[:], data1=l[:], initial=0.0, op0=ALU.mult, op1=ALU.add),
                                    r=["rmask", tk(cc, 6)], w=[tk(cc, 7)]),
                    lambda cc: P.act(epos[:, cc, :], T(cc, 7)[:], AF.Exp, r=[tk(cc, 7)], w=["epos"]),
                    lambda cc: P.act(T(cc, 8)[:], T(cc, 7)[:], AF.Exp, r=[tk(cc, 7)], w=[tk(cc, 8)], scale=-1.0),
                    lambda cc: P.tt("dve", T(cc, 6)[:], T(cc, 7)[:], T(cc, 6)[:], ALU.subtract, r=[tk(cc, 7), tk(cc, 6)], w=[tk(cc, 6)]),
                    lambda cc: P.act(T(cc, 9)[:], T(cc, 6)[:], AF.Exp, r=[tk(cc, 6)], w=[tk(cc, 9)]),
                    lambda cc: P.stt(AR[:, cc, :, 0, :], v4(T(cc, 2)[:]), -1.0, v4(T(cc, 9)[:]), ALU.mult, ALU.mult, r=[tk(cc, 2), tk(cc, 9)], w=["AR"]),
                    lambda cc: P.tt("pool", AR[:, cc, :, 1, :], v4(r_(cc)), v4(epos[:, cc, :]), ALU.mult, r=rk(cc) + ["epos"], w=["AR"]),
                    lambda cc: P.tt("dve", BK[:, cc, :, 0, :], v4(T(cc, 5)[:]), v4(T(cc, 8)[:]), ALU.mult, r=[tk(cc, 5), tk(cc, 8)], w=["BK"]),
                    lambda cc: P.tt("pool", BK[:, cc, :, 1, :], v4(T(cc, 4)[:]), v4(T(cc, 8)[:]), ALU.mult, r=[tk(cc, 4), tk(cc, 8)], w=["BK"]),
                    lambda cc: P.copy("act", vb[:, cc, :], v_(cc), r=rk(cc), w=["vb"]),
                    lambda cc: P.stt(T(cc, 3)[:], r_(cc), pcol("rk", cc), T(cc, 4)[:], ALU.mult, ALU.mult, r=rk(cc) + [tk(cc, 4), "pc"], w=[tk(cc, 3)]),
                    lambda cc: P.mm(ps[6][:, hb(cc)], bavg[:], T(cc, 3)[:], r=["bavg", tk(cc, 3)], w=["b6"]),
                    lambda cc: P.stt(bonus[:, cc, :], ps[6][:, hb(cc)], 64.0, v_(cc), ALU.mult, ALU.mult, r=["b6"] + rk(cc), w=["bonus"]),
                ]
                def elem_gen():
                    P.act(lin[0:64, :], xs[0:64, 12, :], AF.Tanh, r=[("xs", 12)], w=["lin"])
                    P.copy("act", lin[64:128, :], xs[64:128, 12, :], r=[("xs", 12)], w=["lin"])
                    P.act(sg[:], xs[:, 13, :], AF.Sigmoid, r=[("xs", 13)], w=["sg"])
                    yield
                    for grp in ((0, 1), (2, 3)):
                        for st_ in steps:
                            for cc in grp:
                                st_(cc)
                            yield

                ga, ge = attn_gen(), elem_gen()
                live = [ga, ge]
                while live:
                    for g_, n_ in ((ge, 6), (ga, 1)):
                        if g_ in live:
                            for _ in range(n_):
                                try:
                                    next(g_)
                                except StopIteration:
                                    live.remove(g_)
                                    break

                P.enabled = stage >= 3
                if blk + 1 < nblk:
                    A0(blk + 1)
                P.enabled = stage >= 7
                bkn = lambda n: f"b{n}"
                hs = [(cc, hh) for hh in range(2) for cc in range(4)]
                sl = lambda hh: slice(hh * 64, (hh + 1) * 64)
                v3 = lambda ap, t: ap.rearrange("p (a t) -> p a t", t=t)
                at_ = lambda c, cc, hh: AR[sl(hh), cc, c, 0, :]
                rt_ = lambda c, cc, hh: AR[sl(hh), cc, c, 1, :]
                ar_ = lambda c, cc, hh: AR[sl(hh), cc, c, :, :].rearrange("p a t -> p (a t)")
                bt_ = lambda c, cc, hh: BK[sl(hh), cc, c, 0, :]
                kt_ = lambda c, cc, hh: BK[sl(hh), cc, c, 1, :]
                vt_ = lambda c, cc, hh: vb[sl(hh), cc, c * C:(c + 1) * C]
                idq = lambda hh: identb[sl(hh), sl(hh)]
                f8 = lambda t, hh: t[sl(hh), :, :, :].rearrange("p a b t -> p (a b) t")
                for pair in range(NCH // 2):
                    cs = [(0, 2 * pair), (1, 2 * pair + 1)]
                    for G, lt, gk in ((GB, bt_, "GB"), (GK, kt_, "GK")):
                        for ci, c in cs:
                            for cc, hh in hs:
                                P.mm(ps[2 * ci + hh][sl(hh), cc * 128:(cc + 1) * 128], lt(c, cc, hh), ar_(c, cc, hh), r=["BK", "AR"], w=[bkn(2 * ci + hh)])
                        for ci, c in cs:
                            for hh in range(2):
                                P.tt("dve", G[sl(hh), ci, :, :], v3(ps[2 * ci + hh][sl(hh), :], 128), maskg[sl(hh), :].unsqueeze(1).to_broadcast([64, 4, 128]),
                                     ALU.mult, r=[bkn(2 * ci + hh), "maskg"], w=[gk])
                    for ci, c in cs:
                        for cc, hh in hs:
                            P.mm(ps[4 + hh][sl(hh), (ci * 4 + cc) * 64:(ci * 4 + cc + 1) * 64], at_(c, cc, hh), bt_(c, cc, hh), r=["BK", "AR"], w=[bkn(4 + hh)])
                    for hh in range(2):
                        P.tt("dve", f8(Ab[0], hh), v3(ps[4 + hh][sl(hh), :], 64), maskn[sl(hh), :].unsqueeze(1).to_broadcast([64, 8, 64]), ALU.mult,
                             r=[bkn(4 + hh), "maskn"], w=["Ab0"])
                    for ci, c in cs:
                        P.tt("pool", PTt[:, ci, :, :], GB[:, ci, :, 0:64], identd[:].unsqueeze(1).to_broadcast([128, 4, 64]), ALU.add, r=["GB", "identd"], w=["PTt"])
                    for bb, lt, key in ((0, bt_, "BK"), (2, kt_, "BK"), (4, vt_, "vb")):
                        for ci, c in cs:
                            for cc, hh in hs:
                                P.mm(ps[bb + hh][sl(hh), (ci * 4 + cc) * 64:(ci * 4 + cc + 1) * 64], lt(c, cc, hh), idq(hh), r=[key, "identb"], w=[bkn(bb + hh)])
                    for hh in range(2):
                        P.copy("act", f8(Btk, hh), v3(ps[0 + hh][sl(hh), :], 64), r=[bkn(0 + hh)], w=["Btk"])
                        P.copy("act", f8(Ktk, hh), v3(ps[2 + hh][sl(hh), :], 64), r=[bkn(2 + hh)], w=["Ktk"])
                        P.copy("act", f8(Vtk, hh), v3(ps[4 + hh][sl(hh), :], 64), r=[bkn(4 + hh)], w=["Vtk"])
                    A, B, ka, kb = Ab[0], GB[:, :, :, 0:64], "Ab0", "GB"
                    for lev in range(1, 6):
                        An, Bn = Ab[lev % 2], Bb[lev % 2]
                        kan, kbn = f"Ab{lev % 2}", f"Bb{lev % 2}"
                        for ci, c in cs:
                            for cc, hh in hs:
                                P.mm(ps[0 + hh][sl(hh), (ci * 4 + cc) * 64:(ci * 4 + cc + 1) * 64], B[sl(hh), ci, cc, :], A[sl(hh), ci, cc, :], r=[ka, kb], w=[bkn(0 + hh)])
                        if lev < 5:
                            for ci, c in cs:
                                for cc, hh in hs:
                                    P.mm(ps[2 + hh][sl(hh), (ci * 4 + cc) * 64:(ci * 4 + cc + 1) * 64], A[sl(hh), ci, cc, :], B[sl(hh), ci, cc, :], r=[ka, kb], w=[bkn(2 + hh)])
                        for hh in range(2):
                            P.copy("act", f8(An, hh), v3(ps[0 + hh][sl(hh), :], 64), r=[bkn(0 + hh)], w=[kan])
                        if lev < 5:
                            for hh in range(2):
                                P.copy("dve", f8(Bn, hh), v3(ps[2 + hh][sl(hh), :], 64), r=[bkn(2 + hh)], w=[kbn])
                        for ci, c in cs:
                            for cc, hh in hs:
                                P.mm(ps[4 + hh][sl(hh), (ci * 4 + cc) * 64:(ci * 4 + cc + 1) * 64], An[sl(hh), ci, cc, :], PTt[sl(hh), ci, cc, :], r=[kan, "PTt"], w=[bkn(4 + hh)])
                        for hh in range(2):
                            P.tt("dve", f8(PTt, hh), v3(ps[4 + hh][sl(hh), :], 64), f8(PTt, hh), ALU.add, r=[bkn(4 + hh), "PTt"], w=["PTt"])
                        A, B, ka, kb = An, Bn, kan, kbn
                    for ci, c in cs:
                        gc = blk * NCH + c
                        S0 = ST[gc % 2]
                        S1 = ST[(gc + 1) % 2]
                        k0, k1 = f"ST{gc % 2}", f"ST{(gc + 1) % 2}"
                        for hh in range(2):
                            P.copy("act", STb[sl(hh), :, :], S0[sl(hh), :, :], r=[k0], w=["STb"])
                        for cc, hh in hs:
                            o = ps[0 + hh][sl(hh), cc * 64:(cc + 1) * 64]
                            P.mm(o, GK[sl(hh), ci, cc, 0:64], Vtk[sl(hh), ci, cc, :], start=True, stop=False, r=["GK", "Vtk"], w=[bkn(0 + hh)])
                            P.mm(o, at_(c, cc, hh), STb[sl(hh), cc, :], start=False, stop=True, r=["AR", "STb"], w=[bkn(0 + hh)])
                        for hh in range(2):
                            P.copy("act", Zs[sl(hh), :, :], v3(ps[0 + hh][sl(hh), 0:256], 64), r=[bkn(0 + hh)], w=["Zs"])
                        for cc, hh in hs:
                            P.mm(ps[0 + hh][sl(hh), 256 + cc * 64:256 + (cc + 1) * 64], PTt[sl(hh), ci, cc, :], Zs[sl(hh), cc, :], r=["PTt", "Zs"], w=[bkn(0 + hh)])
                        for hh in range(2):
                            P.copy("dve", Utk[sl(hh), :, :], v3(ps[0 + hh][sl(hh), 256:512], 64), r=[bkn(0 + hh)], w=["Utk"])
                        for cc, hh in hs:
                            o = ps[2 + hh][sl(hh), cc * 64:(cc + 1) * 64]
                            P.mm(o, STb[sl(hh), cc, :], rt_(c, cc, hh), start=True, stop=False, r=["STb", "AR"], w=[bkn(2 + hh)])
                            P.mm(o, Utk[sl(hh), cc, :], GB[sl(hh), ci, cc, 64:128], start=False, stop=False, r=["Utk", "GB"], w=[bkn(2 + hh)])
                            P.mm(o, Vtk[sl(hh), ci, cc, :], GK[sl(hh), ci, cc, 64:128], start=False, stop=True, r=["Vtk", "GK"], w=[bkn(2 + hh)])
                        for cc, hh in hs:
                            o = ps[4 + hh][sl(hh), cc * 64:(cc + 1) * 64]
                            P.mm(o, Btk[sl(hh), ci, cc, :], Utk[sl(hh), cc, :], start=True, stop=False, r=["Btk", "Utk"], w=[bkn(4 + hh)])
                            P.mm(o, Ktk[sl(hh), ci, cc, :], Vtk[sl(hh), ci, cc, :], start=False, stop=True, r=["Ktk", "Vtk"], w=[bkn(4 + hh)])
                        for hh in range(2):
                            P.tt("dve", Zf[sl(hh), :, :], v3(ps[4 + hh][sl(hh), 0:256], 64), S0[sl(hh), :, :], ALU.add, r=[bkn(4 + hh), k0], w=["Zf"])
                            P.tt("pool", S1[sl(hh), :, :], Zf[sl(hh), :, :],
                                 epos[sl(hh), :, c * C + C - 1:c * C + C].to_broadcast([64, 4, 64]), ALU.mult, r=["Zf", "epos"], w=[k1])
                        for hh in range(2):
                            P.copy("act", yT[sl(hh), :, c * C:(c + 1) * C], v3(ps[2 + hh][sl(hh), 0:256], 64), r=[bkn(2 + hh)], w=["yT"])

                P.enabled = stage >= 8
                gb = lambda cc: (ps[cc], f"b{cc}")
                gsteps = [
                    lambda cc: P.mm(gb(cc)[0][:, 0:TB], bavg[:], yT[:, cc, :], r=["bavg", "yT"], w=[gb(cc)[1]]),
                    lambda cc: P.tt("dve", T(cc, 0)[:], yT[:, cc, :], gb(cc)[0][:, 0:TB], ALU.subtract, r=["yT", gb(cc)[1]], w=[tk(cc, 0)]),
                    lambda cc: P.act(T(cc, 1)[:], T(cc, 0)[:], AF.Square, r=[tk(cc, 0)], w=[tk(cc, 1)]),
                    lambda cc: P.mm(gb(cc)[0][:, TB:2 * TB], bavg[:], T(cc, 1)[:], r=["bavg", tk(cc, 1)], w=[gb(cc)[1]]),
                    lambda cc: P.act(T(cc, 1)[:], gb(cc)[0][:, TB:2 * TB], AF.Ln, r=[gb(cc)[1]], w=[tk(cc, 1)], bias=64e-5),
                    lambda cc: P.act(T(cc, 1)[:], T(cc, 1)[:], AF.Exp, r=[tk(cc, 1)], w=[tk(cc, 1)], scale=-0.5),
                    lambda cc: P.tt("dve", T(cc, 0)[:], T(cc, 0)[:], T(cc, 1)[:], ALU.mult, r=[tk(cc, 0), tk(cc, 1)], w=[tk(cc, 0)]),
                    lambda cc: P.ts("dve", T(cc, 0)[:], T(cc, 0)[:], pcol("lnw", cc), ALU.mult, pcol("lnb", cc), ALU.add, r=[tk(cc, 0), "pc"], w=[tk(cc, 0)]),
                    lambda cc: P.tt("pool", T(cc, 0)[:], T(cc, 0)[:], bonus[:, cc, :], ALU.add, r=[tk(cc, 0), "bonus"], w=[tk(cc, 0)]),
                    lambda cc: P.tt("pool", mixT[:, cc, :], T(cc, 0)[:], gT[:, cc, :], ALU.mult, r=[tk(cc, 0), "gT"], w=["mixT"]),
                ]
                for st_ in gsteps:
                    for cc in range(4):
                        st_(cc)
                P.enabled = stage >= 3
                for kc in range(8):
                    P.dma("sp", mixD[kc, :, blk * TB:(blk + 1) * TB], mixT[:, kc, :], r=["mixT"], w=["mixD"])

            P.enabled = stage >= 9
            Sf = ST[(nblk * NCH) % 2]
            kf = f"ST{(nblk * NCH) % 2}"
            for hh in range(2):
                for cc in range(4):
                    P.mm(ps[hh][hh * 64:(hh + 1) * 64, cc * 64:(cc + 1) * 64], Sf[hh * 64:(hh + 1) * 64, cc, :],
                         ident[hh * 64:(hh + 1) * 64, hh * 64:(hh + 1) * 64], r=[kf, "ident"], w=[f"b{hh}"])
                P.copy("act", Zf[hh * 64:(hh + 1) * 64, :, :], ps[hh][hh * 64:(hh + 1) * 64, 0:256].rearrange("p (a t) -> p a t", t=64), r=[f"b{hh}"], w=["Zf"])
                P.dma("sp", srp.rearrange("(c two) v k -> two v c k", two=2)[hh], Zf[hh * 64:(hh + 1) * 64, :, :], r=["Zf"])
            P.dma("sp", shp.rearrange("(c p) -> p c", p=128), pT[:, :, 0], r=[("pT", i) for i in range(14)] + ["pT"], allow_slow_non_contiguous=True)
            P.tr(ps[1][:, 0:128], KTf[:, 128:256], ident[:], r=["KTf", "ident"], w=["b1"])
            P.copy("act", tmp[0][:, 0:128], ps[1][:, 0:128], r=["b1"], w=["t0"])
            P.dma("sp", kbp, tmp[0][:, 0:128], r=["t0"])
            P.dma("sp", vbp, Vf[:], r=["Vf"])
            P.enabled = True
            P.barrier()
            P.emit()

        if do_samp:
            with ExitStack() as phS:
                def sS(name, shape, dt=F32):
                    return sb(name, shape, dt, phS)
                win = sS("winS", [128, 8, 2560], BF16)
                lwa = sS("lwaS", [128, 512], BF16)
                lwg = sS("lwgS", [128, 512], BF16)
                bavg = sS("bavgS", [128, 128])
                gbc = sS("gbcS", [128, D])
                x16 = sS("x16", [128, D])
                xn16 = sS("xn16", [128, D], BF16)
                ss16 = sS("ss16", [128, 4])
                hTs = sS("hTs", [128, 8, 16], BF16)
                pTs = sS("pTs", [128, 20, 16])
                shl = sS("shl", [16, 1792])
                prevT = sS("prevT", [128, 14, 16])
                xss = sS("xss", [128, 14, 16])
                tm = sS("tm", [16, 1792])
                TMv = sS("TMv", [16, 8, 512])
                lin16 = sS("lin16", [128, 16], BF16)
                sg16 = sS("sg16", [128, 16], BF16)
                q = [sS(f"q{i}", [128, 16]) for i in range(10)]
                Fv = sS("Fv", [128, 4, 8, 16])
                VS = sS("VS", [128, 8, 64])
                big = sS("big", [128, 8256])
                Sst = big[:, 0:4096].rearrange("p (v k) -> p v k", k=64)
                tmpA = big[:, 4096:8192].rearrange("p (v k) -> p v k", k=64)
                sm = [sS(f"sm{i}", [128, 64]) for i in range(5)]
                st4 = sS("st4", [128, 4])
                lnwbh = sS("lnwbh", [128, 64])
                lnbbh = sS("lnbbh", [128, 64])
                QTs = sS("QTs", [128, 4, 16])
                KTs = sS("KTs", [128, 16])
                Kc = sS("Kc", [64, 129, 64])
                Vc = sS("Vc", [64, 129, 64])
                prod = big[0:64, :].rearrange("p (j d) -> p j d", d=64)
                qd = sS("qd", [64, 64])
                sc = sS("sc", [64, 256])
                bdec = sS("bdec", [64, 129])
                od = sS("od", [64, 64])
                o2 = sS("o2", [64, 64])
                skd = sS("skd", [64, 4])

                P.dma("sp", gbc[:], g1bc_d, w=["gbc"])
                P.dma("sp", bavg[:], bavg_d, w=["bavg"])
                for kc in range(8):
                    P.dma("pool", win[:, kc, :], w_in[kc * 128:(kc + 1) * 128, :], w=[("win", kc)], max_dma_last_dim=4096)
                P.dma("pool", lwa[:], lwa_d, w=["lwa"])
                P.dma("pool", lwg[:], lwg_d, w=["lwg"])
                P.dma("sp", lnwbh[:], lnwbh_d, w=["lnwbh"])
                P.dma("sp", lnbbh[:], lnbbh_d, w=["lnbbh"])
                P.dma("sp", skd[:, 0:1], skd_d, w=["skd"])
                P.dma("sp", shl[:], shift_s, w=["shl"])
                P.dma("sp", Sst[:], srs_in.rearrange("b h v k -> (b h) v k"), w=["Sst"])
                for g in range(2):
                    for kvh in range(2):
                        rows = slice(g * 32 + kvh * 16, g * 32 + kvh * 16 + 16)
                        P.dma("pool", Kc[rows, 0:128, :], ck_in[:, :, kvh, :], w=["Kc"])
                        P.dma("pool", Vc[rows, 0:128, :], cv_in[:, :, kvh, :], w=["Vc"])

                P.dma("sp", q[0][0:32, 0:4], relb, w=["q0"])
                P.dma("sp", sc[0:32, 0:129], onehr_d, w=["sc"])
                P.mm(ps[0][0:4, 0:129], q[0][0:32, 0:4], sc[0:32, 0:129], r=["q0", "sc"], w=["b0"])
                P.copy("act", bdec[0:4, 0:129], ps[0][0:4, 0:129], r=["b0"], w=["bdec"])
                P.dma("sp", fdscr, bdec[0:4, 0:129], r=["bdec"], w=["fdscr"])

                P.dma("sp", x16[0:16, :], xsamp, w=["x16"])
                norm_rows(x16, xn16, gbc, "x16", "xn16", ss16, nrows=16)
                for kc in range(8):
                    P.tr(psT[:, kc * 128:kc * 128 + 16], xn16[0:16, kc * 128:(kc + 1) * 128], identb[0:16, 0:16], r=["xn16", "identb"], w=["bT"])
                P.copy("act", hTs[:, :, :], psT[:].rearrange("p (k t) -> p k t", t=128)[:, :, 0:16], r=["bT"], w=["hTs"])
                for oc in range(20):
                    if oc == 17:
                        continue
                    bank = 5 + oc % 2
                    for kc in range(8):
                        P.mm(ps[bank][:, 0:16], win[:, kc, oc * 128:(oc + 1) * 128], hTs[:, kc, :], start=(kc == 0), stop=(kc == 7),
                             r=[("win", kc), "hTs"], w=[f"b{bank}"])
                    P.copy("act", pTs[:, oc, :], ps[bank][:, 0:16], r=[f"b{bank}"], w=[("pTs", oc)])
                for kc in range(8):
                    P.mm(ps[4][0:16, 0:128], hTs[:, kc, :], win[:, kc, 2176:2304], start=(kc == 0), stop=(kc == 7), r=[("win", kc), "hTs"], w=["b4"])
                P.copy("act", x16[0:16, 0:128], ps[4][0:16, 0:128], r=["b4"], w=["x16v"])
                P.dma("sp", scrVn, x16[0:16, 0:128], r=["x16v"], w=["scrVn"])
                for g4 in range(4):
                    ocs = list(range(g4 * 4, min(14, g4 * 4 + 4)))
                    for i, oc in enumerate(ocs):
                        P.tr(ps[g4 % 2][0:16, i * 128:(i + 1) * 128], pTs[:, oc, :], ident[:], r=[("pTs", oc), "ident"], w=[f"b{g4 % 2}"])
                    n = len(ocs) * 128
                    P.copy("act", tm[0:16, g4 * 512:g4 * 512 + n], ps[g4 % 2][0:16, 0:n], r=[f"b{g4 % 2}"], w=["tm"])
                P.dma("sp", shs, tm[0:16, 0:1792], r=["tm"])
                for oc in range(14):
                    P.tr(ps[2][:, oc * 16:(oc + 1) * 16], shl[0:16, oc * 128:(oc + 1) * 128], ident[0:16, 0:16], r=["shl", "ident"], w=["b2"])
                P.copy("act", prevT[:].rearrange("p a t -> p (a t)"), ps[2][:, 0:224], r=["b2"], w=["prevT"])
                for oc in range(14):
                    P.ts("pool", q[0][:], prevT[:, oc, :], pcol("mu", oc), ALU.mult, r=["prevT", "pc"], w=["q0"])
                    P.stt(xss[:, oc, :], pTs[:, oc, :], pcol("omu", oc), q[0][:], ALU.mult, ALU.add, r=[("pTs", oc), "q0", "pc"], w=["xss"])
                P.act(lin16[0:64, :], xss[0:64, 12, :], AF.Tanh, r=["xss"], w=["lin16"])
                P.copy("act", lin16[64:128, :], xss[64:128, 12, :], r=["xss"], w=["lin16"])
                P.act(sg16[:], xss[:, 13, :], AF.Sigmoid, r=["xss"], w=["sg16"])
                for cc in range(4):
                    r_, k_, v_ = xss[:, cc, :], xss[:, 4 + cc, :], xss[:, 8 + cc, :]
                    P.mm(ps[0][:, 0:16], lwa[0:64, cc * 128:(cc + 1) * 128], lin16[0:64, :], r=["lwa", "lin16"], w=["b0"])
                    P.mm(ps[2][:, 0:16], lwa[64:128, cc * 128:(cc + 1) * 128], lin16[64:128, :], r=["lwa", "lin16"], w=["b2"])
                    P.mm(ps[1][:, 0:16], lwg[:, cc * 128:(cc + 1) * 128], sg16[:], r=["lwg", "sg16"], w=["b1"])
                    P.act(q[0][:], ps[0][:, 0:16], AF.Sigmoid, r=["b0", "pc"], w=["q0"], bias=pcol("w0", cc))
                    P.act(q[1][:], ps[2][:, 0:16], AF.Sigmoid, r=["b2", "pc"], w=["q1"], bias=pcol("a0", cc))
                    P.copy("act", Fv[:, cc, 6, :], ps[1][:, 0:16], r=["b1"], w=["Fv"])
                    P.ts("dve", q[2][:], k_, pcol("kk", cc), ALU.mult, r=["xss", "pc"], w=["q2"])
                    P.act(q[3][:], q[2][:], AF.Square, r=["q2"], w=["q3"])
                    P.mm(ps[1][:, 16:32], bavg[:], q[3][:], r=["bavg", "q3"], w=["b1"])
                    P.act(q[3][:], ps[1][:, 16:32], AF.Sqrt, r=["b1"], w=["q3"], scale=64.0)
                    P.ts("dve", q[3][:], q[3][:], 1e-12, ALU.max, r=["q3"], w=["q3"])
                    P.recip(q[3][:], q[3][:], r=["q3"], w=["q3"])
                    P.tt("dve", q[2][:], q[2][:], q[3][:], ALU.mult, r=["q2", "q3"], w=["q2"])
                    P.ts("dve", q[3][:], q[1][:], pcol("ka", cc), ALU.mult, pcol("omka", cc), ALU.add, r=["q1", "pc"], w=["q3"])
                    P.tt("dve", Fv[:, cc, 2, :], k_, q[3][:], ALU.mult, r=["xss", "q3"], w=["Fv"])
                    P.tt("dve", Fv[:, cc, 5, :], q[2][:], q[1][:], ALU.mult, r=["q2", "q1"], w=["Fv"])
                    P.ts("dve", Fv[:, cc, 4, :], q[2][:], -1.0, ALU.mult, r=["q2"], w=["Fv"])
                    P.act(Fv[:, cc, 1, :], q[0][:], AF.Exp, r=["q0"], w=["Fv"], scale=-EXPM05)
                    P.copy("act", Fv[:, cc, 0, :], r_, r=["xss"], w=["Fv"])
                    P.copy("act", Fv[:, cc, 3, :], v_, r=["xss"], w=["Fv"])
                    P.stt(q[4][:], r_, pcol("rk", cc), Fv[:, cc, 2, :], ALU.mult, ALU.mult, r=["xss", "Fv", "pc"], w=["q4"])
                    P.mm(ps[1][:, 32:48], bavg[:], q[4][:], r=["bavg", "q4"], w=["b1"])
                    P.stt(Fv[:, cc, 7, :], ps[1][:, 32:48], 64.0, v_, ALU.mult, ALU.mult, r=["b1", "xss"], w=["Fv"])
                for vec in range(8):
                    bank = vec % 2
                    for cc in range(4):
                        P.tr(ps[bank][0:16, cc * 128:(cc + 1) * 128], Fv[:, cc, vec, :], ident[:], r=["Fv", "ident"], w=[f"b{bank}"])
                    P.copy("act", TMv[:, vec, :], ps[bank][0:16, :], r=[f"b{bank}"], w=["TMv"])
                for vec in range(8):
                    P.dma("sp", scrV[:, :, vec, :], TMv[:, vec, :].rearrange("b (h c) -> b h c", c=64), r=["TMv"], w=["scrV"])
                P.dma("sp", VS[:], scrV.rearrange("b h v c -> (b h) v c"), r=["scrV"], w=["VS"])
                bc_v = lambda i: VS[:, i, :].unsqueeze(1).to_broadcast([128, 64, 64])
                bc_k = lambda ap: ap.unsqueeze(2).to_broadcast([128, 64, 64])
                P.tt("dve", tmpA[:], Sst[:], bc_v(4), ALU.mult, r=["Sst", "VS"], w=["tmpA"])
                P.op("dve", lambda e: e.tensor_reduce(out=sm[0][:], in_=tmpA[:], axis=AX.X, op=ALU.add), r=["tmpA"], w=["sm0"])
                P.tt("dve", Sst[:], Sst[:], bc_v(1), ALU.mult, r=["Sst", "VS", "tmpA"], w=["Sst"])
                P.tt("dve", tmpA[:], bc_k(sm[0][:]), bc_v(5), ALU.mult, r=["sm0", "VS"], w=["tmpA"])
                P.tt("pool", Sst[:], Sst[:], tmpA[:], ALU.add, r=["Sst", "tmpA"], w=["Sst"])
                P.tt("dve", tmpA[:], bc_k(VS[:, 3, :]), bc_v(2), ALU.mult, r=["VS", "Sst"], w=["tmpA"])
                P.tt("pool", Sst[:], Sst[:], tmpA[:], ALU.add, r=["Sst", "tmpA"], w=["Sst"])
                P.dma("sp", srs.rearrange("b h v k -> (b h) v k"), Sst[:], r=["Sst"])
                P.tt("dve", tmpA[:], Sst[:], bc_v(0), ALU.mult, r=["Sst", "VS"], w=["tmpA"])
                P.op("dve", lambda e: e.tensor_reduce(out=sm[1][:], in_=tmpA[:], axis=AX.X, op=ALU.add), r=["tmpA"], w=["sm1"])
                P.op("dve", lambda e: e.tensor_reduce(out=st4[:, 0:1], in_=sm[1][:], axis=AX.X, op=ALU.add), r=["sm1"], w=["st4"])
                P.ts("dve", st4[:, 0:1], st4[:, 0:1], 1.0 / 64, ALU.mult, r=["st4"], w=["st4"])
                P.ts("dve", sm[2][:], sm[1][:], st4[:, 0:1], ALU.subtract, r=["sm1", "st4"], w=["sm2"])
                P.tt("dve", sm[3][:], sm[2][:], sm[2][:], ALU.mult, r=["sm2"], w=["sm3"])
                P.op("dve", lambda e: e.tensor_reduce(out=st4[:, 1:2], in_=sm[3][:], axis=AX.X, op=ALU.add), r=["sm3"], w=["st4"])
                P.act(st4[:, 2:3], st4[:, 1:2], AF.Sqrt, r=["st4"], w=["st4"], scale=1.0 / 64, bias=64e-5)
                P.recip(st4[:, 2:3], st4[:, 2:3], r=["st4"], w=["st4"])
                P.ts("dve", sm[2][:], sm[2][:], st4[:, 2:3], ALU.mult, r=["sm2", "st4"], w=["sm2"])
                P.tt("dve", sm[2][:], sm[2][:], lnwbh[:], ALU.mult, r=["sm2", "lnwbh"], w=["sm2"])
                P.tt("dve", sm[2][:], sm[2][:], lnbbh[:], ALU.add, r=["sm2", "lnbbh"], w=["sm2"])
                P.tt("dve", sm[2][:], sm[2][:], VS[:, 7, :], ALU.add, r=["sm2", "VS"], w=["sm2"])
                P.tt("dve", sm[2][:], sm[2][:], VS[:, 6, :], ALU.mult, r=["sm2", "VS"], w=["sm2"])
                P.dma("sp", scrMix.rearrange("b (h c) -> (b h) c", c=64), sm[2][:], r=["sm2"], w=["scrMix"])

                def to_mix(scr, key, ncols, mixc):
                    P.dma("sp", tm[0:16, 0:ncols], scr, r=[key], w=["tm"])
                    for i in range(ncols // 128):
                        P.tr(ps[0][:, i * 16:(i + 1) * 16], tm[0:16, i * 128:(i + 1) * 128], ident[0:16, 0:16], r=["tm", "ident"], w=["b0"])
                    n = ncols // 128
                    P.copy("act", mixTs[:, mixc:mixc + n, :], ps[0][:, 0:n * 16].rearrange("p (a t) -> p a t", t=16), r=["b0"], w=["mixTs"])

                to_mix(scrMix, "scrMix", 512, 0)

                for j in range(2):
                    headnorm(pTs[:, 14 + j, :], 16, pcol("qns"), [(QTs[:, j, :], "QTs")], (q[8], q[9]), ("pTs", 14 + j))
                    headnorm(pTs[:, 18 + j, :], 16, pcol("qnm"), [(QTs[:, 2 + j, :], "QTs")], (q[8], q[9]), ("pTs", 18 + j))
                headnorm(pTs[:, 16, :], 16, pcol("kns"), [(KTs[:, :], "KTs")], (q[8], q[9]), ("pTs", 16))
                for j in range(4):
                    P.tr(ps[1][0:16, j * 128:(j + 1) * 128], QTs[:, j, :], ident[:], r=["QTs", "ident"], w=["b1"])
                P.copy("act", tm[0:16, 0:512], ps[1][0:16, 0:512], r=["b1"], w=["tm"])
                P.dma("sp", scrQ, tm[0:16, 0:512], r=["tm"], w=["scrQ"])
                P.tr(ps[1][0:16, 0:128], KTs[:, :], ident[:], r=["KTs", "ident"], w=["b1"])
                P.copy("act", x16[0:16, 128:256], ps[1][0:16, 0:128], r=["b1"], w=["x16k"])
                P.dma("sp", scrK, x16[0:16, 128:256], r=["x16k"], w=["scrK"])
                ck2 = ck_in.rearrange("b j h d -> b j (h d)")
                cv2 = cv_in.rearrange("b j h d -> b j (h d)")
                P.dma("sp", kbs[:, 0:127, :], ck2[:, 1:128, :])
                P.dma("sp", vbs[:, 0:127, :], cv2[:, 1:128, :])
                P.dma("sp", kbs[:, 127, :], scrK, r=["scrK"])
                P.dma("sp", vbs[:, 127, :], scrVn, r=["scrVn"])

                for g in range(2):
                    for kvh in range(2):
                        rows = slice(g * 32 + kvh * 16, g * 32 + kvh * 16 + 16)
                        P.dma("sp", Kc[rows, 128, :], scrK[:, kvh * 64:(kvh + 1) * 64], r=["scrK"], w=["Kc"])
                        P.dma("sp", Vc[rows, 128, :], scrVn[:, kvh * 64:(kvh + 1) * 64], r=["scrVn"], w=["Vc"])
                        P.dma("sp", qd[rows, :], scrQ[:, g * 128 + kvh * 64:g * 128 + (kvh + 1) * 64], r=["scrQ"], w=["qd"])
                        P.dma("sp", bdec[rows, :], bass.AP(fdscr.tensor, (2 * kvh + g) * 129, [[0, 16], [1, 129]]), r=["fdscr"], w=["bdec"])
                P.act(skd[:, 1:2], skd[:, 0:1], AF.Exp, r=["skd"], w=["skd"])
                P.tt("dve", prod[:], Kc[:], qd[:].unsqueeze(1).to_broadcast([64, 129, 64]), ALU.mult, r=["Kc", "qd"], w=["prod", "Sst", "tmpA"])
                P.op("dve", lambda e: e.tensor_reduce(out=sc[:, 0:129], in_=prod[:], axis=AX.X, op=ALU.add), r=["prod"], w=["sc"])
                P.stt(sc[:, 0:129], sc[:, 0:129], SCALE, bdec[:], ALU.mult, ALU.add, r=["sc", "bdec"], w=["sc"])
                P.act(sc[:, 0:129], sc[:, 0:129], AF.Exp, r=["sc"], w=["sc", "skd"], accum_out=skd[:, 2:3])
                P.tt("dve", skd[:, 2:3], skd[:, 2:3], skd[:, 1:2], ALU.add, r=["skd"], w=["skd"])
                P.recip(skd[:, 2:3], skd[:, 2:3], r=["skd"], w=["skd"])
                P.tt("dve", prod[:], Vc[:], sc[:, 0:129].unsqueeze(2).to_broadcast([64, 129, 64]), ALU.mult, r=["Vc", "sc"], w=["prod"])
                P.op("dve", lambda e: e.tensor_reduce(out=od[:], in_=prod[:].rearrange("p j d -> p d j"), axis=AX.X, op=ALU.add), r=["prod"], w=["od"])
                P.ts("dve", od[:], od[:], skd[:, 2:3], ALU.mult, r=["od", "skd"], w=["od"])
                for g in range(2):
                    for kvh in range(2):
                        rows = slice(g * 32 + kvh * 16, g * 32 + kvh * 16 + 16)
                        P.dma("sp", scrO[:, g * 128 + kvh * 64:g * 128 + (kvh + 1) * 64], od[rows, :], r=["od"], w=["scrO"])
                to_mix(scrO, "scrO", 256, 4)

                for mh in range(4):
                    rows = slice(mh * 16, (mh + 1) * 16)
                    P.dma("sp", qd[rows, :], scrQ[:, 256 + mh * 64:256 + (mh + 1) * 64], r=["scrQ"], w=["qd"])
                for half in range(2):
                    for mh in range(4):
                        rows = slice(mh * 16, (mh + 1) * 16)
                        P.dma("pool", Kc[rows, 0:128, :], cmk_in[:, half * 128:(half + 1) * 128, mh, :], w=["Kc"])
                        P.dma("pool", Vc[rows, 0:128, :], cmv_in[:, half * 128:(half + 1) * 128, mh, :], w=["Vc"])
                    P.tt("dve", prod[:, 0:128, :], Kc[:, 0:128, :], qd[:].unsqueeze(1).to_broadcast([64, 128, 64]), ALU.mult, r=["Kc", "qd"], w=["prod"])
                    P.op("dve", lambda e, h=half: e.tensor_reduce(out=sc[:, h * 128:(h + 1) * 128], in_=prod[:, 0:128, :], axis=AX.X, op=ALU.add),
                         r=["prod"], w=["sc"])
                    P.act(sc[:, half * 128:(half + 1) * 128], sc[:, half * 128:(half + 1) * 128], AF.Exp, r=["sc"], w=["sc", "skd"],
                          scale=SCALE, accum_out=skd[:, 2 + half:3 + half])
                    P.tt("dve", prod[:, 0:128, :], Vc[:, 0:128, :], sc[:, half * 128:(half + 1) * 128].unsqueeze(2).to_broadcast([64, 128, 64]), ALU.mult,
                         r=["Vc", "sc"], w=["prod"])
                    P.op("dve", lambda e, o=(od if half == 0 else o2): e.tensor_reduce(out=o[:], in_=prod[:, 0:128, :].rearrange("p j d -> p d j"), axis=AX.X, op=ALU.add),
                         r=["prod"], w=["od" if half == 0 else "o2"])
                P.tt("dve", od[:], od[:], o2[:], ALU.add, r=["od", "o2"], w=["od"])
                P.tt("dve", skd[:, 2:3], skd[:, 2:3], skd[:, 3:4], ALU.add, r=["skd"], w=["skd"])
                P.recip(skd[:, 2:3], skd[:, 2:3], r=["skd"], w=["skd"])
                P.ts("dve", od[:], od[:], skd[:, 2:3], ALU.mult, r=["od", "skd"], w=["od"])
                for mh in range(4):
                    rows = slice(mh * 16, (mh + 1) * 16)
                    P.dma("sp", scrOm[:, mh * 64:(mh + 1) * 64], od[rows, :], r=["od"], w=["scrOm"])
                to_mix(scrOm, "scrOm", 256, 6)
                P.barrier()
                P.emit()

        P.enabled = True
        if do_ffn:
            with ExitStack() as ph2:
                def s2(name, shape, dt=F32):
                    return sb(name, shape, dt, ph2)
                wout = s2("wout", [128, 8, D], BF16)
                wff1 = s2("wff1", [128, 8, 4096], BF16)
                wff2 = s2("wff2", [128, 32, D], BF16)
                g2 = s2("g2", [128, D])
                mixb = [s2(f"mixb{i}", [128, 8, TB], BF16) for i in range(2)]
                xb2 = [s2(f"x2_{i}", [128, D]) for i in range(4)]
                xn2 = s2("xn2", [128, D], BF16)
                h2T = s2("h2T", [128, 8, TB], BF16)
                aT = s2("aT", [128, 32, TB], BF16)
                rl = [s2(f"rl{i}", [128, TB]) for i in range(2)]
                ss2 = s2("ss2", [128, 4])
                P.dma("sp", g2[:], g2bc_d, w=["gbc"])
                for kc in range(8):
                    P.dma("pool", wout[:, kc, :], w_out[kc * 128:(kc + 1) * 128, :], w=[("wout", kc)])
                for kc in range(8):
                    for q in range(2):
                        P.dma("pool", wff1[:, kc, q * 2048:(q + 1) * 2048], w_ff1[kc * 128:(kc + 1) * 128, q * 2048:(q + 1) * 2048], w=[("wff1", kc)])
                for kc in range(32):
                    P.dma("pool", wff2[:, kc, :], w_ff2[kc * 128:(kc + 1) * 128, :], w=[("wff2", kc)])
                def ffn_block(mb, km, tiles, W):
                    for (xin, yout, nr, c0, ti) in tiles:
                        kx = ("x2", ti)
                        P.dma("pool", xb2[ti][0:nr, :], xin, w=[kx])
                        for half in range(2):
                            for kc in range(8):
                                P.mm(ps[half][0:nr, :], mb[:, kc, c0:c0 + nr], wout[:, kc, half * 512:(half + 1) * 512],
                                     start=(kc == 0), stop=(kc == 7), r=[km, ("wout", kc)], w=[f"b{half}"])
                            P.tt("dve", xb2[ti][0:nr, half * 512:(half + 1) * 512], ps[half][0:nr, :], xb2[ti][0:nr, half * 512:(half + 1) * 512], ALU.add,
                                 r=[f"b{half}", kx], w=[kx])
                        norm_rows(xb2[ti], xn2, g2, kx, "xn2", ss2, nrows=nr)
                        for kc in range(8):
                            P.tr(psT[:, kc * 128:kc * 128 + nr], xn2[0:nr, kc * 128:(kc + 1) * 128], identb[0:nr, 0:nr], r=["xn2", "identb"], w=["bT"])
                        P.copy("act", h2T[:, :, c0:c0 + nr], psT[:].rearrange("p (k t) -> p k t", t=128)[:, :, 0:nr], r=["bT"], w=["h2T"])
                    for oc in range(32):
                        bank = 2 + oc % 2
                        for kc in range(8):
                            P.mm(ps[bank][:, 0:W], wff1[:, kc, oc * 128:(oc + 1) * 128], h2T[:, kc, 0:W], start=(kc == 0), stop=(kc == 7),
                                 r=[("wff1", kc), "h2T"], w=[f"b{bank}"])
                        P.act(rl[oc % 2][:, 0:W], ps[bank][:, 0:W], AF.Relu, r=[f"b{bank}"], w=[f"rl{oc % 2}"])
                        P.tt("pool", aT[:, oc, 0:W], rl[oc % 2][:, 0:W], rl[oc % 2][:, 0:W], ALU.mult, r=[f"rl{oc % 2}"], w=[("aT", oc)])
                    for (xin, yout, nr, c0, ti) in tiles:
                        kx = ("x2", ti)
                        for half in range(2):
                            bank = 4 + half
                            for kc in range(32):
                                P.mm(ps[bank][0:nr, :], aT[:, kc, c0:c0 + nr], wff2[:, kc, half * 512:(half + 1) * 512],
                                     start=(kc == 0), stop=(kc == 31), r=[("aT", kc), ("wff2", kc)], w=[f"b{bank}"])
                            P.tt("dve", xb2[ti][0:nr, half * 512:(half + 1) * 512], ps[bank][0:nr, :], xb2[ti][0:nr, half * 512:(half + 1) * 512], ALU.add,
                                 r=[f"b{bank}", kx], w=[kx])
                        P.dma("sp", yout, xb2[ti][0:nr, :], r=[kx])

                for blk in range(nblk):
                    mb = mixb[blk % 2]
                    km = f"mixb{blk % 2}"
                    for kc in range(8):
                        P.dma("pool", mb[:, kc, :], mixD[kc, :, blk * TB:(blk + 1) * TB], r=["mixD"], w=[km])
                    tiles = [(xseq[(blk * 2 + ti) * 128:(blk * 2 + ti + 1) * 128, :], y_p[(blk * 2 + ti) * 128:(blk * 2 + ti + 1) * 128, :], 128, ti * 128,
                              (blk * 2 + ti) % 4) for ti in range(2)]
                    ffn_block(mb, km, tiles, TB)
                if do_samp:
                    ffn_block(mixTs, "mixTs", [(xsamp, y_s, 16, 0, 0)], 16)
                P.emit()
        else:
            P.emit()
    return nc


def _consts():
    ident = np.eye(128, dtype=np.float32)
    maskg = np.zeros((128, 128), np.float32)
    for r in range(128):
        j = r % 64
        for c in range(128):
            t = c % 64
            maskg[r, c] = 1.0 if ((c < 64 and j < t) or (c >= 64 and j <= t)) else 0.0
    maskn = np.tile(np.tril(np.ones((64, 64), np.float32), -1), (2, 1))
    identd = np.tile(np.eye(64, dtype=np.float32), (2, 1))
    bavg = np.zeros((128, 128), np.float32)
    bavg[:64, :64] = 1.0 / 64
    bavg[64:, 64:] = 1.0 / 64
    ones = np.ones((128, 64), np.float32)
    rmask = np.ones((128, TB), np.float32)
    rmask[:, ::C] = 0.0
    bk = t5_bucket_np(np.arange(129))
    oneh = np.zeros((32, 129), np.float32)
    oneh[bk, np.arange(129)] = 1.0
    bkr = t5_bucket_np(128 - np.arange(129))
    onehr = np.zeros((32, 129), np.float32)
    onehr[bkr, np.arange(129)] = 1.0
    return dict(onehr=onehr, identd=identd, aident=np.ascontiguousarray(ident[::-1]), ident=ident, maskg=maskg, maskn=maskn, bavg=bavg, ones=ones, rmask=rmask, oneh=oneh)


def _prep_shared(inp):
    f = lambda a: np.ascontiguousarray(np.asarray(a, dtype=np.float32))
    sh = _consts()
    w_in = f(inp["w_in"][0])
    perm = np.arange(2560)
    for j in range(2):
        for hh in range(2):
            dst = 1792 + j * 128 + hh * 64
            src = 1792 + (2 * hh + j) * 64
            perm[dst:dst + 64] = np.arange(src, src + 64)
    sh["w_in"] = np.ascontiguousarray(w_in[:, perm])
    w_out = f(inp["w_out"][0])
    rperm = np.arange(1024)
    for j in range(2):
        for hh in range(2):
            dst = 512 + j * 128 + hh * 64
            src = 512 + (2 * hh + j) * 64
            rperm[dst:dst + 64] = np.arange(src, src + 64)
    sh["w_out"] = np.ascontiguousarray(w_out[rperm, :])
    sh["w_ff1"] = f(inp["w_ff1"][0])
    sh["w_ff2"] = f(inp["w_ff2"][0])
    sh["w_mem"] = f(inp["w_mem_kv"][0])
    sh["lwa"] = np.ascontiguousarray(np.concatenate([f(inp["w_up_w"][0]), f(inp["w_up_a"][0])], 0))
    sh["lwg"] = f(inp["w_up_g"][0])
    sh["relb"] = f(inp["rel_bias"])
    pc = np.zeros((128, NPC), np.float32)
    col = lambda v, n: np.asarray(v, np.float32).reshape(n, 128).T
    pc[:, PC["mu"]:PC["mu"] + 14] = col(inp["mu_shift"][0], 14)
    pc[:, PC["w0"]:PC["w0"] + 4] = col(inp["w0"][0], 4)
    pc[:, PC["a0"]:PC["a0"] + 4] = col(inp["a0"][0], 4)
    pc[:, PC["kk"]:PC["kk"] + 4] = col(inp["k_k"][0], 4)
    pc[:, PC["ka"]:PC["ka"] + 4] = col(inp["k_a"][0], 4)
    pc[:, PC["rk"]:PC["rk"] + 4] = col(np.asarray(inp["r_k"][0]).reshape(512), 4)
    pc[:, PC["lnw"]:PC["lnw"] + 4] = col(inp["lnx_w"][0], 4)
    pc[:, PC["lnb"]:PC["lnb"] + 4] = col(inp["lnx_b"][0], 4)
    t2 = lambda v: np.tile(np.asarray(v, np.float32).reshape(64), 2)
    pc[:, PC["qns"]] = t2(inp["q_norm_swa"][0])
    pc[:, PC["kns"]] = t2(inp["k_norm_swa"][0])
    pc[:, PC["qnm"]] = t2(inp["q_norm_mem"][0])
    pc[:, PC["knm"]] = t2(inp["k_norm_mem"][0])
    sk = np.asarray(inp["sinks"][0], np.float32)
    for j in range(2):
        for hh in range(2):
            pc[hh * 64:(hh + 1) * 64, PC["sink"] + j] = sk[2 * hh + j]
    sh["pc"] = pc
    bc = lambda v: np.ascontiguousarray(np.broadcast_to(np.asarray(v, np.float32).reshape(1, -1), (128, np.asarray(v).size)))
    sh["g1bc"] = bc(inp["norm1_g"][0])
    sh["g2bc"] = bc(inp["norm2_g"][0])
    sh["gmbc"] = bc(inp["mem_norm_g"][0])
    sh["knmbc"] = bc(np.tile(np.asarray(inp["k_norm_mem"][0], np.float32), 4))
    sh["lnwbh"] = np.ascontiguousarray(np.tile(np.asarray(inp["lnx_w"][0], np.float32).reshape(8, 64), (16, 1)))
    sh["lnbbh"] = np.ascontiguousarray(np.tile(np.asarray(inp["lnx_b"][0], np.float32).reshape(8, 64), (16, 1)))
    skd = np.zeros((64, 1), np.float32)
    for g in range(2):
        for kvh in range(2):
            skd[g * 32 + kvh * 16:g * 32 + kvh * 16 + 16, 0] = sk[2 * kvh + g]
    sh["skd"] = skd
    return sh


def _core_inputs(inp, sh, c):
    f = lambda a: np.ascontiguousarray(np.asarray(a, dtype=np.float32))
    m = dict(sh)
    m["xseq"] = f(inp["x_prompt"][c % 2])
    m["mem"] = f(inp["mem_prompt"][c % 2])
    b = slice(16 * c, 16 * c + 16)
    m["xsamp"] = f(inp["x_sample"][b, 0, :])
    m["srs_in"] = f(inp["state_rwkv"][0, b])
    m["shift_s"] = f(inp["state_shift"][0, b])
    m["ck_in"] = f(inp["cache_swa_k"][0, b])
    m["cv_in"] = f(inp["cache_swa_v"][0, b])
    m["cmk_in"] = f(inp["cache_mem_k"][0, b])
    m["cmv_in"] = f(inp["cache_mem_v"][0, b])
    return m


def kernel(**inp):
    nc = build()
    sh = _prep_shared(inp)
    in_maps = [_core_inputs(inp, sh, c) for c in range(8)]
    res = run_bass_kernel_spmd(nc, in_maps, core_ids=list(range(8)))
    R = res.results
    yp = np.stack([R[b]["y_p"] for b in range(2)])
    srp = np.stack([R[b]["srp"] for b in range(2)])[None]
    shp = np.stack([R[b]["shp"] for b in range(2)])[None]
    kbp = np.stack([R[b]["kbp"].reshape(128, 2, 64) for b in range(2)])[None]
    vbp = np.stack([R[b]["vbp"].reshape(128, 2, 64) for b in range(2)])[None]
    mkp = np.stack([R[b]["mkp"].reshape(256, 4, 64) for b in range(2)])[None]
    mvp = np.stack([R[b]["mvp"].reshape(256, 4, 64) for b in range(2)])[None]
    cat = lambda k: np.concatenate([R[c][k] for c in range(8)], 0)
    ys = cat("y_s").reshape(128, 1, 1024)
    srs = cat("srs")[None]
    shs = cat("shs")[None]
    kbs = cat("kbs").reshape(128, 128, 2, 64)[None]
    vbs = cat("vbs").reshape(128, 128, 2, 64)[None]
    return tuple(np.ascontiguousarray(a, dtype=np.float32) for a in (yp, ys, srp, shp, kbp, vbp, mkp, mvp, srs, shs, kbs, vbs))
```

```python
import numpy as np
import concourse.bass as bass
import concourse.mybir as mybir
from concourse.bass_utils import run_bass_kernel_spmd

F32 = mybir.dt.float32
BF16 = mybir.dt.bfloat16
AF = mybir.ActivationFunctionType
ALU = mybir.AluOpType
AX = mybir.AxisListType


class Prog:
    CE = ("pe", "act", "dve", "pool")

    def __init__(self, nc, n_dma_sems=12):
        self.nc = nc
        self.sem = {e: nc.alloc_semaphore(name=f"s_{e}") for e in self.CE}
        self.cnt = {e: 0 for e in self.CE}
        self.dsem = {q: [nc.alloc_semaphore(name=f"d_{q}{i}") for i in range(n_dma_sems)]
                     for q in ("sp", "pool", "act")}
        self.dval = {q: [0] * n_dma_sems for q in ("sp", "pool", "act")}
        self.dnext = {q: 0 for q in ("sp", "pool", "act")}
        self.semobj = {}
        for e in self.CE:
            self.semobj[("c", e)] = self.sem[e]
        for q in self.dsem:
            for i, s in enumerate(self.dsem[q]):
                self.semobj[("d", q, i)] = s
        self.ops = {e: [] for e in ("pe", "act", "dve", "pool", "sp")}
        self.lastw = {}
        self.readers = {}
        self.seen = {e: {} for e in self.ops}
        self.nops = 0

    def _deps(self, eng, reads, writes):
        need = {}

        def add(tok):
            sid, val = tok
            if need.get(sid, 0) < val:
                need[sid] = val

        for k in reads:
            t = self.lastw.get(k)
            if t is not None:
                add(t)
        for k in writes:
            t = self.lastw.get(k)
            if t is not None:
                add(t)
            for sid, val in self.readers.get(k, {}).items():
                add((sid, val))
        out = []
        seen = self.seen[eng]
        for sid, val in need.items():
            if eng == "pe" and sid == ("c", "pe"):
                continue
            if seen.get(sid, 0) >= val:
                continue
            seen[sid] = val
            out.append((sid, val))
        return out

    def _commit(self, tok, reads, writes):
        sid, val = tok
        for k in writes:
            self.lastw[k] = tok
            self.readers[k] = {}
        for k in reads:
            r = self.readers.setdefault(k, {})
            if r.get(sid, 0) < val:
                r[sid] = val

    enabled = True

    def op(self, eng, fn, r=(), w=()):
        if not self.enabled:
            return
        if eng in ("act", "dve"):
            banks = [k for k in list(r) + list(w) if isinstance(k, str) and len(k) == 2 and k[0] == "b"]
            if banks:
                w = list(w) + [("psrd", k) for k in banks]
        waits = self._deps(eng, r, w)
        self.cnt[eng] += 1
        tok = (("c", eng), self.cnt[eng])
        self.ops[eng].append((waits, fn, ("c", eng), 1))
        self._commit(tok, r, w)
        self.nops += 1

    def dma(self, q, out, in_, r=(), w=(), **kw):
        if not self.enabled:
            return
        i = self.dnext[q]
        n = len(self.dsem[q])
        self.dnext[q] = (i + 1) % n
        sid = ("d", q, i)
        waits = self._deps(q, r, w)
        prev = self.dval[q][i]
        if prev > 0 and self.seen[q].get(sid, 0) < prev:
            self.seen[q][sid] = prev
            waits.append((sid, prev))
        self.dval[q][i] += 16
        tok = (sid, self.dval[q][i])
        self.ops[q].append((waits, (lambda e, o=out, s=in_, kw=kw: e.dma_start(out=o, in_=s, **kw)), sid, 16))
        self._commit(tok, r, w)
        self.nops += 1

    def barrier(self):
        toks = []
        for q in self.dsem:
            for i in range(len(self.dsem[q])):
                if self.dval[q][i] > 0:
                    toks.append((("d", q, i), self.dval[q][i]))
        for e in self.CE:
            if self.cnt[e] > 0:
                toks.append((("c", e), self.cnt[e]))
        for e in self.ops:
            w = [(sid, v) for sid, v in toks if self.seen[e].get(sid, 0) < v]
            for sid, v in w:
                self.seen[e][sid] = v
            self.ops[e].append((w, None, None, 0))

    def emit(self):
        nc = self.nc
        fin = []
        for q in self.dsem:
            for i in range(len(self.dsem[q])):
                if self.dval[q][i] > 0:
                    fin.append((("d", q, i), self.dval[q][i]))
        for e in self.CE:
            if self.cnt[e] > 0:
                fin.append((("c", e), self.cnt[e]))
        ops = self.ops
        semobj = self.semobj

        def run(eng, lst, final=None):
            for waits, fn, sid, inc in lst:
                for ws, wv in waits:
                    eng.wait_ge(semobj[ws], wv)
                if fn is not None:
                    if inc == 0:
                        fn(eng)
                    else:
                        fn(eng).then_inc(semobj[sid], inc)
            if final:
                for ws, wv in final:
                    eng.wait_ge(semobj[ws], wv)

        with nc.Block() as block:
            @block.sync
            def _(e):
                run(e, ops["sp"], fin)

            @block.tensor
            def _(e):
                run(e, ops["pe"])

            @block.scalar
            def _(e):
                run(e, ops["act"])

            @block.vector
            def _(e):
                run(e, ops["dve"])

            @block.gpsimd
            def _(e):
                run(e, ops["pool"])
        for e in self.ops:
            self.ops[e] = []

    pe_mode = None

    def _pe_mode(self, st):
        if not self.enabled:
            return
        ru = lambda n: 32 if n <= 32 else (64 if n <= 64 else 128)
        k = st.partition_size()
        m = st.free_size()
        mode = (ru(k), ru(m))
        if self.pe_mode is not None and mode != self.pe_mode:
            self.ops["pe"].append(([], (lambda e: e.drain()), None, 0))
        self.pe_mode = mode

    def mm(self, out, lhsT, rhs, start=True, stop=True, r=(), w=()):
        self._pe_mode(lhsT)
        self.op("pe", lambda e: e.matmul(out, lhsT, rhs, start=start, stop=stop), r, w)

    def tr(self, out, in_, ident, r=(), w=()):
        self._pe_mode(in_)
        self.op("pe", lambda e: e.transpose(out, in_, ident), r, w)

    def act(self, out, in_, func, r=(), w=(), **kw):
        self.op("act", lambda e: e.activation(out=out, in_=in_, func=func, **kw), r, w)

    def tt(self, eng, out, in0, in1, op, r=(), w=()):
        self.op(eng, lambda e: e.tensor_tensor(out=out, in0=in0, in1=in1, op=op), r, w)

    def ts(self, eng, out, in0, s1, op0, s2=None, op1=None, r=(), w=()):
        if op1 is None:
            self.op(eng, lambda e: e.tensor_scalar(out=out, in0=in0, scalar1=s1, scalar2=None, op0=op0), r, w)
        else:
            self.op(eng, lambda e: e.tensor_scalar(out=out, in0=in0, scalar1=s1, scalar2=s2, op0=op0, op1=op1), r, w)

    def stt(self, out, in0, scalar, in1, op0, op1, r=(), w=()):
        self.op("dve", lambda e: e.scalar_tensor_tensor(out=out, in0=in0, scalar=scalar, in1=in1, op0=op0, op1=op1), r, w)

    def copy(self, eng, out, in_, r=(), w=()):
        if eng == "act":
            self.op("act", lambda e: e.copy(out=out, in_=in_), r, w)
        else:
            self.op(eng, lambda e: e.tensor_scalar(out=out, in0=in_, scalar1=1.0, scalar2=None, op0=ALU.mult), r, w)

    def recip(self, out, in_, r=(), w=()):
        self.op("dve", lambda e: e.reciprocal(out=out, in_=in_), r, w)


L = 8192
D = 1024
TB = 256
C = 64
NCH = TB // C
EXPM05 = float(np.exp(-0.5))
SCALE = 0.125

PC = {}
_o = 0
for _n, _w in [("mu", 14), ("omu", 14), ("w0", 4), ("a0", 4), ("kk", 4), ("ka", 4), ("omka", 4), ("rk", 4),
               ("lnw", 4), ("lnb", 4), ("qns", 1), ("kns", 1), ("qnm", 1), ("knm", 1), ("sink", 2), ("esk", 2)]:
    PC[_n] = _o
    _o += _w
NPC = _o


def t5_bucket_np(dist):
    dist = np.asarray(dist)
    d = np.maximum(dist, 1).astype(np.float32)
    large = 16 + (np.log(d / np.float32(16)) / np.float32(np.log(128 / 16)) * np.float32(16)).astype(np.int32)
    large = np.minimum(large, 31)
    return np.where(dist < 16, dist, large)


def build(nblk=L // TB, do_ffn=True, stage=9, do_samp=True):
    nc = bass.Bass("TRN2", target_bir_lowering=False)
    NTOK = nblk * TB

    def din(name, shape, dt=F32):
        return nc.dram_tensor(name, list(shape), dt, kind="ExternalInput").ap()

    def dout(name, shape, dt=F32):
        return nc.dram_tensor(name, list(shape), dt, kind="ExternalOutput").ap()

    xseq = din("xseq", [L, D])
    mem = din("mem", [256, D])
    w_in = din("w_in", [D, 2560])
    w_out = din("w_out", [D, D])
    w_ff1 = din("w_ff1", [D, 4096])
    w_ff2 = din("w_ff2", [4096, D])
    w_mem = din("w_mem", [D, 512])
    lwa_d = din("lwa", [128, 512])
    lwg_d = din("lwg", [128, 512])
    relb = din("relb", [32, 4])
    pc_d = din("pc", [128, NPC])
    g1bc_d = din("g1bc", [128, D])
    g2bc_d = din("g2bc", [128, D])
    gmbc_d = din("gmbc", [128, D])
    ident_d = din("ident", [128, 128])
    maskg_d = din("maskg", [128, 128])
    maskn_d = din("maskn", [128, 64])
    identd_d = din("identd", [128, 64])
    bavg_d = din("bavg", [128, 128])
    ones_d = din("ones", [128, 64])
    rmask_d = din("rmask", [128, TB])
    oneh_d = din("oneh", [32, 129])
    knmbc_d = din("knmbc", [128, 256])
    aident_d = din("aident", [128, 128])

    xsamp = din("xsamp", [16, D])
    srs_in = din("srs_in", [16, 8, 64, 64])
    shift_s = din("shift_s", [16, 1792])
    ck_in = din("ck_in", [16, 128, 2, 64])
    cv_in = din("cv_in", [16, 128, 2, 64])
    cmk_in = din("cmk_in", [16, 256, 4, 64])
    cmv_in = din("cmv_in", [16, 256, 4, 64])
    lnwbh_d = din("lnwbh", [128, 64])
    lnbbh_d = din("lnbbh", [128, 64])
    skd_d = din("skd", [64, 1])
    onehr_d = din("onehr", [32, 129])
    y_s = dout("y_s", [16, D])
    srs = dout("srs", [16, 8, 64, 64])
    shs = dout("shs", [16, 1792])
    kbs = dout("kbs", [16, 128, 128])
    vbs = dout("vbs", [16, 128, 128])
    y_p = dout("y_p", [L, D])
    srp = dout("srp", [8, 64, 64])
    shp = dout("shp", [1792])
    kbp = dout("kbp", [128, 128])
    vbp = dout("vbp", [128, 128])
    mkp = dout("mkp", [256, 256])
    mvp = dout("mvp", [256, 256])

    mixD = nc.dram_tensor("mixD", [8, 128, L], BF16, kind="Internal").ap()
    fscr = nc.dram_tensor("fscr", [4, 512], F32, kind="Internal").ap()
    fdscr = nc.dram_tensor("fdscr", [4, 129], F32, kind="Internal").ap()
    scrV = nc.dram_tensor("scrV", [16, 8, 8, 64], F32, kind="Internal").ap()
    scrMix = nc.dram_tensor("scrMix", [16, 512], F32, kind="Internal").ap()
    scrQ = nc.dram_tensor("scrQ", [16, 512], F32, kind="Internal").ap()
    scrK = nc.dram_tensor("scrK", [16, 128], F32, kind="Internal").ap()
    scrVn = nc.dram_tensor("scrVn", [16, 128], F32, kind="Internal").ap()
    scrO = nc.dram_tensor("scrO", [16, 256], F32, kind="Internal").ap()
    scrOm = nc.dram_tensor("scrOm", [16, 256], F32, kind="Internal").ap()

    P = Prog(nc)

    def pcol(name, j=0):
        return pc[:, PC[name] + j:PC[name] + j + 1]

    from contextlib import ExitStack
    with ExitStack() as top:
        def sb(name, shape, dt=F32, st=top):
            return st.enter_context(nc.sbuf_tensor("s_" + name, list(shape), dt))

        ident = sb("ident", [128, 128])
        identb = sb("identb", [128, 128], BF16)
        pc = sb("pc", [128, NPC])
        mixTs = sb("mixTs", [128, 8, 16], BF16)
        ps = [top.enter_context(nc.psum_tensor(f"b{i}", [128, 512], F32)) for i in range(7)]
        psT = top.enter_context(nc.psum_tensor("bT", [128, 1024], BF16))
        P.dma("sp", ident[:], ident_d, w=["ident"])
        P.dma("pool", identb[:], ident_d, w=["identb"])
        P.dma("sp", pc[:], pc_d, w=["pc"])
        P.ts("dve", pc[:, PC["omu"]:PC["omu"] + 14], pc[:, PC["mu"]:PC["mu"] + 14], -1.0, ALU.mult, 1.0, ALU.add, r=["pc"], w=["pc"])
        P.ts("dve", pc[:, PC["omka"]:PC["omka"] + 4], pc[:, PC["ka"]:PC["ka"] + 4], -1.0, ALU.mult, 1.0, ALU.add, r=["pc"], w=["pc"])
        P.act(pc[:, PC["esk"]:PC["esk"] + 2], pc[:, PC["sink"]:PC["sink"] + 2], AF.Exp, r=["pc"], w=["pc"])

        def norm_rows(x_t, xn_t, gbc, key_x, key_xn, ss, nrows=128):
            P.act(xn_t[0:nrows, :], x_t[0:nrows, :], AF.Square, r=[key_x], w=[key_xn, "ss"], accum_out=ss[0:nrows, 0:1])
            P.act(ss[0:nrows, 1:2], ss[0:nrows, 0:1], AF.Sqrt, r=["ss"], w=["ss"], scale=1.0 / D, bias=1e-6)
            P.recip(ss[0:nrows, 2:3], ss[0:nrows, 1:2], r=["ss"], w=["ss"])
            P.stt(xn_t[0:nrows, :], x_t[0:nrows, :], ss[0:nrows, 2:3], gbc[0:nrows, :], ALU.mult, ALU.mult, r=[key_x, "ss", "gbc"], w=[key_xn])

        def headnorm(psx, ncols, gcol, outs, tmps, keyp, eps=1e-6):
            sq, sd = tmps
            P.act(sq[:, 0:ncols], psx, AF.Square, r=[keyp], w=["hn_sq"])
            P.mm(ps[3][:, 0:ncols], bavg[:], sq[:, 0:ncols], r=["bavg", "hn_sq"], w=["b3"])
            P.act(sd[:, 0:ncols], ps[3][:, 0:ncols], AF.Ln, r=["b3"], w=["hn_sd"], bias=eps)
            P.act(sd[:, 0:ncols], sd[:, 0:ncols], AF.Exp, r=["hn_sd"], w=["hn_sd"], scale=-0.5)
            for o, k in outs:
                P.stt(o, psx, gcol, sd[:, 0:ncols], ALU.mult, ALU.mult, r=[keyp, "hn_sd", "pc"], w=[k])

        with ExitStack() as ph1:
            def s1(name, shape, dt=F32):
                return sb(name, shape, dt, ph1)

            win = s1("win", [128, 8, 2560], BF16)
            lwa = s1("lwab", [128, 512], BF16)
            lwg = s1("lwgb", [128, 512], BF16)
            maskg = s1("maskg", [128, 128])
            maskn = s1("maskn", [128, 64])
            bavg = s1("bavg", [128, 128])
            ones = s1("onesb", [128, 64], BF16)
            rmask = s1("rmask", [128, TB])
            gbc = s1("gbc", [128, D])
            biasT = s1("biasT", [128, 1024])
            ss = s1("ss", [128, 4])
            xbuf = [s1(f"xb{i}", [128, D]) for i in range(2)]
            xn = s1("xn", [128, D], BF16)
            hTb = [s1("hT0", [128, 8, TB], BF16), s1("hT1", [128, 8, TB], BF16)]
            pT = s1("pT", [128, 14, TB + 1])
            xs = s1("xs", [128, 14, TB])
            lin = s1("lin", [128, TB], BF16)
            sg = s1("sg", [128, TB], BF16)
            tmp = [s1(f"t{i}", [128, TB]) for i in range(12)]
            hnA = s1("hnA", [128, TB])
            hnB = s1("hnB", [128, TB])
            gT = s1("gT", [128, 4, TB])
            bonus = s1("bonus", [128, 4, TB])
            epos = s1("epos", [128, 4, TB])
            AR = s1("AR", [128, 4, NCH, 2, C], BF16)
            BK = s1("BK", [128, 4, NCH, 2, C], BF16)
            GB = s1("GB", [128, 2, 4, 128], BF16)
            GK = s1("GK", [128, 2, 4, 128], BF16)
            Ab = [s1(f"Ab{i}", [128, 2, 4, 64], BF16) for i in range(2)]
            Bb = [s1(f"Bb{i}", [128, 2, 4, 64], BF16) for i in range(2)]
            PTt = s1("PTt", [128, 2, 4, 64], BF16)
            Btk = s1("Btk", [128, 2, 4, 64], BF16)
            Ktk = s1("Ktk", [128, 2, 4, 64], BF16)
            Vtk = s1("Vtk", [128, 2, 4, 64], BF16)
            Utk = s1("Utk", [128, 4, 64], BF16)
            Zs = s1("Zs", [128, 4, 64], BF16)
            Zf = s1("Zf", [128, 4, 64])
            STb = s1("STb", [128, 4, 64], BF16)
            vb = s1("vb", [128, 4, TB], BF16)
            identd = s1("identd", [128, 64])
            yT = s1("yT", [128, 4, TB])
            ST = [s1(f"ST{i}", [128, 4, 64]) for i in range(2)]
            QT = s1("QT", [128, 4, TB], BF16)
            KTr = s1("KTr", [128, 3, 128], BF16)
            KTf = s1("KTf", [128, TB])
            Vs = s1("Vs", [128, 3, 128], BF16)
            Vf = s1("Vf", [128, 128])
            tmpS = s1("tmpS", [128, 1024])
            PTb = s1("PTb", [128, 1024], BF16)
            den = s1("den", [128, 256])
            mkT = s1("mkT", [128, 2, 256], BF16)
            mv = s1("mv", [128, 2, 256], BF16)
            mixT = s1("mixT", [128, 8, TB], BF16)

            for kc in range(8):
                P.dma("pool", win[:, kc, :], w_in[kc * 128:(kc + 1) * 128, :], w=[("win", kc)], max_dma_last_dim=4096)
            P.dma("pool", lwa[:], lwa_d, w=["lwa"])
            P.dma("pool", lwg[:], lwg_d, w=["lwg"])
            P.dma("pool", ones[:], ones_d, w=["ones"])
            P.dma("sp", maskg[:], maskg_d, w=["maskg"])
            P.dma("sp", maskn[:], maskn_d, w=["maskn"])
            P.dma("sp", identd[:], identd_d, w=["identd"])
            P.dma("sp", bavg[:], bavg_d, w=["bavg"])
            P.dma("sp", rmask[:], rmask_d, w=["rmask"])

            P.enabled = stage >= 1
            fpad = tmp[0]
            relb_s = tmp[1]
            oneh_s = tmp[2]
            P.dma("sp", relb_s[0:32, 0:4], relb, w=["t1"])
            P.dma("sp", oneh_s[0:32, 0:129], oneh_d, w=["t2"])
            P.op("dve", lambda e: e.memset(tmpS[0:4, 0:512], -30000.0), w=["tmpS"])
            P.mm(ps[0][0:4, 0:129], relb_s[0:32, 0:4], oneh_s[0:32, 0:129], r=["t1", "t2"], w=["b0"])
            P.copy("dve", tmpS[0:4, 128:257], ps[0][0:4, 0:129], r=["b0"], w=["tmpS"])
            P.dma("sp", fscr, tmpS[0:4, 0:512], r=["tmpS"], w=["fscr"])
            aid = tmp[3]
            P.dma("sp", aid[:, 0:128], aident_d, w=["t3"])
            for j in range(2):
                for hh in range(2):
                    qh = 2 * hh + j
                    for blk in range(2):
                        idx = (hh * 2 + j) * 2 + blk
                        base = 256 if blk == 0 else 128
                        src = bass.AP(fscr.tensor, qh * 512 + base - 127, [[1, 128], [1, 128]])
                        P.dma("sp", PTb[:, 0:256].bitcast(F32)[:, 0:128] if False else tmp[4 + (idx % 2)][:, 0:128], src, r=["fscr"], w=[f"t{4 + idx % 2}"])
                        bank = idx // 4
                        P.mm(ps[bank][:, (idx % 4) * 128:(idx % 4 + 1) * 128], aid[:, 0:128], tmp[4 + (idx % 2)][:, 0:128], r=["t3", f"t{4 + idx % 2}"], w=[f"b{bank}"])
            for bank in range(2):
                P.copy("act", biasT[:, bank * 512:(bank + 1) * 512], ps[bank][:, :], r=[f"b{bank}"], w=["biasT"])
            P.enabled = stage >= 2
            P.dma("sp", gbc[:], gmbc_d, w=["gbc"])
            phM = ExitStack()
            knmbc = sb("knmbc", [128, 256], F32, phM)
            P.dma("sp", knmbc[:], knmbc_d, w=["knmbc"])
            mhT = sb("mhT", [128, 8, 256], BF16, phM)
            wmem = sb("wmemb", [128, 8, 512], BF16, phM)
            for kc in range(8):
                P.dma("pool", wmem[:, kc, :], w_mem[kc * 128:(kc + 1) * 128, :], w=[("wmem", kc)])
            for mt in range(2):
                xt = xbuf[mt]
                P.dma("sp", xt[:], mem[mt * 128:(mt + 1) * 128, :], w=[("xb", mt)])
                norm_rows(xt, xn, gbc, ("xb", mt), "xn", ss)
                for kc in range(8):
                    P.tr(psT[:, kc * 128:(kc + 1) * 128], xn[:, kc * 128:(kc + 1) * 128], identb[:], r=["xn", "identb"], w=["bT"])
                P.mm(ps[6][:, 0:16], identb[:, 0:128], identb[:, 0:16], r=["identb"], w=["bT", "b6"])
                P.copy("act", mhT[:, :, mt * 128:(mt + 1) * 128], psT[:].rearrange("p (k t) -> p k t", t=128), r=["bT"], w=["mhT"])
            P.enabled = stage >= 2.2
            for j in range(2):
                for kc in range(8):
                    P.mm(ps[0][:, 0:256], wmem[:, kc, j * 128:(j + 1) * 128], mhT[:, kc, :], start=(kc == 0), stop=(kc == 7),
                         r=[("wmem", kc), "mhT"], w=["b0"])
                headnorm(ps[0][:, 0:256], 256, pcol("knm"), [(mkT[:, j, :], "mkT")], (hnA, hnB), "b0")
            P.enabled = stage >= 2.4
            for mt in range(2):
                for kc in range(8):
                    P.mm(ps[1][:, 0:512], mhT[:, kc, mt * 128:(mt + 1) * 128], wmem[:, kc, :], start=(kc == 0), stop=(kc == 7),
                         r=[("wmem", kc), "mhT"], w=["b1"])
                P.copy("act", mv[:, mt, :], ps[1][:, 256:512], r=["b1"], w=["mv"])
                P.copy("act", tmp[5][:, 0:256], ps[1][:, 256:512], r=["b1"], w=["t5"])
                P.dma("sp", mvp[mt * 128:(mt + 1) * 128, :], tmp[5][:, 0:256], r=["t5"])
                P.act(tmp[6][:, 0:256], ps[1][:, 0:256], AF.Square, r=["b1"], w=["t6"])
                P.op("dve", lambda e: e.tensor_reduce(out=ss[:, 0:4], in_=tmp[6][:, 0:256].rearrange("p (h d) -> p h d", d=64), axis=AX.X, op=ALU.add),
                     r=["t6"], w=["ss"])
                P.act(ss[:, 0:4], ss[:, 0:4], AF.Sqrt, r=["ss"], w=["ss"], scale=1.0 / 64, bias=1e-6)
                P.recip(ss[:, 0:4], ss[:, 0:4], r=["ss"], w=["ss"])
                P.tt("dve", tmp[7][:, 0:256].rearrange("p (h d) -> p h d", d=64), ps[1][:, 0:256].rearrange("p (h d) -> p h d", d=64),
                     ss[:, 0:4].unsqueeze(2).to_broadcast([128, 4, 64]), ALU.mult, r=["b1", "ss"], w=["t7"])
                P.tt("dve", tmp[7][:, 0:256], tmp[7][:, 0:256], knmbc[:], ALU.mult, r=["t7", "knmbc"], w=["t7"])
                P.dma("sp", mkp[mt * 128:(mt + 1) * 128, :], tmp[7][:, 0:256], r=["t7"])

            P.enabled = True
            P.barrier()
            P.emit()
            phM.close()
            tmp = tmp + [s1(f"t{i}", [128, TB]) for i in range(12, 40)]
            P.enabled = stage >= 3
            P.dma("sp", gbc[:], g1bc_d, w=["gbc"])
            P.op("dve", lambda e: e.memset(pT[:, :, 0:1], 0.0), w=["pT"])
            P.op("dve", lambda e: e.memset(ST[0][:], 0.0), w=["ST0"])
            P.op("pool", lambda e: e.memset(KTr[:], 0.0), w=[("KTr", 0), ("KTr", 1), ("KTr", 2)])
            P.op("pool", lambda e: e.memset(Vs[:], 0.0), w=[("Vs", 0), ("Vs", 1), ("Vs", 2)])
            xs_flat = xs[:].rearrange("p a t -> p (a t)")

            def A0(b):
                hT_ = hTb[b % 2]
                for ti in range(2):
                    t = b * 2 + ti
                    xt = xbuf[ti]
                    P.dma("sp", xt[:], xseq[t * 128:(t + 1) * 128, :], w=[("xb", ti)])
                    norm_rows(xt, xn, gbc, ("xb", ti), "xn", ss)
                    for kc in range(8):
                        P.tr(psT[:, kc * 128:(kc + 1) * 128], xn[:, kc * 128:(kc + 1) * 128], identb[:], r=["xn", "identb"], w=["bT"])
                    P.copy("act", hT_[:, :, ti * 128:(ti + 1) * 128], psT[:].rearrange("p (k t) -> p k t", t=128), r=["bT"], w=[f"hT{b % 2}"])

            A0(0)
            for blk in range(nblk):
                hT = hTb[blk % 2]
                kh = f"hT{blk % 2}"
                def proj(oc, bank):
                    for kc in range(8):
                        P.mm(ps[bank][:, 0:TB], win[:, kc, oc * 128:(oc + 1) * 128], hT[:, kc, :], start=(kc == 0), stop=(kc == 7),
                             r=[("win", kc), kh], w=[f"b{bank}"])

                for oc in range(14):
                    bank = 5 + (oc % 2)
                    proj(oc, bank)
                    P.copy("act", pT[:, oc, 1:TB + 1], ps[bank][:, 0:TB], r=[f"b{bank}"], w=[("pT", oc)])
                    P.act(tmp[0][:], pT[:, oc, 0:TB], AF.Copy, r=[("pT", oc), "pT", "pc"], w=["t0"], scale=pcol("mu", oc))
                    P.stt(xs[:, oc, :], pT[:, oc, 1:TB + 1], pcol("omu", oc), tmp[0][:], ALU.mult, ALU.add, r=[("pT", oc), "t0", "pc"], w=[("xs", oc)])
                    P.copy("pool", pT[:, oc, 0:1], pT[:, oc, TB:TB + 1], r=[("pT", oc)], w=[("pT", oc)])

                P.enabled = stage >= 4
                for j in range(2):
                    proj(14 + j, 5)
                    headnorm(ps[5][:, 0:TB], TB, pcol("qns"), [(QT[:, j, :], "QT")], (hnA, hnB), "b5")
                proj(16, 5)
                for ti in range(2):
                    t = blk * 2 + ti
                    headnorm(ps[5][:, ti * 128:(ti + 1) * 128], 128, pcol("kns"),
                             [(KTr[:, t % 3, :], ("KTr", t % 3)), (KTf[:, ti * 128:(ti + 1) * 128], "KTf")], (hnA, hnB), "b5")
                for j in range(2):
                    proj(18 + j, 5)
                    headnorm(ps[5][:, 0:TB], TB, pcol("qnm"), [(QT[:, 2 + j, :], "QT")], (hnA, hnB), "b5")
                for ti in range(2):
                    t = blk * 2 + ti
                    for kc in range(8):
                        P.mm(ps[6][:, 0:128], hT[:, kc, ti * 128:(ti + 1) * 128], win[:, kc, 2176:2304], start=(kc == 0), stop=(kc == 7),
                             r=[("win", kc), kh], w=["b6"])
                    P.copy("act", Vs[:, t % 3, :], ps[6][:, 0:128], r=["b6"], w=[("Vs", t % 3)])
                    if blk == nblk - 1 and ti == 1:
                        P.copy("dve", Vf[:], ps[6][:, 0:128], r=["b6"], w=["Vf"])

                def attn_tail(o_lhs, nblkk, has_sink, mixc, tcols, okeys):
                    for j in range(2):
                        for hh in range(2):
                            for b in range(nblkk[0], 2):
                                idx = (hh * 2 + j) * 2 + b
                                P.mm(ps[2][hh * 64:(hh + 1) * 64, j * 128:(j + 1) * 128], o_lhs(j, hh, b), PTb[:, idx * 128:(idx + 1) * 128],
                                     start=(b == nblkk[0]), stop=(b == 1), r=["PTb"] + okeys, w=["b2"])
                            for b in range(nblkk[0], 2):
                                idx = (hh * 2 + j) * 2 + b
                                P.mm(ps[2][hh * 64:(hh + 1) * 64, 256 + j * 128:256 + (j + 1) * 128], ones[:, 0:64], PTb[:, idx * 128:(idx + 1) * 128],
                                     start=(b == nblkk[0]), stop=(b == 1), r=["PTb", "ones"], w=["b2"])
                    if stage < 5.3:
                        return
                    if has_sink:
                        P.tt("dve", den[:].rearrange("p (j q) -> p j q", q=128), ps[2][:, 256:512].rearrange("p (j q) -> p j q", q=128),
                             pc[:, PC["esk"]:PC["esk"] + 2].unsqueeze(2).to_broadcast([128, 2, 128]), ALU.add, r=["b2", "pc"], w=["den"])
                        P.act(den[:], den[:], AF.Ln, r=["den"], w=["den"])
                    else:
                        P.act(den[:], ps[2][:, 256:512], AF.Ln, r=["b2"], w=["den"])
                    P.act(den[:], den[:], AF.Exp, r=["den"], w=["den"], scale=-1.0)
                    P.tt("dve", mixT[:, mixc:mixc + 2, tcols], ps[2][:, 0:256].rearrange("p (j q) -> p j q", q=128),
                         den[:].rearrange("p (j q) -> p j q", q=128), ALU.mult, r=["b2", "den"], w=["mixT"])

                def attn_gen():
                    for ti in range(2):
                        t = blk * 2 + ti
                        tcols = slice(ti * 128, (ti + 1) * 128)
                        b0 = 1 if t == 0 else 0
                        for j in range(2):
                            for hh in range(2):
                                for b in range(0, 2):
                                    idx = (hh * 2 + j) * 2 + b
                                    slot = (t - 1 + b) % 3
                                    bank = idx // 4
                                    P.mm(ps[bank][:, (idx % 4) * 128:(idx % 4 + 1) * 128], KTr[hh * 64:(hh + 1) * 64, slot, :],
                                         QT[hh * 64:(hh + 1) * 64, j, tcols], r=[("KTr", slot), "QT"], w=[f"b{bank}"])
                        for bank in range(2):
                            P.stt(tmpS[:, bank * 512:(bank + 1) * 512], ps[bank][:, :], SCALE, biasT[:, bank * 512:(bank + 1) * 512], ALU.mult, ALU.add,
                                  r=[f"b{bank}", "biasT"], w=["tmpS"])
                        yield
                        P.act(PTb[:], tmpS[:], AF.Exp, r=["tmpS"], w=["PTb"])
                        yield
                        attn_tail(lambda j, hh, b: Vs[:, (t - 1 + b) % 3, hh * 64:(hh + 1) * 64], (b0,), True, 4, tcols, [("Vs", (t - 1) % 3), ("Vs", t % 3)])
                        yield
                        for j in range(2):
                            for hh in range(2):
                                for b in range(2):
                                    idx = (hh * 2 + j) * 2 + b
                                    bank = idx // 4
                                    P.mm(ps[bank][:, (idx % 4) * 128:(idx % 4 + 1) * 128], mkT[hh * 64:(hh + 1) * 64, j, b * 128:(b + 1) * 128],
                                         QT[hh * 64:(hh + 1) * 64, 2 + j, tcols], r=["mkT", "QT"], w=[f"b{bank}"])
                        for bank in range(2):
                            P.act(PTb[:, bank * 512:(bank + 1) * 512], ps[bank][:, :], AF.Exp, r=[f"b{bank}"], w=["PTb"], scale=SCALE)
                        yield
                        attn_tail(lambda j, hh, b: mv[:, b, (2 * j + hh) * 64:(2 * j + hh + 1) * 64], (0,), False, 6, tcols, ["mv"])
                        yield


                T = lambda cc, k: tmp[cc * 10 + k]
                tk = lambda cc, k: f"t{cc * 10 + k}"
                r_ = lambda cc: xs[:, cc, :]
                k_ = lambda cc: xs[:, 4 + cc, :]
                v_ = lambda cc: xs[:, 8 + cc, :]
                rk = lambda cc: [("xs", cc), ("xs", 4 + cc), ("xs", 8 + cc)]
                hb = lambda cc: slice((cc % 2) * TB, (cc % 2 + 1) * TB)
                pW = lambda cc: ps[4][:, hb(cc)]
                pA = lambda cc: ps[5][:, hb(cc)]
                v4 = lambda a: a.rearrange("p (c t) -> p c t", t=C)
                steps = [
                    lambda cc: P.mm(pW(cc), lwa[0:64, cc * 128:(cc + 1) * 128], lin[0:64, :], r=["lwa", "lin"], w=["b4"]),
                    lambda cc: P.mm(pA(cc), lwa[64:128, cc * 128:(cc + 1) * 128], lin[64:128, :], r=["lwa", "lin"], w=["b5"]),
                    lambda cc: P.mm(ps[6][:, hb(cc)], lwg[:, cc * 128:(cc + 1) * 128], sg[:], r=["lwg", "sg"], w=["b6"]),
                    lambda cc: P.act(T(cc, 0)[:], pW(cc), AF.Sigmoid, r=["b4", "pc"], w=[tk(cc, 0)], bias=pcol("w0", cc)),
                    lambda cc: P.act(T(cc, 1)[:], pA(cc), AF.Sigmoid, r=["b5", "pc"], w=[tk(cc, 1)], bias=pcol("a0", cc)),
                    lambda cc: P.copy("act", gT[:, cc, :], ps[6][:, hb(cc)], r=["b6"], w=["gT"]),
                    lambda cc: P.ts("dve", T(cc, 2)[:], k_(cc), pcol("kk", cc), ALU.mult, r=rk(cc) + ["pc"], w=[tk(cc, 2)]),
                    lambda cc: P.act(T(cc, 3)[:], T(cc, 2)[:], AF.Square, r=[tk(cc, 2)], w=[tk(cc, 3)]),
                    lambda cc: P.mm(ps[3][:, hb(cc)], bavg[:], T(cc, 3)[:], r=["bavg", tk(cc, 3)], w=["b3"]),
                    lambda cc: P.act(T(cc, 3)[:], ps[3][:, hb(cc)], AF.Ln, r=["b3"], w=[tk(cc, 3)], scale=64.0, bias=1e-18),
                    lambda cc: P.act(T(cc, 3)[:], T(cc, 3)[:], AF.Exp, r=[tk(cc, 3)], w=[tk(cc, 3)], scale=-0.5),
                    lambda cc: P.tt("dve", T(cc, 2)[:], T(cc, 2)[:], T(cc, 3)[:], ALU.mult, r=[tk(cc, 2), tk(cc, 3)], w=[tk(cc, 2)]),
                    lambda cc: P.ts("dve", T(cc, 3)[:], T(cc, 1)[:], pcol("ka", cc), ALU.mult, pcol("omka", cc), ALU.add, r=[tk(cc, 1), "pc"], w=[tk(cc, 3)]),
                    lambda cc: P.tt("pool", T(cc, 4)[:], k_(cc), T(cc, 3)[:], ALU.mult, r=rk(cc) + [tk(cc, 3)], w=[tk(cc, 4)]),
                    lambda cc: P.tt("pool", T(cc, 5)[:], T(cc, 2)[:], T(cc, 1)[:], ALU.mult, r=[tk(cc, 2), tk(cc, 1)], w=[tk(cc, 5)]),
                    lambda cc: P.ts("dve", T(cc, 6)[:], T(cc, 0)[:], -EXPM05, ALU.mult, r=[tk(cc, 0)], w=[tk(cc, 6)]),
                    lambda cc: P.op("dve", lambda e, o=T(cc, 7), l=T(cc, 6): e.tensor_tensor_scan(out=o[:], data0=rmask[:], data1=l[:], initial=0.0, op0=ALU.mult, op1=ALU.add),
                                    r=["rmask", tk(cc, 6)], w=[tk(cc, 7)]),
                    lambda cc: P.act(epos[:, cc, :], T(cc, 7)[:], AF.Exp, r=[tk(cc, 7)], w=["epos"]),
                    lambda cc: P.act(T(cc, 8)[:], T(cc, 7)[:], AF.Exp, r=[tk(cc, 7)], w=[tk(cc, 8)], scale=-1.0),
                    lambda cc: P.tt("dve", T(cc, 6)[:], T(cc, 7)[:], T(cc, 6)[:], ALU.subtract, r=[tk(cc, 7), tk(cc, 6)], w=[tk(cc, 6)]),
                    lambda cc: P.act(T(cc, 9)[:], T(cc, 6)[:], AF.Exp, r=[tk(cc, 6)], w=[tk(cc, 9)]),
                    lambda cc: P.stt(AR[:, cc, :, 0, :], v4(T(cc, 2)[:]), -1.0, v4(T(cc, 9)[:]), ALU.mult, ALU.mult, r=[tk(cc, 2), tk(cc, 9)], w=["AR"]),
                    lambda cc: P.tt("pool", AR[:, cc, :, 1, :], v4(r_(cc)), v4(epos[:, cc, :]), ALU.mult, r=rk(cc) + ["epos"], w=["AR"]),
                    lambda cc: P.tt("dve", BK[:, cc, :, 0, :], v4(T(cc, 5)[:]), v4(T(cc, 8)[:]), ALU.mult, r=[tk(cc, 5), tk(cc, 8)], w=["BK"]),
                    lambda cc: P.tt("pool", BK[:, cc, :, 1, :], v4(T(cc, 4)[:]), v4(T(cc, 8)[:]), ALU.mult, r=[tk(cc, 4), tk(cc, 8)], w=["BK"]),
                    lambda cc: P.copy("act", vb[:, cc, :], v_(cc), r=rk(cc), w=["vb"]),
                    lambda cc: P.stt(T(cc, 3)[:], r_(cc), pcol("rk", cc), T(cc, 4)[:], ALU.mult, ALU.mult, r=rk(cc) + [tk(cc, 4), "pc"], w=[tk(cc, 3)]),
                    lambda cc: P.mm(ps[6][:, hb(cc)], bavg[:], T(cc, 3)[:], r=["bavg", tk(cc, 3)], w=["b6"]),
                    lambda cc: P.stt(bonus[:, cc, :], ps[6][:, hb(cc)], 64.0, v_(cc), ALU.mult, ALU.mult, r=["b6"] + rk(cc), w=["bonus"]),
                ]
                def elem_gen():
                    P.act(lin[0:64, :], xs[0:64, 12, :], AF.Tanh, r=[("xs", 12)], w=["lin"])
                    P.copy("act", lin[64:128, :], xs[64:128, 12, :], r=[("xs", 12)], w=["lin"])
                    P.act(sg[:], xs[:, 13, :], AF.Sigmoid, r=[("xs", 13)], w=["sg"])
                    yield
                    for grp in ((0, 1), (2, 3)):
                        for st_ in steps:
                            for cc in grp:
                                st_(cc)
                            yield

                ga, ge = attn_gen(), elem_gen()
                live = [ga, ge]
                while live:
                    for g_, n_ in ((ge, 5), (ga, 1)):
                        if g_ in live:
                            for _ in range(n_):
                                try:
                                    next(g_)
                                except StopIteration:
                                    live.remove(g_)
                                    break

                P.enabled = stage >= 3
                if blk + 1 < nblk:
                    A0(blk + 1)
                P.enabled = stage >= 7
                bkn = lambda n: f"b{n}"
                hs = [(cc, hh) for hh in range(2) for cc in range(4)]
                sl = lambda hh: slice(hh * 64, (hh + 1) * 64)
                v3 = lambda ap, t: ap.rearrange("p (a t) -> p a t", t=t)
                at_ = lambda c, cc, hh: AR[sl(hh), cc, c, 0, :]
                rt_ = lambda c, cc, hh: AR[sl(hh), cc, c, 1, :]
                ar_ = lambda c, cc, hh: AR[sl(hh), cc, c, :, :].rearrange("p a t -> p (a t)")
                bt_ = lambda c, cc, hh: BK[sl(hh), cc, c, 0, :]
                kt_ = lambda c, cc, hh: BK[sl(hh), cc, c, 1, :]
                vt_ = lambda c, cc, hh: vb[sl(hh), cc, c * C:(c + 1) * C]
                idq = lambda hh: identb[sl(hh), sl(hh)]
                f8 = lambda t, hh: t[sl(hh), :, :, :].rearrange("p a b t -> p (a b) t")
                for pair in range(NCH // 2):
                    cs = [(0, 2 * pair), (1, 2 * pair + 1)]
                    for G, lt, gk in ((GB, bt_, "GB"), (GK, kt_, "GK")):
                        for ci, c in cs:
                            for cc, hh in hs:
                                P.mm(ps[2 * ci + hh][sl(hh), cc * 128:(cc + 1) * 128], lt(c, cc, hh), ar_(c, cc, hh), r=["BK", "AR"], w=[bkn(2 * ci + hh)])
                        for ci, c in cs:
                            for hh in range(2):
                                P.tt("dve", G[sl(hh), ci, :, :], v3(ps[2 * ci + hh][sl(hh), :], 128), maskg[sl(hh), :].unsqueeze(1).to_broadcast([64, 4, 128]),
                                     ALU.mult, r=[bkn(2 * ci + hh), "maskg"], w=[gk])
                    for ci, c in cs:
                        for cc, hh in hs:
                            P.mm(ps[4 + hh][sl(hh), (ci * 4 + cc) * 64:(ci * 4 + cc + 1) * 64], at_(c, cc, hh), bt_(c, cc, hh), r=["BK", "AR"], w=[bkn(4 + hh)])
                    for hh in range(2):
                        P.tt("dve", f8(Ab[0], hh), v3(ps[4 + hh][sl(hh), :], 64), maskn[sl(hh), :].unsqueeze(1).to_broadcast([64, 8, 64]), ALU.mult,
                             r=[bkn(4 + hh), "maskn"], w=["Ab0"])
                    for ci, c in cs:
                        P.tt("pool", PTt[:, ci, :, :], GB[:, ci, :, 0:64], identd[:].unsqueeze(1).to_broadcast([128, 4, 64]), ALU.add, r=["GB", "identd"], w=["PTt"])
                    for bb, lt, key in ((0, bt_, "BK"), (2, kt_, "BK"), (4, vt_, "vb")):
                        for ci, c in cs:
                            for cc, hh in hs:
                                P.mm(ps[bb + hh][sl(hh), (ci * 4 + cc) * 64:(ci * 4 + cc + 1) * 64], lt(c, cc, hh), idq(hh), r=[key, "identb"], w=[bkn(bb + hh)])
                    for hh in range(2):
                        P.copy("act", f8(Btk, hh), v3(ps[0 + hh][sl(hh), :], 64), r=[bkn(0 + hh)], w=["Btk"])
                        P.copy("act", f8(Ktk, hh), v3(ps[2 + hh][sl(hh), :], 64), r=[bkn(2 + hh)], w=["Ktk"])
                        P.copy("act", f8(Vtk, hh), v3(ps[4 + hh][sl(hh), :], 64), r=[bkn(4 + hh)], w=["Vtk"])
                    A, B, ka, kb = Ab[0], GB[:, :, :, 0:64], "Ab0", "GB"
                    for lev in range(1, 6):
                        An, Bn = Ab[lev % 2], Bb[lev % 2]
                        kan, kbn = f"Ab{lev % 2}", f"Bb{lev % 2}"
                        for ci, c in cs:
                            for cc, hh in hs:
                                P.mm(ps[0 + hh][sl(hh), (ci * 4 + cc) * 64:(ci * 4 + cc + 1) * 64], B[sl(hh), ci, cc, :], A[sl(hh), ci, cc, :], r=[ka, kb], w=[bkn(0 + hh)])
                        if lev < 5:
                            for ci, c in cs:
                                for cc, hh in hs:
                                    P.mm(ps[2 + hh][sl(hh), (ci * 4 + cc) * 64:(ci * 4 + cc + 1) * 64], A[sl(hh), ci, cc, :], B[sl(hh), ci, cc, :], r=[ka, kb], w=[bkn(2 + hh)])
                        for hh in range(2):
                            P.copy("act", f8(An, hh), v3(ps[0 + hh][sl(hh), :], 64), r=[bkn(0 + hh)], w=[kan])
                        if lev < 5:
                            for hh in range(2):
                                P.copy("dve", f8(Bn, hh), v3(ps[2 + hh][sl(hh), :], 64), r=[bkn(2 + hh)], w=[kbn])
                        for ci, c in cs:
                            for cc, hh in hs:
                                P.mm(ps[4 + hh][sl(hh), (ci * 4 + cc) * 64:(ci * 4 + cc + 1) * 64], An[sl(hh), ci, cc, :], PTt[sl(hh), ci, cc, :], r=[kan, "PTt"], w=[bkn(4 + hh)])
                        for hh in range(2):
                            P.tt("dve", f8(PTt, hh), v3(ps[4 + hh][sl(hh), :], 64), f8(PTt, hh), ALU.add, r=[bkn(4 + hh), "PTt"], w=["PTt"])
                        A, B, ka, kb = An, Bn, kan, kbn
                    for ci, c in cs:
                        gc = blk * NCH + c
                        S0 = ST[gc % 2]
                        S1 = ST[(gc + 1) % 2]
                        k0, k1 = f"ST{gc % 2}", f"ST{(gc + 1) % 2}"
                        for hh in range(2):
                            P.copy("act", STb[sl(hh), :, :], S0[sl(hh), :, :], r=[k0], w=["STb"])
                        for cc, hh in hs:
                            o = ps[0 + hh][sl(hh), cc * 64:(cc + 1) * 64]
                            P.mm(o, GK[sl(hh), ci, cc, 0:64], Vtk[sl(hh), ci, cc, :], start=True, stop=False, r=["GK", "Vtk"], w=[bkn(0 + hh)])
                            P.mm(o, at_(c, cc, hh), STb[sl(hh), cc, :], start=False, stop=True, r=["AR", "STb"], w=[bkn(0 + hh)])
                        for hh in range(2):
                            P.copy("act", Zs[sl(hh), :, :], v3(ps[0 + hh][sl(hh), 0:256], 64), r=[bkn(0 + hh)], w=["Zs"])
                        for cc, hh in hs:
                            P.mm(ps[0 + hh][sl(hh), 256 + cc * 64:256 + (cc + 1) * 64], PTt[sl(hh), ci, cc, :], Zs[sl(hh), cc, :], r=["PTt", "Zs"], w=[bkn(0 + hh)])
                        for hh in range(2):
                            P.copy("dve", Utk[sl(hh), :, :], v3(ps[0 + hh][sl(hh), 256:512], 64), r=[bkn(0 + hh)], w=["Utk"])
                        for cc, hh in hs:
                            o = ps[2 + hh][sl(hh), cc * 64:(cc + 1) * 64]
                            P.mm(o, STb[sl(hh), cc, :], rt_(c, cc, hh), start=True, stop=False, r=["STb", "AR"], w=[bkn(2 + hh)])
                            P.mm(o, Utk[sl(hh), cc, :], GB[sl(hh), ci, cc, 64:128], start=False, stop=False, r=["Utk", "GB"], w=[bkn(2 + hh)])
                            P.mm(o, Vtk[sl(hh), ci, cc, :], GK[sl(hh), ci, cc, 64:128], start=False, stop=True, r=["Vtk", "GK"], w=[bkn(2 + hh)])
                        for cc, hh in hs:
                            o = ps[4 + hh][sl(hh), cc * 64:(cc + 1) * 64]
                            P.mm(o, Btk[sl(hh), ci, cc, :], Utk[sl(hh), cc, :], start=True, stop=False, r=["Btk", "Utk"], w=[bkn(4 + hh)])
                            P.mm(o, Ktk[sl(hh), ci, cc, :], Vtk[sl(hh), ci, cc, :], start=False, stop=True, r=["Ktk", "Vtk"], w=[bkn(4 + hh)])
                        for hh in range(2):
                            P.tt("dve", Zf[sl(hh), :, :], v3(ps[4 + hh][sl(hh), 0:256], 64), S0[sl(hh), :, :], ALU.add, r=[bkn(4 + hh), k0], w=["Zf"])
                            P.tt("pool", S1[sl(hh), :, :], Zf[sl(hh), :, :],
                                 epos[sl(hh), :, c * C + C - 1:c * C + C].to_broadcast([64, 4, 64]), ALU.mult, r=["Zf", "epos"], w=[k1])
                        for hh in range(2):
                            P.copy("act", yT[sl(hh), :, c * C:(c + 1) * C], v3(ps[2 + hh][sl(hh), 0:256], 64), r=[bkn(2 + hh)], w=["yT"])

                P.enabled = stage >= 8
                gb = lambda cc: (ps[cc], f"b{cc}")
                gsteps = [
                    lambda cc: P.mm(gb(cc)[0][:, 0:TB], bavg[:], yT[:, cc, :], r=["bavg", "yT"], w=[gb(cc)[1]]),
                    lambda cc: P.tt("dve", T(cc, 0)[:], yT[:, cc, :], gb(cc)[0][:, 0:TB], ALU.subtract, r=["yT", gb(cc)[1]], w=[tk(cc, 0)]),
                    lambda cc: P.act(T(cc, 1)[:], T(cc, 0)[:], AF.Square, r=[tk(cc, 0)], w=[tk(cc, 1)]),
                    lambda cc: P.mm(gb(cc)[0][:, TB:2 * TB], bavg[:], T(cc, 1)[:], r=["bavg", tk(cc, 1)], w=[gb(cc)[1]]),
                    lambda cc: P.act(T(cc, 1)[:], gb(cc)[0][:, TB:2 * TB], AF.Ln, r=[gb(cc)[1]], w=[tk(cc, 1)], bias=64e-5),
                    lambda cc: P.act(T(cc, 1)[:], T(cc, 1)[:], AF.Exp, r=[tk(cc, 1)], w=[tk(cc, 1)], scale=-0.5),
                    lambda cc: P.tt("dve", T(cc, 0)[:], T(cc, 0)[:], T(cc, 1)[:], ALU.mult, r=[tk(cc, 0), tk(cc, 1)], w=[tk(cc, 0)]),
                    lambda cc: P.ts("dve", T(cc, 0)[:], T(cc, 0)[:], pcol("lnw", cc), ALU.mult, pcol("lnb", cc), ALU.add, r=[tk(cc, 0), "pc"], w=[tk(cc, 0)]),
                    lambda cc: P.tt("pool", T(cc, 0)[:], T(cc, 0)[:], bonus[:, cc, :], ALU.add, r=[tk(cc, 0), "bonus"], w=[tk(cc, 0)]),
                    lambda cc: P.tt("pool", mixT[:, cc, :], T(cc, 0)[:], gT[:, cc, :], ALU.mult, r=[tk(cc, 0), "gT"], w=["mixT"]),
                ]
                for st_ in gsteps:
                    for cc in range(4):
                        st_(cc)
                P.enabled = stage >= 3
                P.dma("sp", mixD[:, :, blk * TB:(blk + 1) * TB].rearrange("k p t -> p k t"), mixT[:, :, :], r=["mixT"], w=["mixD"])

            P.enabled = stage >= 9
            Sf = ST[(nblk * NCH) % 2]
            kf = f"ST{(nblk * NCH) % 2}"
            for hh in range(2):
                for cc in range(4):
                    P.mm(ps[hh][hh * 64:(hh + 1) * 64, cc * 64:(cc + 1) * 64], Sf[hh * 64:(hh + 1) * 64, cc, :],
                         ident[hh * 64:(hh + 1) * 64, hh * 64:(hh + 1) * 64], r=[kf, "ident"], w=[f"b{hh}"])
                P.copy("act", Zf[hh * 64:(hh + 1) * 64, :, :], ps[hh][hh * 64:(hh + 1) * 64, 0:256].rearrange("p (a t) -> p a t", t=64), r=[f"b{hh}"], w=["Zf"])
                P.dma("sp", srp.rearrange("(c two) v k -> two v c k", two=2)[hh], Zf[hh * 64:(hh + 1) * 64, :, :], r=["Zf"])
            P.dma("sp", shp.rearrange("(c p) -> p c", p=128), pT[:, :, 0], r=[("pT", i) for i in range(14)] + ["pT"], allow_slow_non_contiguous=True)
            P.tr(ps[1][:, 0:128], KTf[:, 128:256], ident[:], r=["KTf", "ident"], w=["b1"])
            P.copy("act", tmp[0][:, 0:128], ps[1][:, 0:128], r=["b1"], w=["t0"])
            P.dma("sp", kbp, tmp[0][:, 0:128], r=["t0"])
            P.dma("sp", vbp, Vf[:], r=["Vf"])
            P.enabled = True
            P.barrier()
            P.emit()

        if do_samp:
            with ExitStack() as phS:
                def sS(name, shape, dt=F32):
                    return sb(name, shape, dt, phS)
                win = sS("winS", [128, 8, 2560], BF16)
                lwa = sS("lwaS", [128, 512], BF16)
                lwg = sS("lwgS", [128, 512], BF16)
                bavg = sS("bavgS", [128, 128])
                gbc = sS("gbcS", [128, D])
                x16 = sS("x16", [128, D])
                xn16 = sS("xn16", [128, D], BF16)
                ss16 = sS("ss16", [128, 4])
                hTs = sS("hTs", [128, 8, 16], BF16)
                pTs = sS("pTs", [128, 20, 16])
                shl = sS("shl", [16, 1792])
                prevT = sS("prevT", [128, 14, 16])
                xss = sS("xss", [128, 14, 16])
                tm = sS("tm", [16, 1792])
                TMv = sS("TMv", [16, 8, 512])
                lin16 = sS("lin16", [128, 16], BF16)
                sg16 = sS("sg16", [128, 16], BF16)
                q = [sS(f"q{i}", [128, 16]) for i in range(10)]
                Fv = sS("Fv", [128, 4, 8, 16])
                VS = sS("VS", [128, 8, 64])
                big = sS("big", [128, 8256])
                Sst = big[:, 0:4096].rearrange("p (v k) -> p v k", k=64)
                tmpA = big[:, 4096:8192].rearrange("p (v k) -> p v k", k=64)
                sm = [sS(f"sm{i}", [128, 64]) for i in range(5)]
                st4 = sS("st4", [128, 4])
                lnwbh = sS("lnwbh", [128, 64])
                lnbbh = sS("lnbbh", [128, 64])
                QTs = sS("QTs", [128, 4, 16])
                KTs = sS("KTs", [128, 16])
                Kc = sS("Kc", [64, 129, 64])
                Vc = sS("Vc", [64, 129, 64])
                prod = big[0:64, :].rearrange("p (j d) -> p j d", d=64)
                qd = sS("qd", [64, 64])
                sc = sS("sc", [64, 256])
                bdec = sS("bdec", [64, 129])
                od = sS("od", [64, 64])
                o2 = sS("o2", [64, 64])
                skd = sS("skd", [64, 4])

                P.dma("sp", gbc[:], g1bc_d, w=["gbc"])
                P.dma("sp", bavg[:], bavg_d, w=["bavg"])
                for kc in range(8):
                    P.dma("pool", win[:, kc, :], w_in[kc * 128:(kc + 1) * 128, :], w=[("win", kc)], max_dma_last_dim=4096)
                P.dma("pool", lwa[:], lwa_d, w=["lwa"])
                P.dma("pool", lwg[:], lwg_d, w=["lwg"])
                P.dma("sp", lnwbh[:], lnwbh_d, w=["lnwbh"])
                P.dma("sp", lnbbh[:], lnbbh_d, w=["lnbbh"])
                P.dma("sp", skd[:, 0:1], skd_d, w=["skd"])
                P.dma("sp", shl[:], shift_s, w=["shl"])
                P.dma("sp", Sst[:], srs_in.rearrange("b h v k -> (b h) v k"), w=["Sst"])
                for g in range(2):
                    for kvh in range(2):
                        rows = slice(g * 32 + kvh * 16, g * 32 + kvh * 16 + 16)
                        P.dma("pool", Kc[rows, 0:128, :], ck_in[:, :, kvh, :], w=["Kc"])
                        P.dma("pool", Vc[rows, 0:128, :], cv_in[:, :, kvh, :], w=["Vc"])

                P.dma("sp", q[0][0:32, 0:4], relb, w=["q0"])
                P.dma("sp", sc[0:32, 0:129], onehr_d, w=["sc"])
                P.mm(ps[0][0:4, 0:129], q[0][0:32, 0:4], sc[0:32, 0:129], r=["q0", "sc"], w=["b0"])
                P.copy("act", bdec[0:4, 0:129], ps[0][0:4, 0:129], r=["b0"], w=["bdec"])
                P.dma("sp", fdscr, bdec[0:4, 0:129], r=["bdec"], w=["fdscr"])

                P.dma("sp", x16[0:16, :], xsamp, w=["x16"])
                norm_rows(x16, xn16, gbc, "x16", "xn16", ss16, nrows=16)
                for kc in range(8):
                    P.tr(psT[:, kc * 128:kc * 128 + 16], xn16[0:16, kc * 128:(kc + 1) * 128], identb[0:16, 0:16], r=["xn16", "identb"], w=["bT"])
                P.copy("act", hTs[:, :, :], psT[:].rearrange("p (k t) -> p k t", t=128)[:, :, 0:16], r=["bT"], w=["hTs"])
                for oc in range(20):
                    if oc == 17:
                        continue
                    bank = 5 + oc % 2
                    for kc in range(8):
                        P.mm(ps[bank][:, 0:16], win[:, kc, oc * 128:(oc + 1) * 128], hTs[:, kc, :], start=(kc == 0), stop=(kc == 7),
                             r=[("win", kc), "hTs"], w=[f"b{bank}"])
                    P.copy("act", pTs[:, oc, :], ps[bank][:, 0:16], r=[f"b{bank}"], w=[("pTs", oc)])
                for kc in range(8):
                    P.mm(ps[4][0:16, 0:128], hTs[:, kc, :], win[:, kc, 2176:2304], start=(kc == 0), stop=(kc == 7), r=[("win", kc), "hTs"], w=["b4"])
                P.copy("act", x16[0:16, 0:128], ps[4][0:16, 0:128], r=["b4"], w=["x16v"])
                P.dma("sp", scrVn, x16[0:16, 0:128], r=["x16v"], w=["scrVn"])
                for g4 in range(4):
                    ocs = list(range(g4 * 4, min(14, g4 * 4 + 4)))
                    for i, oc in enumerate(ocs):
                        P.tr(ps[g4 % 2][0:16, i * 128:(i + 1) * 128], pTs[:, oc, :], ident[:], r=[("pTs", oc), "ident"], w=[f"b{g4 % 2}"])
                    n = len(ocs) * 128
                    P.copy("act", tm[0:16, g4 * 512:g4 * 512 + n], ps[g4 % 2][0:16, 0:n], r=[f"b{g4 % 2}"], w=["tm"])
                P.dma("sp", shs, tm[0:16, 0:1792], r=["tm"])
                for oc in range(14):
                    P.tr(ps[2][:, oc * 16:(oc + 1) * 16], shl[0:16, oc * 128:(oc + 1) * 128], ident[0:16, 0:16], r=["shl", "ident"], w=["b2"])
                P.copy("act", prevT[:].rearrange("p a t -> p (a t)"), ps[2][:, 0:224], r=["b2"], w=["prevT"])
                for oc in range(14):
                    P.ts("pool", q[0][:], prevT[:, oc, :], pcol("mu", oc), ALU.mult, r=["prevT", "pc"], w=["q0"])
                    P.stt(xss[:, oc, :], pTs[:, oc, :], pcol("omu", oc), q[0][:], ALU.mult, ALU.add, r=[("pTs", oc), "q0", "pc"], w=["xss"])
                P.act(lin16[0:64, :], xss[0:64, 12, :], AF.Tanh, r=["xss"], w=["lin16"])
                P.copy("act", lin16[64:128, :], xss[64:128, 12, :], r=["xss"], w=["lin16"])
                P.act(sg16[:], xss[:, 13, :], AF.Sigmoid, r=["xss"], w=["sg16"])
                for cc in range(4):
                    r_, k_, v_ = xss[:, cc, :], xss[:, 4 + cc, :], xss[:, 8 + cc, :]
                    P.mm(ps[0][:, 0:16], lwa[0:64, cc * 128:(cc + 1) * 128], lin16[0:64, :], r=["lwa", "lin16"], w=["b0"])
                    P.mm(ps[2][:, 0:16], lwa[64:128, cc * 128:(cc + 1) * 128], lin16[64:128, :], r=["lwa", "lin16"], w=["b2"])
                    P.mm(ps[1][:, 0:16], lwg[:, cc * 128:(cc + 1) * 128], sg16[:], r=["lwg", "sg16"], w=["b1"])
                    P.act(q[0][:], ps[0][:, 0:16], AF.Sigmoid, r=["b0", "pc"], w=["q0"], bias=pcol("w0", cc))
                    P.act(q[1][:], ps[2][:, 0:16], AF.Sigmoid, r=["b2", "pc"], w=["q1"], bias=pcol("a0", cc))
                    P.copy("act", Fv[:, cc, 6, :], ps[1][:, 0:16], r=["b1"], w=["Fv"])
                    P.ts("dve", q[2][:], k_, pcol("kk", cc), ALU.mult, r=["xss", "pc"], w=["q2"])
                    P.act(q[3][:], q[2][:], AF.Square, r=["q2"], w=["q3"])
                    P.mm(ps[1][:, 16:32], bavg[:], q[3][:], r=["bavg", "q3"], w=["b1"])
                    P.act(q[3][:], ps[1][:, 16:32], AF.Sqrt, r=["b1"], w=["q3"], scale=64.0)
                    P.ts("dve", q[3][:], q[3][:], 1e-12, ALU.max, r=["q3"], w=["q3"])
                    P.recip(q[3][:], q[3][:], r=["q3"], w=["q3"])
                    P.tt("dve", q[2][:], q[2][:], q[3][:], ALU.mult, r=["q2", "q3"], w=["q2"])
                    P.ts("dve", q[3][:], q[1][:], pcol("ka", cc), ALU.mult, pcol("omka", cc), ALU.add, r=["q1", "pc"], w=["q3"])
                    P.tt("dve", Fv[:, cc, 2, :], k_, q[3][:], ALU.mult, r=["xss", "q3"], w=["Fv"])
                    P.tt("dve", Fv[:, cc, 5, :], q[2][:], q[1][:], ALU.mult, r=["q2", "q1"], w=["Fv"])
                    P.ts("dve", Fv[:, cc, 4, :], q[2][:], -1.0, ALU.mult, r=["q2"], w=["Fv"])
                    P.act(Fv[:, cc, 1, :], q[0][:], AF.Exp, r=["q0"], w=["Fv"], scale=-EXPM05)
                    P.copy("act", Fv[:, cc, 0, :], r_, r=["xss"], w=["Fv"])
                    P.copy("act", Fv[:, cc, 3, :], v_, r=["xss"], w=["Fv"])
                    P.stt(q[4][:], r_, pcol("rk", cc), Fv[:, cc, 2, :], ALU.mult, ALU.mult, r=["xss", "Fv", "pc"], w=["q4"])
                    P.mm(ps[1][:, 32:48], bavg[:], q[4][:], r=["bavg", "q4"], w=["b1"])
                    P.stt(Fv[:, cc, 7, :], ps[1][:, 32:48], 64.0, v_, ALU.mult, ALU.mult, r=["b1", "xss"], w=["Fv"])
                for vec in range(8):
                    bank = vec % 2
                    for cc in range(4):
                        P.tr(ps[bank][0:16, cc * 128:(cc + 1) * 128], Fv[:, cc, vec, :], ident[:], r=["Fv", "ident"], w=[f"b{bank}"])
                    P.copy("act", TMv[:, vec, :], ps[bank][0:16, :], r=[f"b{bank}"], w=["TMv"])
                for vec in range(8):
                    P.dma("sp", scrV[:, :, vec, :], TMv[:, vec, :].rearrange("b (h c) -> b h c", c=64), r=["TMv"], w=["scrV"])
                P.dma("sp", VS[:], scrV.rearrange("b h v c -> (b h) v c"), r=["scrV"], w=["VS"])
                bc_v = lambda i: VS[:, i, :].unsqueeze(1).to_broadcast([128, 64, 64])
                bc_k = lambda ap: ap.unsqueeze(2).to_broadcast([128, 64, 64])
                P.tt("dve", tmpA[:], Sst[:], bc_v(4), ALU.mult, r=["Sst", "VS"], w=["tmpA"])
                P.op("dve", lambda e: e.tensor_reduce(out=sm[0][:], in_=tmpA[:], axis=AX.X, op=ALU.add), r=["tmpA"], w=["sm0"])
                P.tt("dve", Sst[:], Sst[:], bc_v(1), ALU.mult, r=["Sst", "VS", "tmpA"], w=["Sst"])
                P.tt("dve", tmpA[:], bc_k(sm[0][:]), bc_v(5), ALU.mult, r=["sm0", "VS"], w=["tmpA"])
                P.tt("pool", Sst[:], Sst[:], tmpA[:], ALU.add, r=["Sst", "tmpA"], w=["Sst"])
                P.tt("dve", tmpA[:], bc_k(VS[:, 3, :]), bc_v(2), ALU.mult, r=["VS", "Sst"], w=["tmpA"])
                P.tt("pool", Sst[:], Sst[:], tmpA[:], ALU.add, r=["Sst", "tmpA"], w=["Sst"])
                P.dma("sp", srs.rearrange("b h v k -> (b h) v k"), Sst[:], r=["Sst"])
                P.tt("dve", tmpA[:], Sst[:], bc_v(0), ALU.mult, r=["Sst", "VS"], w=["tmpA"])
                P.op("dve", lambda e: e.tensor_reduce(out=sm[1][:], in_=tmpA[:], axis=AX.X, op=ALU.add), r=["tmpA"], w=["sm1"])
                P.op("dve", lambda e: e.tensor_reduce(out=st4[:, 0:1], in_=sm[1][:], axis=AX.X, op=ALU.add), r=["sm1"], w=["st4"])
                P.ts("dve", st4[:, 0:1], st4[:, 0:1], 1.0 / 64, ALU.mult, r=["st4"], w=["st4"])
                P.ts("dve", sm[2][:], sm[1][:], st4[:, 0:1], ALU.subtract, r=["sm1", "st4"], w=["sm2"])
                P.tt("dve", sm[3][:], sm[2][:], sm[2][:], ALU.mult, r=["sm2"], w=["sm3"])
                P.op("dve", lambda e: e.tensor_reduce(out=st4[:, 1:2], in_=sm[3][:], axis=AX.X, op=ALU.add), r=["sm3"], w=["st4"])
                P.act(st4[:, 2:3], st4[:, 1:2], AF.Sqrt, r=["st4"], w=["st4"], scale=1.0 / 64, bias=64e-5)
                P.recip(st4[:, 2:3], st4[:, 2:3], r=["st4"], w=["st4"])
                P.ts("dve", sm[2][:], sm[2][:], st4[:, 2:3], ALU.mult, r=["sm2", "st4"], w=["sm2"])
                P.tt("dve", sm[2][:], sm[2][:], lnwbh[:], ALU.mult, r=["sm2", "lnwbh"], w=["sm2"])
                P.tt("dve", sm[2][:], sm[2][:], lnbbh[:], ALU.add, r=["sm2", "lnbbh"], w=["sm2"])
                P.tt("dve", sm[2][:], sm[2][:], VS[:, 7, :], ALU.add, r=["sm2", "VS"], w=["sm2"])
                P.tt("dve", sm[2][:], sm[2][:], VS[:, 6, :], ALU.mult, r=["sm2", "VS"], w=["sm2"])
                P.dma("sp", scrMix.rearrange("b (h c) -> (b h) c", c=64), sm[2][:], r=["sm2"], w=["scrMix"])

                def to_mix(scr, key, ncols, mixc):
                    P.dma("sp", tm[0:16, 0:ncols], scr, r=[key], w=["tm"])
                    for i in range(ncols // 128):
                        P.tr(ps[0][:, i * 16:(i + 1) * 16], tm[0:16, i * 128:(i + 1) * 128], ident[0:16, 0:16], r=["tm", "ident"], w=["b0"])
                    n = ncols // 128
                    P.copy("act", mixTs[:, mixc:mixc + n, :], ps[0][:, 0:n * 16].rearrange("p (a t) -> p a t", t=16), r=["b0"], w=["mixTs"])

                to_mix(scrMix, "scrMix", 512, 0)

                for j in range(2):
                    headnorm(pTs[:, 14 + j, :], 16, pcol("qns"), [(QTs[:, j, :], "QTs")], (q[8], q[9]), ("pTs", 14 + j))
                    headnorm(pTs[:, 18 + j, :], 16, pcol("qnm"), [(QTs[:, 2 + j, :], "QTs")], (q[8], q[9]), ("pTs", 18 + j))
                headnorm(pTs[:, 16, :], 16, pcol("kns"), [(KTs[:, :], "KTs")], (q[8], q[9]), ("pTs", 16))
                for j in range(4):
                    P.tr(ps[1][0:16, j * 128:(j + 1) * 128], QTs[:, j, :], ident[:], r=["QTs", "ident"], w=["b1"])
                P.copy("act", tm[0:16, 0:512], ps[1][0:16, 0:512], r=["b1"], w=["tm"])
                P.dma("sp", scrQ, tm[0:16, 0:512], r=["tm"], w=["scrQ"])
                P.tr(ps[1][0:16, 0:128], KTs[:, :], ident[:], r=["KTs", "ident"], w=["b1"])
                P.copy("act", x16[0:16, 128:256], ps[1][0:16, 0:128], r=["b1"], w=["x16k"])
                P.dma("sp", scrK, x16[0:16, 128:256], r=["x16k"], w=["scrK"])
                ck2 = ck_in.rearrange("b j h d -> b j (h d)")
                cv2 = cv_in.rearrange("b j h d -> b j (h d)")
                P.dma("sp", kbs[:, 0:127, :], ck2[:, 1:128, :])
                P.dma("sp", vbs[:, 0:127, :], cv2[:, 1:128, :])
                P.dma("sp", kbs[:, 127, :], scrK, r=["scrK"])
                P.dma("sp", vbs[:, 127, :], scrVn, r=["scrVn"])

                for g in range(2):
                    for kvh in range(2):
                        rows = slice(g * 32 + kvh * 16, g * 32 + kvh * 16 + 16)
                        P.dma("sp", Kc[rows, 128, :], scrK[:, kvh * 64:(kvh + 1) * 64], r=["scrK"], w=["Kc"])
                        P.dma("sp", Vc[rows, 128, :], scrVn[:, kvh * 64:(kvh + 1) * 64], r=["scrVn"], w=["Vc"])
                        P.dma("sp", qd[rows, :], scrQ[:, g * 128 + kvh * 64:g * 128 + (kvh + 1) * 64], r=["scrQ"], w=["qd"])
                        P.dma("sp", bdec[rows, :], bass.AP(fdscr.tensor, (2 * kvh + g) * 129, [[0, 16], [1, 129]]), r=["fdscr"], w=["bdec"])
                P.act(skd[:, 1:2], skd[:, 0:1], AF.Exp, r=["skd"], w=["skd"])
                P.tt("dve", prod[:], Kc[:], qd[:].unsqueeze(1).to_broadcast([64, 129, 64]), ALU.mult, r=["Kc", "qd"], w=["prod", "Sst", "tmpA"])
                P.op("dve", lambda e: e.tensor_reduce(out=sc[:, 0:129], in_=prod[:], axis=AX.X, op=ALU.add), r=["prod"], w=["sc"])
                P.stt(sc[:, 0:129], sc[:, 0:129], SCALE, bdec[:], ALU.mult, ALU.add, r=["sc", "bdec"], w=["sc"])
                P.act(sc[:, 0:129], sc[:, 0:129], AF.Exp, r=["sc"], w=["sc", "skd"], accum_out=skd[:, 2:3])
                P.tt("dve", skd[:, 2:3], skd[:, 2:3], skd[:, 1:2], ALU.add, r=["skd"], w=["skd"])
                P.recip(skd[:, 2:3], skd[:, 2:3], r=["skd"], w=["skd"])
                P.tt("dve", prod[:], Vc[:], sc[:, 0:129].unsqueeze(2).to_broadcast([64, 129, 64]), ALU.mult, r=["Vc", "sc"], w=["prod"])
                P.op("dve", lambda e: e.tensor_reduce(out=od[:], in_=prod[:].rearrange("p j d -> p d j"), axis=AX.X, op=ALU.add), r=["prod"], w=["od"])
                P.ts("dve", od[:], od[:], skd[:, 2:3], ALU.mult, r=["od", "skd"], w=["od"])
                for g in range(2):
                    for kvh in range(2):
                        rows = slice(g * 32 + kvh * 16, g * 32 + kvh * 16 + 16)
                        P.dma("sp", scrO[:, g * 128 + kvh * 64:g * 128 + (kvh + 1) * 64], od[rows, :], r=["od"], w=["scrO"])
                to_mix(scrO, "scrO", 256, 4)

                for mh in range(4):
                    rows = slice(mh * 16, (mh + 1) * 16)
                    P.dma("sp", qd[rows, :], scrQ[:, 256 + mh * 64:256 + (mh + 1) * 64], r=["scrQ"], w=["qd"])
                for half in range(2):
                    for mh in range(4):
                        rows = slice(mh * 16, (mh + 1) * 16)
                        P.dma("pool", Kc[rows, 0:128, :], cmk_in[:, half * 128:(half + 1) * 128, mh, :], w=["Kc"])
                        P.dma("pool", Vc[rows, 0:128, :], cmv_in[:, half * 128:(half + 1) * 128, mh, :], w=["Vc"])
                    P.tt("dve", prod[:, 0:128, :], Kc[:, 0:128, :], qd[:].unsqueeze(1).to_broadcast([64, 128, 64]), ALU.mult, r=["Kc", "qd"], w=["prod"])
                    P.op("dve", lambda e, h=half: e.tensor_reduce(out=sc[:, h * 128:(h + 1) * 128], in_=prod[:, 0:128, :], axis=AX.X, op=ALU.add),
                         r=["prod"], w=["sc"])
                    P.act(sc[:, half * 128:(half + 1) * 128], sc[:, half * 128:(half + 1) * 128], AF.Exp, r=["sc"], w=["sc", "skd"],
                          scale=SCALE, accum_out=skd[:, 2 + half:3 + half])
                    P.tt("dve", prod[:, 0:128, :], Vc[:, 0:128, :], sc[:, half * 128:(half + 1) * 128].unsqueeze(2).to_broadcast([64, 128, 64]), ALU.mult,
                         r=["Vc", "sc"], w=["prod"])
                    P.op("dve", lambda e, o=(od if half == 0 else o2): e.tensor_reduce(out=o[:], in_=prod[:, 0:128, :].rearrange("p j d -> p d j"), axis=AX.X, op=ALU.add),
                         r=["prod"], w=["od" if half == 0 else "o2"])
                P.tt("dve", od[:], od[:], o2[:], ALU.add, r=["od", "o2"], w=["od"])
                P.tt("dve", skd[:, 2:3], skd[:, 2:3], skd[:, 3:4], ALU.add, r=["skd"], w=["skd"])
                P.recip(skd[:, 2:3], skd[:, 2:3], r=["skd"], w=["skd"])
                P.ts("dve", od[:], od[:], skd[:, 2:3], ALU.mult, r=["od", "skd"], w=["od"])
                for mh in range(4):
                    rows = slice(mh * 16, (mh + 1) * 16)
                    P.dma("sp", scrOm[:, mh * 64:(mh + 1) * 64], od[rows, :], r=["od"], w=["scrOm"])
                to_mix(scrOm, "scrOm", 256, 6)
                P.barrier()
                P.emit()

        P.enabled = True
        if do_ffn:
            with ExitStack() as ph2:
                def s2(name, shape, dt=F32):
                    return sb(name, shape, dt, ph2)
                wout = s2("wout", [128, 8, D], BF16)
                wff1 = s2("wff1", [128, 8, 4096], BF16)
                wff2 = s2("wff2", [128, 32, D], BF16)
                g2 = s2("g2", [128, D])
                mixb = [s2(f"mixb{i}", [128, 8, TB], BF16) for i in range(2)]
                xb2 = [s2(f"x2_{i}", [128, D]) for i in range(4)]
                xn2 = s2("xn2", [128, D], BF16)
                h2T = s2("h2T", [128, 8, TB], BF16)
                aT = s2("aT", [128, 32, TB], BF16)
                rl = [s2(f"rl{i}", [128, TB]) for i in range(2)]
                ss2 = s2("ss2", [128, 4])
                P.dma("sp", g2[:], g2bc_d, w=["gbc"])
                for kc in range(8):
                    P.dma("pool", wout[:, kc, :], w_out[kc * 128:(kc + 1) * 128, :], w=[("wout", kc)])
                for kc in range(8):
                    for q in range(2):
                        P.dma("pool", wff1[:, kc, q * 2048:(q + 1) * 2048], w_ff1[kc * 128:(kc + 1) * 128, q * 2048:(q + 1) * 2048], w=[("wff1", kc)])
                for kc in range(32):
                    P.dma("pool", wff2[:, kc, :], w_ff2[kc * 128:(kc + 1) * 128, :], w=[("wff2", kc)])
                def ffn_block(mb, km, tiles, W):
                    for (xin, yout, nr, c0, ti) in tiles:
                        kx = ("x2", ti)
                        P.dma("pool", xb2[ti][0:nr, :], xin, w=[kx])
                        for half in range(2):
                            for kc in range(8):
                                P.mm(ps[half][0:nr, :], mb[:, kc, c0:c0 + nr], wout[:, kc, half * 512:(half + 1) * 512],
                                     start=(kc == 0), stop=(kc == 7), r=[km, ("wout", kc)], w=[f"b{half}"])
                            P.tt("dve", xb2[ti][0:nr, half * 512:(half + 1) * 512], ps[half][0:nr, :], xb2[ti][0:nr, half * 512:(half + 1) * 512], ALU.add,
                                 r=[f"b{half}", kx], w=[kx])
                        norm_rows(xb2[ti], xn2, g2, kx, "xn2", ss2, nrows=nr)
                        for kc in range(8):
                            P.tr(psT[:, kc * 128:kc * 128 + nr], xn2[0:nr, kc * 128:(kc + 1) * 128], identb[0:nr, 0:nr], r=["xn2", "identb"], w=["bT"])
                        P.copy("act", h2T[:, :, c0:c0 + nr], psT[:].rearrange("p (k t) -> p k t", t=128)[:, :, 0:nr], r=["bT"], w=["h2T"])
                    for oc in range(32):
                        bank = 2 + oc % 2
                        for kc in range(8):
                            P.mm(ps[bank][:, 0:W], wff1[:, kc, oc * 128:(oc + 1) * 128], h2T[:, kc, 0:W], start=(kc == 0), stop=(kc == 7),
                                 r=[("wff1", kc), "h2T"], w=[f"b{bank}"])
                        P.act(rl[oc % 2][:, 0:W], ps[bank][:, 0:W], AF.Relu, r=[f"b{bank}"], w=[f"rl{oc % 2}"])
                        P.tt("pool", aT[:, oc, 0:W], rl[oc % 2][:, 0:W], rl[oc % 2][:, 0:W], ALU.mult, r=[f"rl{oc % 2}"], w=[("aT", oc)])
                    for (xin, yout, nr, c0, ti) in tiles:
                        kx = ("x2", ti)
                        for half in range(2):
                            bank = 4 + half
                            for kc in range(32):
                                P.mm(ps[bank][0:nr, :], aT[:, kc, c0:c0 + nr], wff2[:, kc, half * 512:(half + 1) * 512],
                                     start=(kc == 0), stop=(kc == 31), r=[("aT", kc), ("wff2", kc)], w=[f"b{bank}"])
                            P.tt("dve", xb2[ti][0:nr, half * 512:(half + 1) * 512], ps[bank][0:nr, :], xb2[ti][0:nr, half * 512:(half + 1) * 512], ALU.add,
                                 r=[f"b{bank}", kx], w=[kx])
                        P.dma("sp", yout, xb2[ti][0:nr, :], r=[kx])

                for blk in range(nblk):
                    mb = mixb[blk % 2]
                    km = f"mixb{blk % 2}"
                    P.dma("pool", mb[:, :, :], mixD[:, :, blk * TB:(blk + 1) * TB].rearrange("k p t -> p k t"), r=["mixD"], w=[km])
                    tiles = [(xseq[(blk * 2 + ti) * 128:(blk * 2 + ti + 1) * 128, :], y_p[(blk * 2 + ti) * 128:(blk * 2 + ti + 1) * 128, :], 128, ti * 128,
                              (blk * 2 + ti) % 4) for ti in range(2)]
                    ffn_block(mb, km, tiles, TB)
                if do_samp:
                    ffn_block(mixTs, "mixTs", [(xsamp, y_s, 16, 0, 0)], 16)
                P.emit()
        else:
            P.emit()
    return nc


def _consts():
    ident = np.eye(128, dtype=np.float32)
    maskg = np.zeros((128, 128), np.float32)
    for r in range(128):
        j = r % 64
        for c in range(128):
            t = c % 64
            maskg[r, c] = 1.0 if ((c < 64 and j < t) or (c >= 64 and j <= t)) else 0.0
    maskn = np.tile(np.tril(np.ones((64, 64), np.float32), -1), (2, 1))
    identd = np.tile(np.eye(64, dtype=np.float32), (2, 1))
    bavg = np.zeros((128, 128), np.float32)
    bavg[:64, :64] = 1.0 / 64
    bavg[64:, 64:] = 1.0 / 64
    ones = np.ones((128, 64), np.float32)
    rmask = np.ones((128, TB), np.float32)
    rmask[:, ::C] = 0.0
    bk = t5_bucket_np(np.arange(129))
    oneh = np.zeros((32, 129), np.float32)
    oneh[bk, np.arange(129)] = 1.0
    bkr = t5_bucket_np(128 - np.arange(129))
    onehr = np.zeros((32, 129), np.float32)
    onehr[bkr, np.arange(129)] = 1.0
    return dict(onehr=onehr, identd=identd, aident=np.ascontiguousarray(ident[::-1]), ident=ident, maskg=maskg, maskn=maskn, bavg=bavg, ones=ones, rmask=rmask, oneh=oneh)


def _prep_shared(inp):
    f = lambda a: np.ascontiguousarray(np.asarray(a, dtype=np.float32))
    sh = _consts()
    w_in = f(inp["w_in"][0])
    perm = np.arange(2560)
    for j in range(2):
        for hh in range(2):
            dst = 1792 + j * 128 + hh * 64
            src = 1792 + (2 * hh + j) * 64
            perm[dst:dst + 64] = np.arange(src, src + 64)
    sh["w_in"] = np.ascontiguousarray(w_in[:, perm])
    w_out = f(inp["w_out"][0])
    rperm = np.arange(1024)
    for j in range(2):
        for hh in range(2):
            dst = 512 + j * 128 + hh * 64
            src = 512 + (2 * hh + j) * 64
            rperm[dst:dst + 64] = np.arange(src, src + 64)
    sh["w_out"] = np.ascontiguousarray(w_out[rperm, :])
    sh["w_ff1"] = f(inp["w_ff1"][0])
    sh["w_ff2"] = f(inp["w_ff2"][0])
    sh["w_mem"] = f(inp["w_mem_kv"][0])
    sh["lwa"] = np.ascontiguousarray(np.concatenate([f(inp["w_up_w"][0]), f(inp["w_up_a"][0])], 0))
    sh["lwg"] = f(inp["w_up_g"][0])
    sh["relb"] = f(inp["rel_bias"])
    pc = np.zeros((128, NPC), np.float32)
    col = lambda v, n: np.asarray(v, np.float32).reshape(n, 128).T
    pc[:, PC["mu"]:PC["mu"] + 14] = col(inp["mu_shift"][0], 14)
    pc[:, PC["w0"]:PC["w0"] + 4] = col(inp["w0"][0], 4)
    pc[:, PC["a0"]:PC["a0"] + 4] = col(inp["a0"][0], 4)
    pc[:, PC["kk"]:PC["kk"] + 4] = col(inp["k_k"][0], 4)
    pc[:, PC["ka"]:PC["ka"] + 4] = col(inp["k_a"][0], 4)
    pc[:, PC["rk"]:PC["rk"] + 4] = col(np.asarray(inp["r_k"][0]).reshape(512), 4)
    pc[:, PC["lnw"]:PC["lnw"] + 4] = col(inp["lnx_w"][0], 4)
    pc[:, PC["lnb"]:PC["lnb"] + 4] = col(inp["lnx_b"][0], 4)
    t2 = lambda v: np.tile(np.asarray(v, np.float32).reshape(64), 2)
    pc[:, PC["qns"]] = t2(inp["q_norm_swa"][0])
    pc[:, PC["kns"]] = t2(inp["k_norm_swa"][0])
    pc[:, PC["qnm"]] = t2(inp["q_norm_mem"][0])
    pc[:, PC["knm"]] = t2(inp["k_norm_mem"][0])
    sk = np.asarray(inp["sinks"][0], np.float32)
    for j in range(2):
        for hh in range(2):
            pc[hh * 64:(hh + 1) * 64, PC["sink"] + j] = sk[2 * hh + j]
    sh["pc"] = pc
    bc = lambda v: np.ascontiguousarray(np.broadcast_to(np.asarray(v, np.float32).reshape(1, -1), (128, np.asarray(v).size)))
    sh["g1bc"] = bc(inp["norm1_g"][0])
    sh["g2bc"] = bc(inp["norm2_g"][0])
    sh["gmbc"] = bc(inp["mem_norm_g"][0])
    sh["knmbc"] = bc(np.tile(np.asarray(inp["k_norm_mem"][0], np.float32), 4))
    sh["lnwbh"] = np.ascontiguousarray(np.tile(np.asarray(inp["lnx_w"][0], np.float32).reshape(8, 64), (16, 1)))
    sh["lnbbh"] = np.ascontiguousarray(np.tile(np.asarray(inp["lnx_b"][0], np.float32).reshape(8, 64), (16, 1)))
    skd = np.zeros((64, 1), np.float32)
    for g in range(2):
        for kvh in range(2):
            skd[g * 32 + kvh * 16:g * 32 + kvh * 16 + 16, 0] = sk[2 * kvh + g]
    sh["skd"] = skd
    return sh


def _core_inputs(inp, sh, c):
    f = lambda a: np.ascontiguousarray(np.asarray(a, dtype=np.float32))
    m = dict(sh)
    m["xseq"] = f(inp["x_prompt"][c % 2])
    m["mem"] = f(inp["mem_prompt"][c % 2])
    b = slice(16 * c, 16 * c + 16)
    m["xsamp"] = f(inp["x_sample"][b, 0, :])
    m["srs_in"] = f(inp["state_rwkv"][0, b])
    m["shift_s"] = f(inp["state_shift"][0, b])
    m["ck_in"] = f(inp["cache_swa_k"][0, b])
    m["cv_in"] = f(inp["cache_swa_v"][0, b])
    m["cmk_in"] = f(inp["cache_mem_k"][0, b])
    m["cmv_in"] = f(inp["cache_mem_v"][0, b])
    return m


def kernel(**inp):
    nc = build()
    sh = _prep_shared(inp)
    in_maps = [_core_inputs(inp, sh, c) for c in range(8)]
    res = run_bass_kernel_spmd(nc, in_maps, core_ids=list(range(8)))
    R = res.results
    yp = np.stack([R[b]["y_p"] for b in range(2)])
    srp = np.stack([R[b]["srp"] for b in range(2)])[None]
    shp = np.stack([R[b]["shp"] for b in range(2)])[None]
    kbp = np.stack([R[b]["kbp"].reshape(128, 2, 64) for b in range(2)])[None]
    vbp = np.stack([R[b]["vbp"].reshape(128, 2, 64) for b in range(2)])[None]
    mkp = np.stack([R[b]["mkp"].reshape(256, 4, 64) for b in range(2)])[None]
    mvp = np.stack([R[b]["mvp"].reshape(256, 4, 64) for b in range(2)])[None]
    cat = lambda k: np.concatenate([R[c][k] for c in range(8)], 0)
    ys = cat("y_s").reshape(128, 1, 1024)
    srs = cat("srs")[None]
    shs = cat("shs")[None]
    kbs = cat("kbs").reshape(128, 128, 2, 64)[None]
    vbs = cat("vbs").reshape(128, 128, 2, 64)[None]
    return tuple(np.ascontiguousarray(a, dtype=np.float32) for a in (yp, ys, srp, shp, kbp, vbp, mkp, mvp, srs, shs, kbs, vbs))
```
